# Optimizing a Trainium2 kernel written in Bass

```python
import math
import jax, jax.numpy as jnp
from jax import lax
import numpy as np

D_MODEL = 1024
BATCH = 8
SEQ = 4096
DEPTH = 1

MIX_WIDTH = D_MODEL
DIFF_HEADS = 4
DIFF_V_DIM = MIX_WIDTH // 2 // DIFF_HEADS
DIFF_QK_DIM = DIFF_V_DIM // 2
RET_HEADS = 4
RET_V_DIM = MIX_WIDTH // 2 // RET_HEADS
RET_QK_DIM = RET_V_DIM // 2
Q_BLOCK = 128
RET_CHUNK = 128
ROPE_BASE = 10000.0
T5_BUCKETS = 32
T5_MAX_EXACT = 16
T5_MAX_DISTANCE = 128
PEER_HEADS = 8
PEER_N_KEYS = 128
PEER_N_EXPERTS = PEER_N_KEYS * PEER_N_KEYS
PEER_TOPK = 16
PEER_QUERY_DIM = 256
PEER_KEY_DIM = PEER_QUERY_DIM // 2
PEER_CHUNK = 128
PLE_DIM = 256

kernel_name = "hymba_diffattn_retnet_peer_block"

_SPLIT_SIZES = [
    2 * DIFF_HEADS * DIFF_QK_DIM,
    2 * DIFF_HEADS * DIFF_QK_DIM,
    DIFF_HEADS * DIFF_V_DIM,
    RET_HEADS * RET_QK_DIM,
    RET_HEADS * RET_QK_DIM,
    RET_HEADS * RET_V_DIM,
    RET_HEADS * RET_V_DIM,
]
IN_WIDTH = sum(_SPLIT_SIZES)


def rmsnorm(x, gain=None, eps=1e-6):
    xf = x.astype(jnp.float32)
    y = xf * lax.rsqrt(jnp.mean(xf * xf, axis=-1, keepdims=True) + eps)
    if gain is not None:
        y = y * gain.astype(jnp.float32)
    return y.astype(x.dtype)


def t5_bucket(n):
    exact = n < T5_MAX_EXACT
    nf = jnp.maximum(n, 1).astype(jnp.float32)
    large = T5_MAX_EXACT + (jnp.log(nf / T5_MAX_EXACT)
                            / math.log(T5_MAX_DISTANCE / T5_MAX_EXACT)
                            * (T5_BUCKETS - T5_MAX_EXACT)).astype(jnp.int32)
    large = jnp.minimum(large, T5_BUCKETS - 1)
    return jnp.where(exact, n, large)


def rotary(x, pos):
    half = x.shape[-1] // 2
    freqs = 1.0 / (ROPE_BASE ** (jnp.arange(half, dtype=jnp.float32) / half))
    ang = pos[:, None] * freqs[None, :]
    cos, sin = jnp.cos(ang), jnp.sin(ang)
    xf = x.astype(jnp.float32)
    x1, x2 = xf[..., :half], xf[..., half:]
    return jnp.concatenate([x1 * cos - x2 * sin, x1 * sin + x2 * cos], axis=-1).astype(x.dtype)


def diff_attention(q, k, v, lam, rel_bias):
    S = q.shape[3]
    scale = DIFF_QK_DIM ** -0.5
    outs = []
    for blk in range(S // Q_BLOCK):
        start = blk * Q_BLOCK
        end = start + Q_BLOCK
        qb = q[:, :, :, start:end]
        kb = k[:, :, :, :end]
        vb = v[:, :, :end]
        dist = (start + jnp.arange(Q_BLOCK, dtype=jnp.int32))[:, None] - jnp.arange(end, dtype=jnp.int32)[None, :]
        bias = jnp.transpose(rel_bias[t5_bucket(jnp.maximum(dist, 0))], (2, 0, 1)).astype(jnp.float32)
        logits = jnp.einsum('bmhqd,bmhkd->bmhqk', qb, kb).astype(jnp.float32) * scale + bias
        logits = jnp.where(dist >= 0, logits, -jnp.inf)
        probs = jax.nn.softmax(logits, axis=-1)
        attn = probs[:, 0] - lam * probs[:, 1]
        outs.append(jnp.einsum('bhqk,bhkd->bhqd', attn.astype(v.dtype), vb))
    return jnp.concatenate(outs, axis=2)


def retention(q, k, v, log_gamma):
    B, H, S, dk = q.shape
    dv = v.shape[-1]
    C = RET_CHUNK
    N = S // C
    qc = q.reshape(B, H, N, C, dk)
    kc = k.reshape(B, H, N, C, dk)
    vc = v.reshape(B, H, N, C, dv)
    i = jnp.arange(C, dtype=jnp.float32)
    rel = i[:, None] - i[None, :]
    decay = jnp.where(rel >= 0, jnp.exp(jnp.maximum(rel, 0.0)[None] * log_gamma[:, None, None]), 0.0)
    zeta = jnp.exp((C - 1 - i)[None, :] * log_gamma[:, None])
    xi = jnp.exp((i + 1)[None, :] * log_gamma[:, None])
    chunk_decay = jnp.exp(C * log_gamma)
    scores = jnp.einsum('bhnid,bhnjd->bhnij', qc, kc) * decay[None, :, None]
    inner = jnp.einsum('bhnij,bhnje->bhnie', scores, vc)
    kv = jnp.einsum('bhnjd,bhnje->nbhde', kc * zeta[None, :, None, :, None], vc)

    def step(state, kv_n):
        return chunk_decay[None, :, None, None] * state + kv_n, state

    _, state_prev = lax.scan(step, jnp.zeros(kv.shape[1:], kv.dtype), kv)
    cross = jnp.einsum('bhnid,nbhde->bhnie', qc * xi[None, :, None, :, None], state_prev)
    return (inner + cross).reshape(B, H, S, dv).astype(v.dtype)


def peer(m, w_query, sub_keys, u, v):
    B, S, D = m.shape
    qry = (m @ w_query).reshape(B, S, PEER_HEADS, 2, PEER_KEY_DIM)
    scores = jnp.einsum('bshcd,hckd->bshck', qry, sub_keys).astype(jnp.float32)
    vals, idx = lax.top_k(scores, PEER_TOPK)
    cand = vals[..., 0, :, None] + vals[..., 1, None, :]
    cand_idx = idx[..., 0, :, None] * PEER_N_KEYS + idx[..., 1, None, :]
    cand = cand.reshape(B, S, PEER_HEADS, PEER_TOPK * PEER_TOPK)
    cand_idx = cand_idx.reshape(B, S, PEER_HEADS, PEER_TOPK * PEER_TOPK)
    top_s, top_pos = lax.top_k(cand, PEER_TOPK)
    expert = jnp.take_along_axis(cand_idx, top_pos, axis=-1)
    gates = jax.nn.softmax(top_s, axis=-1)
    T = B * S
    n_sel = PEER_HEADS * PEER_TOPK
    tokens = m.reshape(T // PEER_CHUNK, PEER_CHUNK, D)
    expert = expert.reshape(T // PEER_CHUNK, PEER_CHUNK, n_sel)
    gates = gates.reshape(T // PEER_CHUNK, PEER_CHUNK, n_sel)

    def chunk(args):
        xt, e, g = args
        u_sel = jnp.take(u, e, axis=0)
        act = jax.nn.gelu(jnp.einsum('cd,ced->ce', xt, u_sel).astype(jnp.float32), approximate=False) * g
        v_sel = jnp.take(v, e, axis=0)
        return jnp.einsum('ce,ced->cd', act.astype(xt.dtype), v_sel)

    out = lax.map(chunk, (tokens, expert, gates))
    return out.reshape(B, S, D)


def setup_inputs(seed: int = 0) -> dict:
    key = jax.random.key(seed)
    ks = jax.random.split(key, 24)
    f32 = jnp.float32
    nrm = lambda k, shape, s: (jax.random.normal(k, shape, f32) * s)
    return {
        "x": nrm(ks[0], (BATCH, SEQ, D_MODEL), 1.0),
        "p": nrm(ks[1], (DEPTH, BATCH, SEQ, PLE_DIM), 1.0),
        "attn_norm": 1.0 + nrm(ks[2], (DEPTH, D_MODEL), 0.02),
        "w_in": nrm(ks[3], (DEPTH, D_MODEL, IN_WIDTH), D_MODEL ** -0.5),
        "lam_q1": nrm(ks[4], (DEPTH, DIFF_QK_DIM), 0.1),
        "lam_k1": nrm(ks[5], (DEPTH, DIFF_QK_DIM), 0.1),
        "lam_q2": nrm(ks[6], (DEPTH, DIFF_QK_DIM), 0.1),
        "lam_k2": nrm(ks[7], (DEPTH, DIFF_QK_DIM), 0.1),
        "subln_gain": 1.0 + nrm(ks[8], (DEPTH, DIFF_V_DIM), 0.02),
        "w_out": nrm(ks[9], (DEPTH, MIX_WIDTH, D_MODEL), MIX_WIDTH ** -0.5),
        "rel_bias": nrm(ks[10], (T5_BUCKETS, DIFF_HEADS), 0.2),
        "ffn_norm": 1.0 + nrm(ks[11], (DEPTH, D_MODEL), 0.02),
        "peer_query": nrm(ks[12], (DEPTH, D_MODEL, PEER_HEADS * PEER_QUERY_DIM), D_MODEL ** -0.5),
        "peer_subkeys": nrm(ks[13], (DEPTH, PEER_HEADS, 2, PEER_N_KEYS, PEER_KEY_DIM), PEER_KEY_DIM ** -0.5),
        "peer_u": nrm(ks[14], (DEPTH, PEER_N_EXPERTS, D_MODEL), D_MODEL ** -0.5),
        "peer_v": nrm(ks[15], (DEPTH, PEER_N_EXPERTS, D_MODEL), (PEER_HEADS * PEER_TOPK) ** -0.5),
        "ple_norm": 1.0 + nrm(ks[16], (DEPTH, D_MODEL), 0.02),
        "ple_gate_w": nrm(ks[17], (DEPTH, D_MODEL, D_MODEL), D_MODEL ** -0.5),
        "ple_gate_b": nrm(ks[18], (DEPTH, D_MODEL), 0.01),
        "ple_proj": nrm(ks[19], (DEPTH, PLE_DIM, D_MODEL), PLE_DIM ** -0.5),
        "final_norm": 1.0 + nrm(ks[20], (D_MODEL,), 0.02),
    }


def reference(x, p, attn_norm, w_in, lam_q1, lam_k1, lam_q2, lam_k2, subln_gain, w_out,
              rel_bias, ffn_norm, peer_query, peer_subkeys, peer_u, peer_v,
              ple_norm, ple_gate_w, ple_gate_b, ple_proj, final_norm):
    B, S, D = x.shape
    pos = jnp.arange(S, dtype=jnp.float32)
    gamma = 1.0 - jnp.exp2(-5.0 - jnp.arange(RET_HEADS, dtype=jnp.float32))
    log_gamma = jnp.log(gamma)
    split_at = [int(s) for s in np.cumsum(_SPLIT_SIZES)[:-1]]
    h = x
    for layer in range(DEPTH):
        a = rmsnorm(h, attn_norm[layer])
        proj = a @ w_in[layer]
        dq, dk, dv, rq, rk, rv, rg = jnp.split(proj, split_at, axis=-1)

        dq = dq.reshape(B, S, 2, DIFF_HEADS, DIFF_QK_DIM).transpose(0, 2, 3, 1, 4)
        dk = dk.reshape(B, S, 2, DIFF_HEADS, DIFF_QK_DIM).transpose(0, 2, 3, 1, 4)
        dv = dv.reshape(B, S, DIFF_HEADS, DIFF_V_DIM).transpose(0, 2, 1, 3)
        lambda_init = 0.8 - 0.6 * math.exp(-0.3 * layer)
        lam = (jnp.exp(jnp.sum(lam_q1[layer].astype(jnp.float32) * lam_k1[layer].astype(jnp.float32)))
               - jnp.exp(jnp.sum(lam_q2[layer].astype(jnp.float32) * lam_k2[layer].astype(jnp.float32)))
               + lambda_init)
        o = diff_attention(dq, dk, dv, lam, rel_bias)
        o = rmsnorm(o, subln_gain[layer], eps=1e-5) * (1.0 - lambda_init)
        diff_out = o.transpose(0, 2, 1, 3).reshape(B, S, DIFF_HEADS * DIFF_V_DIM)

        rq = rotary(rq.reshape(B, S, RET_HEADS, RET_QK_DIM).transpose(0, 2, 1, 3), pos)
        rk = rotary(rk.reshape(B, S, RET_HEADS, RET_QK_DIM).transpose(0, 2, 1, 3), pos) * (RET_QK_DIM ** -0.5)
        rv = rv.reshape(B, S, RET_HEADS, RET_V_DIM).transpose(0, 2, 1, 3)
        r = rmsnorm(retention(rq, rk, rv, log_gamma))
        ret_out = r.transpose(0, 2, 1, 3).reshape(B, S, RET_HEADS * RET_V_DIM) * jax.nn.silu(rg)

        h = h + jnp.concatenate([diff_out, ret_out], axis=-1) @ w_out[layer]

        m = rmsnorm(h, ffn_norm[layer])
        h = h + peer(m, peer_query[layer], peer_subkeys[layer], peer_u[layer], peer_v[layer])

        n = rmsnorm(h, ple_norm[layer])
        gate = jax.nn.sigmoid(n @ ple_gate_w[layer] + ple_gate_b[layer])
        h = h + gate * (p[layer] @ ple_proj[layer])
    return rmsnorm(h, final_norm)
```

```python
import math
from contextlib import ExitStack
import numpy as np
import ml_dtypes
import concourse.bass as bass
import concourse.mybir as mybir
from concourse.bass_utils import run_bass_kernel_spmd

F32 = mybir.dt.float32
BF16 = mybir.dt.bfloat16
U32 = mybir.dt.uint32
I32 = mybir.dt.int32
ALU = mybir.AluOpType
AF = mybir.ActivationFunctionType
AX = mybir.AxisListType

D = 1024
NE = 16384
LAMBDA_INIT = 0.8 - 0.6 * math.exp(-0.3 * 0)
GAMMAS = [1.0 - 2.0 ** (-5.0 - h) for h in range(4)]
PIPE = 2


class Chan:
    def __init__(self, sem, step, name):
        self.sem, self.step, self.n, self.name = sem, step, 0, name


class Buf:
    __slots__ = ("name", "last_w", "readers")

    def __init__(self, name):
        self.name = name
        self.last_w = None
        self.readers = []


class Op:
    __slots__ = ("emit", "waits", "chan")


class Prog:
    ENGS = ("pe", "act", "dve", "pool", "sp")

    def __init__(self, nc, es):
        self.nc, self.es = nc, es
        self.ops = {e: [] for e in self.ENGS}
        self.allchans = []
        self.chan = {e: self.new_chan("c_" + e, 1) for e in self.ENGS}
        self.seen = {e: {} for e in self.ENGS}

    def new_chan(self, name, step=16):
        sem = self.es.enter_context(self.nc.semaphore(name))
        c = Chan(sem, step, name)
        self.allchans.append(c)
        return c

    defer_list = None

    def add(self, eng, emit, reads=(), writes=(), chan=None, serial=True):
        if self.defer_list is not None:
            self.defer_list.append(((eng, emit), dict(reads=reads, writes=writes, chan=chan, serial=serial)))
            return
        deps = {}

        def need(cv):
            if cv is not None and deps.get(cv[0], 0) < cv[1]:
                deps[cv[0]] = cv[1]

        for b in reads:
            need(b.last_w)
        for b in writes:
            need(b.last_w)
            for r in b.readers:
                need(r)
        own = self.chan[eng]
        if chan is None:
            chan = own
        elif chan.n > 0 and serial:
            need((chan, chan.n))
        seen = self.seen[eng]
        waits = []
        for c, v in deps.items():
            if c is own and eng == "pe":
                continue
            if seen.get(c, 0) >= v:
                continue
            seen[c] = v
            waits.append((c.sem, v))
        chan.n += chan.step
        op = Op()
        op.emit, op.waits, op.chan = emit, waits, chan
        self.ops[eng].append(op)
        cv = (chan, chan.n)
        for b in writes:
            b.last_w = cv
            b.readers = []
        for b in reads:
            if b not in writes:
                b.readers.append(cv)

    def barrier(self):
        for eng in self.ENGS:
            seen = self.seen[eng]
            waits = []
            for c in self.allchans:
                if c.n > seen.get(c, 0):
                    seen[c] = c.n
                    waits.append((c.sem, c.n))
            op = Op()
            op.emit, op.waits, op.chan = None, waits, None
            self.ops[eng].append(op)

    def emit_all(self):
        nc = self.nc
        with nc.Block() as block:
            def run(name):
                def body(e):
                    for op in self.ops[name]:
                        for sem, v in op.waits:
                            e.wait_ge(sem, v)
                        if op.emit is not None:
                            op.emit(e).then_inc(op.chan.sem, op.chan.step)
                return body
            block.tensor(run("pe"))
            block.scalar(run("act"))
            block.vector(run("dve"))
            block.gpsimd(run("pool"))
            block.sync(run("sp"))


class Arena:
    def __init__(self, t, words):
        self.t, self.words, self.off = t, words, 0

    def reset(self, off=0):
        self.off = off

    def f32(self, *shape):
        n = int(np.prod(shape[1:]))
        n = (n + 7) // 8 * 8
        assert self.off + n <= self.words, ("arena overflow", self.off + n, self.words)
        ap = self.t[0:shape[0], self.off:self.off + int(np.prod(shape[1:]))]
        self.off += n
        return _shape(ap, shape)

    def typed(self, dt, *shape):
        per = {BF16: 2, U32: 1, I32: 1}[dt]
        n_el = int(np.prod(shape[1:]))
        n = (n_el + per - 1) // per
        n8 = (n + 7) // 8 * 8
        assert self.off + n8 <= self.words, ("arena overflow", self.off + n8, self.words)
        ap = self.t[0:shape[0], self.off:self.off + n].bitcast(dt)
        if per == 2 and n_el % 2:
            ap = ap[:, 0:n_el]
        self.off += n8
        return _shape(ap, shape)


def _shape(ap, shape):
    if len(shape) == 2:
        return ap
    if len(shape) == 3:
        return ap.rearrange("p (a b) -> p a b", a=shape[1], b=shape[2])
    if len(shape) == 4:
        return ap.rearrange("p (a b c) -> p a b c", a=shape[1], b=shape[2], c=shape[3])
    raise ValueError(shape)


def t5_bucket_np(n):
    n = np.asarray(n)
    nf = np.maximum(n, 1).astype(np.float32)
    large = 16 + (np.log(nf / np.float32(16)) / np.float32(math.log(128 / 16)) * np.float32(16)).astype(np.int32)
    large = np.minimum(large, 31)
    return np.where(n < 16, n, large)


def build(S, dbg=False, stop=None):
    NT = S // 128
    nc = bass.Bass("TRN2", target_bir_lowering=False)
    dram = {}

    def din(name, shape, dt=F32):
        dram[name] = nc.dram_tensor(name, list(shape), dt, kind="ExternalInput").ap()
        return dram[name]

    x_d = din("x", [S, D])
    p_d = din("p", [S, 256])
    attn_norm_d = din("attn_norm", [1, D])
    w_in_d = din("w_in", [D, 3072])
    lam_d = din("lam", [4, 64])
    subln_d = din("subln_gain", [1, 128])
    w_out_d = din("w_out", [D, D])
    ffn_norm_d = din("ffn_norm", [1, D])
    wq_d = din("peer_query", [D, 2048])
    sk_d = din("peer_subkeys", [16, 128, 128])
    u_d = din("peer_u", [NE, D])
    v_d = din("peer_v", [NE, D])
    ple_norm_d = din("ple_norm", [1, D])
    gw_d = din("ple_gate_w", [D, D])
    gb_d = din("ple_gate_b", [1, D])
    pp_d = din("ple_proj", [256, D])
    fin_d = din("final_norm", [1, D])
    relb31_d = din("relb31", [1, 4])
    nbias_d = din("nbias", [128, 4, 2, 128])
    cmask_d = din("cmask", [128, 128])
    cos_d = din("cosF", [128, S])
    sin_d = din("sinF", [128, S])
    decay_d = din("decayT", [128, 4, 128])
    zeta_d = din("zeta", [128, 4])
    xi_d = din("xiF", [128, 2, 128])
    iota_d = din("iota16", [128, 16])
    out_d = nc.dram_tensor("out", [S, D], F32, kind="ExternalOutput").ap()
    cT_d = nc.dram_tensor("cT_scr", [NT, 128, 8, 128], BF16, kind=("ExternalOutput" if dbg else "Internal")).ap()
    qk_d = nc.dram_tensor("qk_scr", [128, 2, 4, S], BF16).ap()
    ub_d = nc.dram_tensor("u_bf16", [NE, D], BF16).ap()
    vb_d = nc.dram_tensor("v_bf16", [NE, D], BF16).ap()
    dbg_d = {}
    if dbg:
        dbg_d["h1"] = nc.dram_tensor("dbg_h1", [S, D], F32, kind="ExternalOutput").ap()
        dbg_d["h2"] = nc.dram_tensor("dbg_h2", [S, D], F32, kind="ExternalOutput").ap()
        dbg_d["eidx"] = nc.dram_tensor("dbg_eidx", [S, 128], I32, kind="ExternalOutput").ap()
        dbg_d["gate"] = nc.dram_tensor("dbg_gate", [S, 128], F32, kind="ExternalOutput").ap()

    with ExitStack() as es:
        P = Prog(nc, es)
        AW = 47 * 1024
        arena_t = es.enter_context(nc.sbuf_tensor("arena", [128, AW], F32))
        A = Arena(arena_t, AW)
        psum_t = es.enter_context(nc.psum_tensor("psum", [128, 4096], F32))

        def bank(b, n=1):
            return psum_t[:, b * 512:(b + n) * 512]

        def bank_bf(b):
            return psum_t[:, b * 512:(b + 1) * 512].bitcast(BF16)

        PB = [Buf("bank%d" % i) for i in range(8)]

        ch = {k: P.new_chan(k) for k in
              ["ld0", "ld1", "ld2", "ld3", "st0", "st1", "w0", "w1", "c0", "c1", "c2", "q0", "q1", "pst0", "pst1", "cv0", "cv1", "pcv0", "pcv1"]}
        gch = [P.new_chan("g%d" % i) for i in range(8)]
        d_cT = [Buf("d_cT%d" % t) for t in range(NT)]

        def dma(q, out, in_, chan, reads=(), writes=(), **kw):
            P.add(q, lambda e: e.dma_start(out=out, in_=in_, **kw), reads=reads, writes=writes, chan=chan)

        ident = A.f32(128, 128); b_ident = Buf("ident")
        ident_bf = A.typed(BF16, 128, 128); b_identbf = Buf("identbf")
        P.add("pool", lambda e: e.memset(ident, 0.0), writes=[b_ident])
        P.add("pool", lambda e: e.affine_select(out=ident, in_=ident, pattern=[[-1, 128]], compare_op=ALU.not_equal,
                                                fill=1.0, base=0, channel_multiplier=1), reads=[b_ident], writes=[b_ident])
        P.add("pool", lambda e: e.tensor_copy(out=ident_bf, in_=ident), reads=[b_ident], writes=[b_identbf])

        b_const = Buf("const")
        subln = A.f32(128, 128)
        lamv = A.f32(128, 4, 64)
        relb31 = A.f32(128, 4)
        iota16 = A.f32(128, 16)
        zeta = A.f32(128, 4)
        def bcast(src):
            return src.rearrange("a b -> (a b)").partition_broadcast(128)

        for dst, src in [(subln, subln_d), (relb31, relb31_d)]:
            dma("sp", dst, bcast(src), ch["c0"], writes=[b_const])
        dma("sp", lamv.rearrange("p a b -> p (a b)"), bcast(lam_d), ch["c0"], writes=[b_const])
        dma("sp", iota16, iota_d, ch["c0"], writes=[b_const])
        dma("sp", zeta, zeta_d, ch["c0"], writes=[b_const])

        sm = A.f32(128, 16); b_sm = Buf("sm")
        lt = A.f32(128, 2, 64); b_lt = Buf("lt")
        P.add("dve", lambda e: e.tensor_tensor(out=lt[:, 0, :], in0=lamv[:, 0, :], in1=lamv[:, 1, :], op=ALU.mult), reads=[b_const], writes=[b_lt])
        P.add("dve", lambda e: e.tensor_tensor(out=lt[:, 1, :], in0=lamv[:, 2, :], in1=lamv[:, 3, :], op=ALU.mult), reads=[b_const, b_lt], writes=[b_lt])
        P.add("dve", lambda e: e.tensor_reduce(out=sm[:, 0:2], in_=lt, axis=AX.X, op=ALU.add), reads=[b_lt], writes=[b_sm])
        P.add("act", lambda e: e.activation(out=sm[:, 2:4], in_=sm[:, 0:2], func=AF.Exp), reads=[b_sm], writes=[b_sm])
        P.add("dve", lambda e: e.scalar_tensor_tensor(out=sm[:, 4:5], in0=sm[:, 3:4], scalar=-LAMBDA_INIT, in1=sm[:, 2:3], op0=ALU.add, op1=ALU.subtract), reads=[b_sm], writes=[b_sm])
        neglam = sm[:, 4:5]
        P.add("dve", lambda e: e.tensor_scalar(out=subln, in0=subln, scalar1=1.0 - LAMBDA_INIT, scalar2=None, op0=ALU.mult), reads=[b_const], writes=[b_const])
        const_end = A.off
        if stop == "const":
            P.barrier()
            P.emit_all()
            return nc

        v_all = A.typed(BF16, 128, NT, 4, 130)
        b_q = [Buf("q%d" % t) for t in range(NT)]
        b_v0 = Buf("v_ones")
        P.add("pool", lambda e: e.memset(v_all[:, :, :, 128:130], 1.0), writes=[b_v0])
        nb_bf = A.typed(BF16, 128, 4, 2, 128); b_nb = Buf("nb")
        RPC = 2
        cv_in = [A.f32(128, RPC * D) for _ in range(2)]; b_cvin = [Buf("cvin0"), Buf("cvin1")]
        cv_out = [A.typed(BF16, 128, RPC * D) for _ in range(2)]; b_cvout = [Buf("cvout0"), Buf("cvout1")]
        b_tab = Buf("tab_bf")
        rpp = NE // 128
        conv_jobs = []
        for src_t, dst_t in [(u_d, ub_d), (v_d, vb_d)]:
            sv = src_t.rearrange("(p r) d -> p r d", p=128)
            dv = dst_t.rearrange("(p r) d -> p r d", p=128)
            for j in range(0, rpp, RPC):
                conv_jobs.append((sv[:, j:j + RPC, :], dv[:, j:j + RPC, :]))
        conv_i = [0]

        def conv_step(n=1):
            for _ in range(n):
                if conv_i[0] >= len(conv_jobs):
                    return
                src, dst = conv_jobs[conv_i[0]]
                sl = conv_i[0] % 2
                conv_i[0] += 1
                dma("sp", cv_in[sl].rearrange("p (r d) -> p r d", r=RPC), src, ch["cv%d" % sl], writes=[b_cvin[sl]])
                P.add("pool", lambda e, sl=sl: e.tensor_copy(out=cv_out[sl], in_=cv_in[sl]), reads=[b_cvin[sl]], writes=[b_cvout[sl]])
                dma("pool", dst, cv_out[sl].rearrange("p (r d) -> p r d", r=RPC), ch["pcv%d" % sl], reads=[b_cvout[sl]], writes=[b_tab])

        ab_end = A.off
        qk_t = [A.typed(BF16, 128, 2, 4, 128) for _ in range(2)]; b_qkt = [Buf("qkt0"), Buf("qkt1")]

        g_attn = A.f32(128, D)
        dma("sp", g_attn, bcast(attn_norm_d), ch["c0"], writes=[b_const])
        w_in_bf = A.typed(BF16, 128, 8, 3072); b_win = Buf("w_in")
        wrot_bf = A.typed(BF16, 128, 8, 512); b_wrot = Buf("wrot")
        stage = [A.f32(128, 3072) for _ in range(1)]; b_stage = [Buf("stage0")]
        w_view = w_in_d.rearrange("(kc kp) n -> kc kp n", kp=128)
        for kc in range(8):
            s = 0
            dma("sp", stage[s], w_view[kc], ch["w%d" % s], writes=[b_stage[s]])
            eng = ["act", "dve"][kc % 2]
            if eng == "act":
                P.add("act", lambda e, s=s, kc=kc: e.copy(out=w_in_bf[:, kc, :], in_=stage[s]), reads=[b_stage[s]], writes=[b_win])
            else:
                P.add("dve", lambda e, s=s, kc=kc: e.tensor_copy(out=w_in_bf[:, kc, :], in_=stage[s]), reads=[b_stage[s]], writes=[b_win])
            src = stage[s][:, 1536:2048].rearrange("p (a b c) -> p a b c", a=8, b=2, c=32)
            dst = wrot_bf[:, kc, :].rearrange("p (a b c) -> p a b c", a=8, b=2, c=32)
            P.add("pool", lambda e, src=src, dst=dst: e.tensor_scalar(out=dst[:, :, 0, :], in0=src[:, :, 1, :], scalar1=-1.0, scalar2=None, op0=ALU.mult), reads=[b_stage[s]], writes=[b_wrot])
            P.add("pool", lambda e, src=src, dst=dst: e.tensor_copy(out=dst[:, :, 1, :], in_=src[:, :, 0, :]), reads=[b_stage[s]], writes=[b_wrot])

        nb_f = A.f32(128, 4, 2, 128); b_nbf = Buf("nbf")
        cmask = A.f32(128, 128)
        dma("sp", nb_f.rearrange("p a b c -> p (a b c)"), nbias_d.rearrange("p a b c -> p (a b c)"), ch["c1"], writes=[b_nbf])
        dma("sp", cmask, cmask_d, ch["c1"], writes=[b_const])
        for h in range(4):
            P.add("dve", lambda e, h=h: e.tensor_scalar(out=nb_f[:, h], in0=nb_f[:, h], scalar1=relb31[:, h:h + 1], scalar2=None, op0=ALU.subtract), reads=[b_const, b_nbf], writes=[b_nbf])
            P.add("dve", lambda e, h=h: e.tensor_tensor(out=nb_f[:, h, 0, :], in0=nb_f[:, h, 0, :], in1=cmask, op=ALU.add), reads=[b_const, b_nbf], writes=[b_nbf])
        P.add("dve", lambda e: e.tensor_copy(out=nb_bf, in_=nb_f), reads=[b_nbf], writes=[b_nb])

        decayT = A.f32(128, 4, 128)
        xiF = A.f32(128, 2, 128)
        dma("sp", decayT.rearrange("p a b -> p (a b)"), decay_d.rearrange("p a b -> p (a b)"), ch["c1"], writes=[b_const])
        dma("sp", xiF.rearrange("p a b -> p (a b)"), xi_d.rearrange("p a b -> p (a b)"), ch["c1"], writes=[b_const])

        xt = [A.f32(128, D) for _ in range(2)]; b_xt = [Buf("xt0"), Buf("xt1")]
        cst = [A.f32(128, 2, 128) for _ in range(2)]; b_cs = [Buf("cs0"), Buf("cs1")]
        junk = A.f32(128, D); b_junk = Buf("junk")
        a_bf = A.typed(BF16, 128, D); b_abf = Buf("a_bf")
        aT = A.typed(BF16, 128, 8, 128); b_aT = Buf("aT")
        st = A.f32(128, 16); b_st = Buf("st")
        rot1 = A.f32(128, 4, 128); b_rot1 = Buf("rot1")
        rot2 = A.f32(128, 4, 128); b_rot2 = Buf("rot2")
        rqT = A.typed(BF16, 128, 2, 128); b_rqT = Buf("rqT")
        rkT = A.typed(BF16, 128, 2, 128); b_rkT = Buf("rkT")
        qxT = A.typed(BF16, 128, 2, 128); b_qxT = Buf("qxT")
        rqTm = A.typed(BF16, 128, 2, 128); b_rqTm = Buf("rqTm")
        qxTm = A.typed(BF16, 128, 2, 128); b_qxTm = Buf("qxTm")
        P.add("pool", lambda e: e.memset(rqTm, 0.0), writes=[b_rqTm])
        P.add("pool", lambda e: e.memset(qxTm, 0.0), writes=[b_qxTm])
        rv_bf = A.typed(BF16, 128, 512); b_rv = Buf("rv")
        sg = A.f32(128, 512); b_sg = Buf("sg")
        sT_bf = A.typed(BF16, 128, 4, 128); b_sT = Buf("sT")
        kz = A.typed(BF16, 128, 2, 128); b_kz = Buf("kz")
        state = A.f32(128, 2, 128); b_state = Buf("state")
        state_bf = A.typed(BF16, 128, 2, 128); b_statebf = Buf("statebf")
        sq = A.f32(128, 512); b_sq = Buf("sq")
        yr = A.f32(128, 4, 128); b_yr = Buf("yr")
        y_bf = A.typed(BF16, 128, 512); b_ybf = Buf("ybf")
        retT = [A.typed(BF16, 128, 4, 128) for _ in range(2)]; b_retT = [Buf("retT0"), Buf("retT1")]
        P.add("dve", lambda e: e.memset(state, 0.0), writes=[b_state])
        P.add("dve", lambda e: e.memset(state_bf, 0.0), writes=[b_statebf])

        x_view = x_d.rearrange("(t p) n -> t p n", p=128)
        if stop == "A0":
            P.barrier(); P.emit_all(); return nc
        for t in range(NT):
            s = t % 2
            tc_ = slice(t * 128, (t + 1) * 128)
            dma("sp", xt[s], x_view[t], ch["ld%d" % s], writes=[b_xt[s]])
            dma("sp", cst[s][:, 0, :], cos_d[:, tc_], ch["ld%d" % (2 + s)], writes=[b_cs[s]])
            dma("sp", cst[s][:, 1, :], sin_d[:, tc_], ch["ld%d" % (2 + s)], writes=[b_cs[s]])
            P.add("act", lambda e, s=s: e.activation(out=junk, in_=xt[s], func=AF.Square, accum_out=st[:, 0:1]), reads=[b_xt[s]], writes=[b_junk, b_st])
            P.add("act", lambda e: e.activation(out=st[:, 1:2], in_=st[:, 0:1], func=AF.Sqrt, scale=1.0 / D, bias=1e-6), reads=[b_st], writes=[b_st])
            P.add("dve", lambda e: e.reciprocal(out=st[:, 2:3], in_=st[:, 1:2]), reads=[b_st], writes=[b_st])
            P.add("dve", lambda e, s=s: e.scalar_tensor_tensor(out=a_bf, in0=xt[s], scalar=st[:, 2:3], in1=g_attn, op0=ALU.mult, op1=ALU.mult), reads=[b_xt[s], b_st, b_const], writes=[b_abf])
            pT = bank_bf(0).rearrange("p (a b) -> p a b", a=8, b=128)
            for kc in range(8):
                P.add("pe", lambda e, kc=kc, pT=pT: e.transpose(out=pT[:, kc, :], in_=a_bf[:, kc * 128:(kc + 1) * 128], identity=ident_bf), reads=[b_abf, b_identbf], writes=[PB[0]])
            P.add("act", lambda e, pT=pT: e.copy(out=aT, in_=pT), reads=[PB[0]], writes=[b_aT])
            fm = [(1, 0, w_in_bf), (2, 512, w_in_bf), (3, 1536, w_in_bf), (4, 0, wrot_bf)]
            for bk, c0, W in fm:
                pv = bank(bk).rearrange("p (a b) -> p a b", a=4, b=128)
                for j in range(4):
                    for kc in range(8):
                        P.add("pe", lambda e, pv=pv, j=j, kc=kc, W=W, c0=c0: e.matmul(out=pv[:, j, :], lhsT=W[:, kc, c0 + j * 128:c0 + (j + 1) * 128], rhs=aT[:, kc, :], start=(kc == 0), stop=(kc == 7)),
                              reads=[b_aT, b_win, b_wrot], writes=[PB[bk]])
            for g, (bk, c0) in enumerate([(5, 1024), (6, 2048), (7, 2560)]):
                for kc in range(8):
                    P.add("pe", lambda e, bk=bk, c0=c0, kc=kc: e.matmul(out=bank(bk), lhsT=aT[:, kc, :], rhs=w_in_bf[:, kc, c0:c0 + 512], start=(kc == 0), stop=(kc == 7)),
                          reads=[b_aT, b_win], writes=[PB[bk]])
            pq = bank(1).rearrange("p (a b) -> p a b", a=4, b=128)
            pk = bank(2).rearrange("p (a b) -> p a b", a=4, b=128)
            P.add("act", lambda e, pq=pq, s=s: e.activation(out=qk_t[s][:, 0], in_=pq, func=AF.Copy, scale=0.125), reads=[PB[1]], writes=[b_qkt[s]])
            P.add("act", lambda e, pk=pk, s=s: e.copy(out=qk_t[s][:, 1], in_=pk), reads=[PB[2]], writes=[b_qkt[s]])
            dma("pool", qk_d[:, 0, :, tc_], qk_t[s][:, 0], ch["q%d" % s], reads=[b_qkt[s]])
            dma("pool", qk_d[:, 1, :, tc_], qk_t[s][:, 1], ch["q%d" % s], reads=[b_qkt[s]])
            pdv = bank(5).rearrange("p (a b) -> p a b", a=4, b=128)
            P.add("act", lambda e, pdv=pdv, t=t: e.copy(out=v_all[:, t, :, 0:128], in_=pdv), reads=[PB[5]], writes=[b_q[t]])
            P.add("dve", lambda e: e.tensor_copy(out=rv_bf, in_=bank(6)), reads=[PB[6]], writes=[b_rv])
            P.add("act", lambda e: e.activation(out=sg, in_=bank(7), func=AF.Silu), reads=[PB[7]], writes=[b_sg])
            if stop == "A1":
                continue
            p3 = bank(3).rearrange("p (a b) -> p a b", a=4, b=128)
            p4 = bank(4).rearrange("p (a b) -> p a b", a=4, b=128)
            cosb = cst[s][:, 0, :].unsqueeze(1).to_broadcast([128, 4, 128])
            sinb = cst[s][:, 1, :].unsqueeze(1).to_broadcast([128, 4, 128])
            P.add("dve", lambda e, p3=p3, cosb=cosb: e.tensor_tensor(out=rot1, in0=p3, in1=cosb, op=ALU.mult), reads=[PB[3], b_cs[s]], writes=[b_rot1])
            P.add("dve", lambda e, p4=p4, sinb=sinb: e.tensor_tensor(out=rot2, in0=p4, in1=sinb, op=ALU.mult), reads=[PB[4], b_cs[s]], writes=[b_rot2])
            P.add("pool", lambda e: e.tensor_tensor(out=rot1, in0=rot1, in1=rot2, op=ALU.add), reads=[b_rot1, b_rot2], writes=[b_rot1])
            P.add("act", lambda e: e.copy(out=rqT, in_=rot1[:, 0:2, :]), reads=[b_rot1], writes=[b_rqT])
            P.add("act", lambda e: e.activation(out=rkT, in_=rot1[:, 2:4, :], func=AF.Copy, scale=0.125), reads=[b_rot1], writes=[b_rkT])
            P.add("pool", lambda e: e.tensor_tensor(out=qxT, in0=rot1[:, 0:2, :], in1=xiF, op=ALU.mult), reads=[b_rot1, b_const], writes=[b_qxT])
            P.add("act", lambda e: e.copy(out=rqTm[64:128], in_=rot1[64:128, 0:2, :]), reads=[b_rot1], writes=[b_rqTm])
            P.add("dve", lambda e: e.tensor_copy(out=qxTm[64:128], in_=qxT[64:128]), reads=[b_qxT], writes=[b_qxTm])
            if stop == "A2":
                continue
            pS = bank(5).rearrange("p (a b) -> p a b", a=4, b=128)
            for h in range(4):
                c, pr = h // 2, slice((h % 2) * 64, (h % 2) * 64 + 64)
                if h % 2 == 0:
                    P.add("pe", lambda e, pS=pS, h=h, c=c: e.matmul(out=pS[:, h, :], lhsT=rkT[0:64, c, :], rhs=rqT[0:64, c, :], start=True, stop=True), reads=[b_rkT, b_rqT], writes=[PB[5]])
                else:
                    P.add("pe", lambda e, pS=pS, h=h, c=c: e.matmul(out=pS[:, h, :], lhsT=rkT[:, c, :], rhs=rqTm[:, c, :], start=True, stop=True), reads=[b_rkT, b_rqTm], writes=[PB[5]])
            P.add("dve", lambda e, pS=pS: e.tensor_tensor(out=sT_bf, in0=pS, in1=decayT, op=ALU.mult), reads=[PB[5], b_const], writes=[b_sT])
            if stop == "A3a":
                continue
            pI = bank(6).rearrange("p (a b) -> p a b", a=4, b=128)
            for h in range(4):
                c, pr = h // 2, slice((h % 2) * 64, (h % 2) * 64 + 64)
                P.add("pe", lambda e, pI=pI, h=h: e.matmul(out=pI[:, h, :], lhsT=sT_bf[:, h, :], rhs=rv_bf[:, h * 128:(h + 1) * 128], start=True, stop=False), reads=[b_sT, b_rv], writes=[PB[6]])
                if h % 2 == 0:
                    P.add("pe", lambda e, pI=pI, h=h, c=c: e.matmul(out=pI[:, h, :], lhsT=qxT[0:64, c, :], rhs=state_bf[0:64, c, :], start=False, stop=True), reads=[b_qxT, b_statebf], writes=[PB[6]])
                else:
                    P.add("pe", lambda e, pI=pI, h=h, c=c: e.matmul(out=pI[:, h, :], lhsT=qxTm[:, c, :], rhs=state_bf[:, c, :], start=False, stop=True), reads=[b_qxTm, b_statebf], writes=[PB[6]])
            if stop == "A3b":
                continue
            pK = bank_bf(0).rearrange("p (a b) -> p a b", a=8, b=128)
            for c in range(2):
                P.add("pe", lambda e, pK=pK, c=c: e.transpose(out=pK[:, c, :], in_=rkT[:, c, :], identity=ident_bf), reads=[b_rkT, b_identbf], writes=[PB[0]])
            P.add("dve", lambda e, pK=pK: e.tensor_tensor(out=kz.rearrange("p a (b c) -> p (a b) c", b=2, c=64), in0=pK[:, 0:2, :].rearrange("p a (b c) -> p (a b) c", b=2, c=64), in1=zeta.unsqueeze(2).to_broadcast([128, 4, 64]), op=ALU.mult), reads=[PB[0], b_const], writes=[b_kz])
            if stop == "A3c":
                continue
            pKV = bank(7).rearrange("p (a b) -> p a b", a=2, b=256)
            for c in range(2):
                P.add("pe", lambda e, pKV=pKV, c=c: e.matmul(out=pKV[:, c, :], lhsT=kz[:, c, :], rhs=rv_bf[:, c * 256:(c + 1) * 256], start=True, stop=True), reads=[b_kz, b_rv], writes=[PB[7]])
            for h in range(4):
                c, hh = h // 2, h % 2
                pr = slice(hh * 64, hh * 64 + 64)
                cd = GAMMAS[h] ** 128
                P.add("dve", lambda e, pKV=pKV, c=c, hh=hh, pr=pr, cd=cd: e.scalar_tensor_tensor(out=state[pr, c, :], in0=state[pr, c, :], scalar=cd, in1=pKV[pr, c, hh * 128:(hh + 1) * 128], op0=ALU.mult, op1=ALU.add), reads=[PB[7], b_state], writes=[b_state])
            P.add("act", lambda e: e.copy(out=state_bf, in_=state), reads=[b_state], writes=[b_statebf])
            if stop == "A3":
                continue
            P.add("act", lambda e: e.activation(out=sq, in_=bank(6), func=AF.Square), reads=[PB[6]], writes=[b_sq])
            P.add("dve", lambda e: e.tensor_reduce(out=st[:, 4:8], in_=sq.rearrange("p (a b) -> p a b", a=4, b=128), axis=AX.X, op=ALU.add), reads=[b_sq], writes=[b_st])
            P.add("act", lambda e: e.activation(out=st[:, 8:12], in_=st[:, 4:8], func=AF.Sqrt, scale=1.0 / 128, bias=1e-6), reads=[b_st], writes=[b_st])
            P.add("dve", lambda e: e.reciprocal(out=st[:, 12:16], in_=st[:, 8:12]), reads=[b_st], writes=[b_st])
            P.add("dve", lambda e, pI=pI: e.tensor_tensor(out=yr, in0=pI, in1=st[:, 12:16].unsqueeze(2).to_broadcast([128, 4, 128]), op=ALU.mult), reads=[PB[6], b_st], writes=[b_yr])
            P.add("pool", lambda e: e.tensor_tensor(out=y_bf, in0=yr.rearrange("p a b -> p (a b)"), in1=sg, op=ALU.mult), reads=[b_yr, b_sg], writes=[b_ybf])
            for h in range(4):
                P.add("pe", lambda e, pK=pK, h=h: e.transpose(out=pK[:, 4 + h, :], in_=y_bf[:, h * 128:(h + 1) * 128], identity=ident_bf), reads=[b_ybf, b_identbf], writes=[PB[0]])
            P.add("act", lambda e, pK=pK, s=s: e.copy(out=retT[s], in_=pK[:, 4:8, :]), reads=[PB[0]], writes=[b_retT[s]])
            dma("pool", cT_d[t, :, 4:8, :], retT[s], ch["pst%d" % s], reads=[b_retT[s]], writes=[d_cT[t]])
            conv_step(max(1, (len(conv_jobs) // 2 + NT - 1) // NT))

        P.barrier()
        if stop in ("A", "A1", "A2", "A3", "A3a", "A3b", "A3c"):
            P.emit_all()
            return nc
        A.reset(ab_end)
        qk_h = [A.typed(BF16, 128, 2, 2, S) for _ in range(2)]
        b_qkh = [Buf("qkh0"), Buf("qkh1")]
        NPT = 3
        PT = [A.typed(BF16, 128, 4, 128) for _ in range(NPT)]; b_PT = [Buf("PT%d" % i) for i in range(NPT)]
        rr = A.f32(128, 16); b_rr = Buf("rr")
        o1 = A.f32(128, 128); b_o1 = Buf("o1")
        o2 = A.f32(128, 128); b_o2 = Buf("o2")
        jk = A.f32(128, 128); b_jk = Buf("jk")
        ob = A.typed(BF16, 128, 128); b_ob = Buf("ob")
        dT = [A.typed(BF16, 128, 128) for _ in range(2)]; b_dT = [Buf("dT0"), Buf("dT1")]
        ST_BANKS = [0, 1, 2]
        O_BANKS = [(3, 4), (5, 6)]
        TR_BANK = 7
        it = 0
        sti = 0
        for h in range(4):
            pr = slice((h % 2) * 64, (h % 2) * 64 + 64)
            hs = h % 2
            for w_ in range(2):
                for m_ in range(2):
                    dma("sp", qk_h[hs][0:64, w_, m_, :], qk_d[pr, w_, m_ * 2 + h // 2, :], ch["ld%d" % (w_ * 2 + m_)], writes=[b_qkh[hs]])
            for qi in range(NT):
                qc = slice(qi * 128, (qi + 1) * 128)
                ob_pair = O_BANKS[it % 2]
                for m in range(2):
                    c = m * 2 + h // 2
                    obk = ob_pair[m]
                    O = bank(obk)[:, 0:130]
                    groups = [list(range(g, min(g + 4, qi + 1))) for g in range(0, qi + 1, 4)]
                    pend = None

                    def do_av(grp, pti, O=O, obk=obk, qi=qi, h=h):
                        for sl, kj in enumerate(grp):
                            P.add("pe", lambda e, sl=sl, kj=kj, pti=pti: e.matmul(out=O, lhsT=PT[pti][:, sl, :], rhs=v_all[:, kj, h, :], start=(kj == 0), stop=(kj == qi)),
                                  reads=[b_PT[pti], b_q[kj], b_v0], writes=[PB[obk]])

                    for grp in groups:
                        sb_ = ST_BANKS[sti % 3]
                        pti = sti % NPT
                        sti += 1
                        STv = bank(sb_).rearrange("p (a b) -> p a b", a=4, b=128)
                        for sl, kj in enumerate(grp):
                            near = kj >= qi - 1
                            kc_ = slice(kj * 128, (kj + 1) * 128)
                            P.add("pe", lambda e, STv=STv, sl=sl, kc_=kc_, near=near, m=m, hs=hs, qc=qc: e.matmul(out=STv[:, sl, :], lhsT=qk_h[hs][0:64, 1, m, kc_], rhs=qk_h[hs][0:64, 0, m, qc], start=True, stop=(not near)),
                                  reads=[b_qkh[hs]], writes=[PB[sb_]])
                            if near:
                                which = 0 if kj == qi else 1
                                P.add("pe", lambda e, STv=STv, sl=sl, which=which, h=h: e.matmul(out=STv[:, sl, :], lhsT=ident_bf, rhs=nb_bf[:, h, which, :], start=False, stop=True),
                                      reads=[b_identbf, b_nb], writes=[PB[sb_]])
                        n = len(grp)
                        P.add("act", lambda e, STv=STv, n=n, pti=pti: e.activation(out=PT[pti][:, 0:n, :], in_=STv[:, 0:n, :], func=AF.Exp), reads=[PB[sb_]], writes=[b_PT[pti]])
                        if pend is not None:
                            do_av(*pend)
                        pend = (grp, pti)
                    do_av(*pend)
                O1 = bank(ob_pair[0]); O2 = bank(ob_pair[1])
                rd = [PB[ob_pair[0]], PB[ob_pair[1]]]
                P.add("dve", lambda e, O1=O1: e.reciprocal(out=rr[:, 0:1], in_=O1[:, 128:129]), reads=rd, writes=[b_rr])
                P.add("dve", lambda e, O2=O2: e.reciprocal(out=rr[:, 1:2], in_=O2[:, 128:129]), reads=rd + [b_rr], writes=[b_rr])
                P.add("dve", lambda e: e.tensor_tensor(out=rr[:, 2:3], in0=rr[:, 1:2], in1=neglam, op=ALU.mult), reads=[b_rr, b_sm], writes=[b_rr])
                P.add("dve", lambda e, O1=O1: e.tensor_scalar(out=o1, in0=O1[:, 0:128], scalar1=rr[:, 0:1], scalar2=None, op0=ALU.mult), reads=rd + [b_rr], writes=[b_o1])
                P.add("dve", lambda e, O2=O2: e.scalar_tensor_tensor(out=o2, in0=O2[:, 0:128], scalar=rr[:, 2:3], in1=o1, op0=ALU.mult, op1=ALU.add), reads=rd + [b_rr, b_o1], writes=[b_o2])
                P.add("act", lambda e: e.activation(out=jk, in_=o2, func=AF.Square, accum_out=rr[:, 3:4]), reads=[b_o2], writes=[b_jk, b_rr])
                P.add("act", lambda e: e.activation(out=rr[:, 4:5], in_=rr[:, 3:4], func=AF.Sqrt, scale=1.0 / 128, bias=1e-5), reads=[b_rr], writes=[b_rr])
                P.add("dve", lambda e: e.reciprocal(out=rr[:, 5:6], in_=rr[:, 4:5]), reads=[b_rr], writes=[b_rr])
                P.add("dve", lambda e: e.scalar_tensor_tensor(out=ob, in0=o2, scalar=rr[:, 5:6], in1=subln, op0=ALU.mult, op1=ALU.mult), reads=[b_o2, b_rr, b_const], writes=[b_ob])
                pTr = bank_bf(TR_BANK)[:, 0:128]
                P.add("pe", lambda e, pTr=pTr: e.transpose(out=pTr, in_=ob, identity=ident_bf), reads=[b_ob, b_identbf], writes=[PB[TR_BANK]])
                s = it % 2
                P.add("act", lambda e, pTr=pTr, s=s: e.copy(out=dT[s], in_=pTr), reads=[PB[TR_BANK]], writes=[b_dT[s]])
                dma("sp", cT_d[qi, :, h, :], dT[s], ch["st%d" % s], reads=[b_dT[s]], writes=[d_cT[qi]])
                it += 1
                conv_step(max(1, (len(conv_jobs) // 2 + 4 * NT - 1) // (4 * NT)))
        conv_step(len(conv_jobs))

        P.barrier()
        if stop == "B":
            P.emit_all()
            return nc
        A.reset(const_end)
        w_out_bf = A.typed(BF16, 128, 8, D); b_wout = Buf("w_out")
        wq_bf = A.typed(BF16, 128, 8, 2048); b_wq = Buf("wq")
        gw_bf = A.typed(BF16, 128, 8, D); b_gw = Buf("gw")
        pp_bf = A.typed(BF16, 128, 2, D); b_pp = Buf("pp")
        skT_bf = A.typed(BF16, 128, 16, 128); b_skT = Buf("skT")
        g_ffn = A.f32(128, D)
        g_ple = A.f32(128, D)
        g_fin = A.f32(128, D)
        gbias = A.f32(128, D)
        for dst, src in [(g_ffn, ffn_norm_d), (g_ple, ple_norm_d), (g_fin, fin_d), (gbias, gb_d)]:
            dma("sp", dst, bcast(src), ch["c0"], writes=[b_const])
        NG = 8
        gall = A.f32(128, 4096)
        gall_bf = gall.bitcast(BF16)
        gbuf = [gall_bf[:, i * D:(i + 1) * D] for i in range(NG)]; b_gbuf = [Buf("gbuf%d" % i) for i in range(NG)]
        stg = [gall[:, 0:2048], gall[:, 2048:4096]]; b_stg = [Buf("stg0"), Buf("stg1")]
        ND = 4
        diag = [A.typed(BF16, 128, 128) for _ in range(ND)]; b_diag = [Buf("diag%d" % i) for i in range(ND)]
        m_bf = A.typed(BF16, 128, D); b_mbf = Buf("m_bf")
        junkb = A.typed(BF16, 128, D); b_junkb = Buf("junkb")
        si = 0

        def load_w(src_ap, ncols, dst_fn, b_dst):
            nonlocal si
            s = si % 2
            si += 1
            dma("sp", stg[s][:, 0:ncols], src_ap, ch["w%d" % s], writes=[b_stg[s]])
            eng = ["act", "dve", "pool"][si % 3]
            if eng == "act":
                P.add("act", lambda e: e.copy(out=dst_fn, in_=stg[s][:, 0:ncols]), reads=[b_stg[s]], writes=[b_dst])
            else:
                P.add(eng, lambda e: e.tensor_copy(out=dst_fn, in_=stg[s][:, 0:ncols]), reads=[b_stg[s]], writes=[b_dst])

        for kc in range(8):
            load_w(w_out_d[kc * 128:(kc + 1) * 128, :], D, w_out_bf[:, kc, :], b_wout)
            load_w(wq_d[kc * 128:(kc + 1) * 128, :], 2048, wq_bf[:, kc, :], b_wq)
            load_w(gw_d[kc * 128:(kc + 1) * 128, :], D, gw_bf[:, kc, :], b_gw)
        for kc in range(2):
            load_w(pp_d[kc * 128:(kc + 1) * 128, :], D, pp_bf[:, kc, :], b_pp)
        for half in range(2):
            s = si % 2
            si += 1
            skv = stg[s].rearrange("p (a b) -> p a b", a=16, b=128)
            dma("sp", skv[:, 0:8, :], sk_d[half * 8:(half + 1) * 8].rearrange("j k d -> k j d"), ch["w%d" % s], writes=[b_stg[s]])
            for q4 in range(2):
                bk = q4
                pv = bank(bk).rearrange("p (a b) -> p a b", a=4, b=128)
                for j in range(4):
                    P.add("pe", lambda e, pv=pv, j=j, skv=skv, q4=q4: e.transpose(out=pv[:, j, :], in_=skv[:, q4 * 4 + j, :], identity=ident), reads=[b_stg[s], b_ident], writes=[PB[bk]])
                P.add("act", lambda e, pv=pv, half=half, q4=q4: e.copy(out=skT_bf[:, half * 8 + q4 * 4: half * 8 + q4 * 4 + 4, :], in_=pv), reads=[PB[bk]], writes=[b_skT])

        cTt = [A.typed(BF16, 128, 8, 128) for _ in range(2)]; b_cTt = [Buf("cTt0"), Buf("cTt1")]
        xt2 = A.f32(128, D); b_xt2 = Buf("xt2")
        ptile = [A.f32(128, 256) for _ in range(2)]; b_pt = [Buf("pt0"), Buf("pt1")]
        hA2 = [A.f32(128, D) for _ in range(2)]; b_hA2 = [Buf("hA0"), Buf("hA1")]
        hB = A.f32(128, D); b_hB = Buf("hB")
        mt = A.f32(128, D); b_mt = Buf("mt")
        mtb = A.f32(128, D); b_mtb = Buf("mtb")
        junk2 = A.f32(128, D); b_junk2 = Buf("junk2")
        junk_f = A.typed(BF16, 128, D); b_junkf = Buf("junk_f")
        mT_bf = A.typed(BF16, 128, 8, 128); b_mT = Buf("mT")
        nT_bf = mT_bf; b_nT = b_mT
        pT_bf = A.typed(BF16, 128, 2, 128); b_pT = Buf("pT")
        qryT_bf = A.typed(BF16, 128, 16, 128); b_qry = Buf("qryT")
        scs = A.f32(128, 16, 128); b_scs = Buf("scs")
        wk = A.f32(128, 256); b_wk = Buf("wk")
        vals = A.f32(128, 16, 16); b_vals = Buf("vals")
        idx = A.typed(U32, 128, 16, 16); b_idx = Buf("idx")
        idxf = A.f32(128, 16, 16); b_idxf = Buf("idxf")
        cand = A.f32(128, 8, 256); b_cand = Buf("cand")
        NJ = 4
        cand_flat = cand.rearrange("p h x -> p (h x)")
        junks = [cand_flat[:, k * 512:(k + 1) * 512].bitcast(BF16) for k in range(NJ)]
        b_junks = [Buf("junk_d%d" % k) for k in range(NJ)]
        ts_ = A.f32(128, 8, 16); b_ts = Buf("ts")
        pos = A.typed(U32, 128, 8, 16); b_pos = Buf("pos")
        pi_u = A.typed(U32, 128, 8, 16); b_piu = Buf("piu")
        pj_u = A.typed(U32, 128, 8, 16); b_pju = Buf("pju")
        pi_f = A.f32(128, 8, 16); b_pif = Buf("pif")
        pj_f = A.f32(128, 8, 16); b_pjf = Buf("pjf")
        oh = scs.rearrange("p (a c) (d e) -> p a c d e", a=8, c=2, d=8, e=16).rearrange("p a c d e -> p a (c d) e"); b_oh = b_scs
        e1 = A.f32(128, 8, 16); b_e1 = Buf("e1")
        e2 = A.f32(128, 8, 16); b_e2 = Buf("e2")
        eidx2 = [A.typed(I32, 128, 128) for _ in range(2)]; b_eidx2 = [Buf("eidx0"), Buf("eidx1")]
        gate2 = [A.f32(128, 8, 16) for _ in range(2)]; b_gate2 = [Buf("gate0"), Buf("gate1")]
        m_bf2 = [m_bf, A.typed(BF16, 128, D)]; b_mbf2 = [b_mbf, Buf("m_bf1")]
        gs = A.f32(128, 16); b_gs = Buf("gs")
        dots = A.f32(128, 128); b_dots = Buf("dots")
        actg = A.f32(128, 128); b_actg = Buf("actg")
        st_f = A.f32(128, 8); b_stf = Buf("st_f")
        st_b = A.f32(128, 8); b_stb = Buf("st_b")
        gsb = junk2; b_gsb = b_junk2
        outt = A.f32(128, D); b_outt = Buf("outt")
        gi = 0
        P.barrier()

        p_view = p_d.rearrange("(t p) n -> t p n", p=128)
        o_view = out_d.rearrange("(t p) n -> t p n", p=128)

        def rms(src, b_src, gain, dst, b_dst, stt, b_stt, col, jk_, b_jk):
            P.add("act", lambda e: e.activation(out=jk_, in_=src, func=AF.Square, accum_out=stt[:, col:col + 1]), reads=[b_src], writes=[b_jk, b_stt])
            P.add("act", lambda e: e.activation(out=stt[:, col + 1:col + 2], in_=stt[:, col:col + 1], func=AF.Sqrt, scale=1.0 / D, bias=1e-6), reads=[b_stt], writes=[b_stt])
            P.add("dve", lambda e: e.reciprocal(out=stt[:, col + 2:col + 3], in_=stt[:, col + 1:col + 2]), reads=[b_stt], writes=[b_stt])
            P.add("dve", lambda e: e.scalar_tensor_tensor(out=dst, in0=src, scalar=stt[:, col + 2:col + 3], in1=gain, op0=ALU.mult, op1=ALU.mult), reads=[b_src, b_stt, b_const], writes=[b_dst])

        def transpose_to_bf(src, b_src, nchunk, dstT, b_dstT, banks):
            for g in range(0, nchunk, 4):
                bk = banks[(g // 4) % len(banks)]
                n = min(4, nchunk - g)
                pv = bank(bk).rearrange("p (a b) -> p a b", a=4, b=128)
                for j in range(n):
                    P.add("pe", lambda e, pv=pv, j=j, g=g: e.transpose(out=pv[:, j, :], in_=src[:, (g + j) * 128:(g + j + 1) * 128], identity=ident), reads=[b_src, b_ident], writes=[PB[bk]])
                P.add("act", lambda e, pv=pv, g=g, n=n: e.copy(out=dstT[:, g:g + n, :], in_=pv[:, 0:n, :]), reads=[PB[bk]], writes=[b_dstT])

        def front(t, part="ABC"):
            if "A" in part:
                frontA(t)
            if "B" in part:
                frontB(t)
            if "C" in part:
                frontC(t)

        def frontA(t):
            s = t % 2
            hA, b_hA = hA2[s], b_hA2[s]
            dma("sp", cTt[s].rearrange("p a b -> p (a b)"), cT_d[t].rearrange("p a b -> p (a b)"), ch["ld%d" % s], reads=[d_cT[t]], writes=[b_cTt[s]])
            dma("sp", xt2, x_view[t], ch["ld2"], writes=[b_xt2])
            dma("sp", ptile[s], p_view[t], ch["c%d" % s], writes=[b_pt[s]])
            for n2 in range(2):
                for c in range(8):
                    P.add("pe", lambda e, n2=n2, c=c, s=s: e.matmul(out=bank(n2), lhsT=cTt[s][:, c, :], rhs=w_out_bf[:, c, n2 * 512:(n2 + 1) * 512], start=(c == 0), stop=(c == 7)),
                          reads=[b_cTt[s], b_wout], writes=[PB[n2]])
            P.add("dve", lambda e: e.tensor_tensor(out=hA, in0=bank(0, 2), in1=xt2, op=ALU.add), reads=[PB[0], PB[1], b_xt2], writes=[b_hA])
            if dbg:
                dma("sp", dbg_d["h1"].rearrange("(t p) n -> t p n", p=128)[t], hA, ch["c2"], reads=[b_hA])
            rms(hA, b_hA, g_ffn, mt, b_mt, st_f, b_stf, 0, junk_f, b_junkf)
            P.add("act", lambda e: e.copy(out=m_bf2[s], in_=mt), reads=[b_mt], writes=[b_mbf2[s]])
            transpose_to_bf(mt, b_mt, 8, mT_bf, b_mT, [2])
            for r4 in range(4):
                pv = bank(3).rearrange("p (a b) -> p a b", a=4, b=128)
                for j in range(4):
                    jj = r4 * 4 + j
                    for kc in range(8):
                        P.add("pe", lambda e, pv=pv, j=j, jj=jj, kc=kc: e.matmul(out=pv[:, j, :], lhsT=wq_bf[:, kc, jj * 128:(jj + 1) * 128], rhs=mT_bf[:, kc, :], start=(kc == 0), stop=(kc == 7)),
                              reads=[b_wq, b_mT], writes=[PB[3]])
                P.add("act", lambda e, pv=pv, r4=r4: e.copy(out=qryT_bf[:, r4 * 4:r4 * 4 + 4, :], in_=pv), reads=[PB[3]], writes=[b_qry])
            for r4 in range(4):
                pv = bank(4).rearrange("p (a b) -> p a b", a=4, b=128)
                for j in range(4):
                    jj = r4 * 4 + j
                    P.add("pe", lambda e, pv=pv, j=j, jj=jj: e.matmul(out=pv[:, j, :], lhsT=qryT_bf[:, jj, :], rhs=skT_bf[:, jj, :], start=True, stop=True), reads=[b_qry, b_skT], writes=[PB[4]])
                P.add("act", lambda e, pv=pv, r4=r4: e.copy(out=scs[:, r4 * 4:r4 * 4 + 4, :], in_=pv), reads=[PB[4]], writes=[b_scs])

        def frontB(t):
            s = t % 2
            eidx, b_eidx = eidx2[s], b_eidx2[s]
            gate, b_gate = gate2[s], b_gate2[s]
            for j in range(16):
                P.add("dve", lambda e, j=j: e.max(out=vals[:, j, 0:8], in_=scs[:, j, :]), reads=[b_scs], writes=[b_vals])
                P.add("dve", lambda e, j=j: e.match_replace(out=wk[:, 0:128], in_to_replace=vals[:, j, 0:8], in_values=scs[:, j, :], imm_value=-1e30), reads=[b_scs, b_vals], writes=[b_wk])
                P.add("dve", lambda e, j=j: e.max(out=vals[:, j, 8:16], in_=wk[:, 0:128]), reads=[b_wk], writes=[b_vals])
                P.add("dve", lambda e, j=j: e.max_index(out=idx[:, j, 0:8], in_max=vals[:, j, 0:8], in_values=scs[:, j, :]), reads=[b_scs, b_vals], writes=[b_idx])
                P.add("dve", lambda e, j=j: e.max_index(out=idx[:, j, 8:16], in_max=vals[:, j, 8:16], in_values=scs[:, j, :]), reads=[b_scs, b_vals], writes=[b_idx])
            P.add("dve", lambda e: e.tensor_copy(out=idxf, in_=idx), reads=[b_idx], writes=[b_idxf])
            v4 = vals.rearrange("p (h c) k -> p h c k", h=8, c=2)
            i4 = idxf.rearrange("p (h c) k -> p h c k", h=8, c=2)
            c4 = cand.rearrange("p h (a b) -> p h a b", a=16, b=16)
            P.add("dve", lambda e: e.tensor_tensor(out=c4, in0=v4[:, :, 0, :].unsqueeze(3).to_broadcast([128, 8, 16, 16]), in1=v4[:, :, 1, :].unsqueeze(2).to_broadcast([128, 8, 16, 16]), op=ALU.add), reads=[b_vals], writes=[b_cand] + b_junks)
            for h in range(8):
                P.add("dve", lambda e, h=h: e.max(out=ts_[:, h, 0:8], in_=cand[:, h, :]), reads=[b_cand], writes=[b_ts])
                P.add("dve", lambda e, h=h: e.match_replace(out=wk, in_to_replace=ts_[:, h, 0:8], in_values=cand[:, h, :], imm_value=-1e30), reads=[b_cand, b_ts], writes=[b_wk])
                P.add("dve", lambda e, h=h: e.max(out=ts_[:, h, 8:16], in_=wk), reads=[b_wk], writes=[b_ts])
                P.add("dve", lambda e, h=h: e.max_index(out=pos[:, h, 0:8], in_max=ts_[:, h, 0:8], in_values=cand[:, h, :]), reads=[b_cand, b_ts], writes=[b_pos])
                P.add("dve", lambda e, h=h: e.max_index(out=pos[:, h, 8:16], in_max=ts_[:, h, 8:16], in_values=cand[:, h, :]), reads=[b_cand, b_ts], writes=[b_pos])
            P.add("dve", lambda e: e.tensor_single_scalar(out=pi_u, in_=pos, scalar=4, op=ALU.logical_shift_right), reads=[b_pos], writes=[b_piu])
            P.add("dve", lambda e: e.tensor_single_scalar(out=pj_u, in_=pos, scalar=15, op=ALU.bitwise_and), reads=[b_pos], writes=[b_pju])
            P.add("dve", lambda e: e.tensor_copy(out=pi_f, in_=pi_u), reads=[b_piu], writes=[b_pif])
            P.add("dve", lambda e: e.tensor_copy(out=pj_f, in_=pj_u), reads=[b_pju], writes=[b_pjf])
            iob = iota16.unsqueeze(1).unsqueeze(1).to_broadcast([128, 8, 16, 16])
            for (pf, b_pf, cc, ee, b_ee) in [(pi_f, b_pif, 0, e1, b_e1), (pj_f, b_pjf, 1, e2, b_e2)]:
                P.add("dve", lambda e, pf=pf: e.tensor_tensor(out=oh, in0=pf.unsqueeze(3).to_broadcast([128, 8, 16, 16]), in1=iob, op=ALU.is_equal), reads=[b_pf, b_const], writes=[b_oh])
                P.add("dve", lambda e, cc=cc: e.tensor_tensor(out=oh, in0=oh, in1=i4[:, :, cc, :].unsqueeze(2).to_broadcast([128, 8, 16, 16]), op=ALU.mult), reads=[b_oh, b_idxf], writes=[b_oh])
                P.add("dve", lambda e, ee=ee: e.tensor_reduce(out=ee, in_=oh, axis=AX.X, op=ALU.add), reads=[b_oh], writes=[b_ee])
            P.add("dve", lambda e: e.scalar_tensor_tensor(out=e1, in0=e1, scalar=128.0, in1=e2, op0=ALU.mult, op1=ALU.add), reads=[b_e1, b_e2], writes=[b_e1])
            P.add("dve", lambda e: e.tensor_copy(out=eidx, in_=e1.rearrange("p a b -> p (a b)")), reads=[b_e1], writes=[b_eidx])
            P.add("dve", lambda e: e.tensor_tensor(out=gate, in0=ts_, in1=ts_[:, :, 0:1].to_broadcast([128, 8, 16]), op=ALU.subtract), reads=[b_ts], writes=[b_gate])

        def frontC(t):
            s = t % 2
            eidx, b_eidx = eidx2[s], b_eidx2[s]
            gate, b_gate = gate2[s], b_gate2[s]
            P.add("act", lambda e: e.activation(out=gate, in_=gate, func=AF.Exp), reads=[b_gate], writes=[b_gate])
            P.add("dve", lambda e: e.tensor_reduce(out=gs[:, 0:8], in_=gate, axis=AX.X, op=ALU.add), reads=[b_gate], writes=[b_gs])
            P.add("dve", lambda e: e.reciprocal(out=gs[:, 8:16], in_=gs[:, 0:8]), reads=[b_gs], writes=[b_gs])
            P.add("dve", lambda e: e.tensor_tensor(out=gate, in0=gate, in1=gs[:, 8:16].unsqueeze(2).to_broadcast([128, 8, 16]), op=ALU.mult), reads=[b_gate, b_gs], writes=[b_gate])
            if dbg:
                dma("sp", dbg_d["eidx"].rearrange("(t p) n -> t p n", p=128)[t], eidx, ch["c2"], reads=[b_eidx])
                dma("sp", dbg_d["gate"].rearrange("(t p) n -> t p n", p=128)[t], gate.rearrange("p a b -> p (a b)"), ch["c2"], reads=[b_gate])

        def back_head(t):
            s = t % 2
            hA, b_hA = hA2[s], b_hA2[s]
            P.add("dve", lambda e: e.tensor_tensor(out=hB, in0=bank(6, 2), in1=hA, op=ALU.add), reads=[PB[6], PB[7], b_hA], writes=[b_hB])

        def back(t):
            s = t % 2
            if dbg:
                dma("sp", dbg_d["h2"].rearrange("(t p) n -> t p n", p=128)[t], hB, ch["c2"], reads=[b_hB])
            rms(hB, b_hB, g_ple, mtb, b_mtb, st_b, b_stb, 0, junk2, b_junk2)
            transpose_to_bf(mtb, b_mtb, 8, nT_bf, b_nT, [2])
            transpose_to_bf(ptile[s], b_pt[s], 2, pT_bf, b_pT, [5])
            for n2 in range(2):
                for kc in range(8):
                    P.add("pe", lambda e, n2=n2, kc=kc: e.matmul(out=bank(n2), lhsT=nT_bf[:, kc, :], rhs=gw_bf[:, kc, n2 * 512:(n2 + 1) * 512], start=(kc == 0), stop=(kc == 7)), reads=[b_nT, b_gw], writes=[PB[n2]])
            for n2 in range(2):
                for kc in range(2):
                    P.add("pe", lambda e, n2=n2, kc=kc: e.matmul(out=bank(3 + n2), lhsT=pT_bf[:, kc, :], rhs=pp_bf[:, kc, n2 * 512:(n2 + 1) * 512], start=(kc == 0), stop=(kc == 1)), reads=[b_pT, b_pp], writes=[PB[3 + n2]])
            P.add("dve", lambda e: e.tensor_tensor(out=gsb, in0=bank(0, 2), in1=gbias, op=ALU.add), reads=[PB[0], PB[1], b_const], writes=[b_gsb])
            P.add("act", lambda e: e.activation(out=gsb, in_=gsb, func=AF.Sigmoid), reads=[b_gsb], writes=[b_gsb])
            P.add("dve", lambda e: e.tensor_tensor(out=gsb, in0=bank(3, 2), in1=gsb, op=ALU.mult), reads=[PB[3], PB[4], b_gsb], writes=[b_gsb])
            P.add("pool", lambda e: e.tensor_tensor(out=hB, in0=gsb, in1=hB, op=ALU.add), reads=[b_gsb, b_hB], writes=[b_hB])
            rms(hB, b_hB, g_fin, outt, b_outt, st_b, b_stb, 4, junk2, b_junk2)
            dma("sp", o_view[t], outt, ch["st0"], reads=[b_outt])

        def record(fns):
            P.defer_list = L = []
            for f in fns:
                f()
            P.defer_list = None
            return L

        def pop(L, n):
            for _ in range(n):
                if not L:
                    return
                a_, k_ = L.pop(0)
                P.add(*a_, **k_)

        front(0)
        for t in range(NT):
            s = t % 2
            eidx, b_eidx = eidx2[s], b_eidx2[s]
            gate, b_gate = gate2[s], b_gate2[s]
            L = []
            if PIPE == 1:
                fns = []
                if t >= 1:
                    fns.append(lambda t=t: back(t - 1))
                if t + 1 < NT:
                    fns.append(lambda t=t: front(t + 1))
                L = record(fns)
            per = (len(L) + 127) // 128
            if PIPE == 2 and t + 1 < NT:
                frontA(t + 1)
            P.add("dve", lambda e: e.memset(dots, 0.0), writes=[b_dots, b_cand] + b_junks)
            for sidx in range(128):
                g = gi % NG
                gi += 1
                P.add("pool", lambda e, g=g, sidx=sidx, eidx=eidx: e.indirect_dma_start(out=gbuf[g], out_offset=None, in_=ub_d, in_offset=bass.IndirectOffsetOnAxis(ap=eidx[:, sidx:sidx + 1], axis=0)),
                      reads=[b_eidx, b_tab], writes=[b_gbuf[g]], chan=gch[g], serial=(gi <= NG))
                P.add("dve", lambda e, g=g, sidx=sidx, s=s: e.scalar_tensor_tensor(out=junks[sidx % NJ], in0=gbuf[g], scalar=1.0, in1=m_bf2[s], op0=ALU.mult, op1=ALU.mult, accum_out=dots[:, sidx:sidx + 1]),
                      reads=[b_gbuf[g], b_mbf2[s]], writes=[b_junks[sidx % NJ]] + ([b_dots] if sidx in (0, 127) else []))
            P.add("act", lambda e: e.activation(out=actg, in_=dots, func=AF.Gelu), reads=[b_dots], writes=[b_actg])
            P.add("dve", lambda e, gate=gate: e.tensor_tensor(out=actg, in0=actg, in1=gate.rearrange("p a b -> p (a b)"), op=ALU.mult), reads=[b_actg, b_gate], writes=[b_actg])
            if PIPE == 2 and t + 1 < NT:
                frontB(t + 1)
            for sidx in range(128):
                g = gi % NG
                gi += 1
                dg = sidx % ND
                P.add("pool", lambda e, g=g, sidx=sidx, eidx=eidx: e.indirect_dma_start(out=gbuf[g], out_offset=None, in_=vb_d, in_offset=bass.IndirectOffsetOnAxis(ap=eidx[:, sidx:sidx + 1], axis=0)),
                      reads=[b_eidx, b_tab], writes=[b_gbuf[g]], chan=gch[g], serial=(gi <= NG))
                P.add("act", lambda e, dg=dg, sidx=sidx: e.activation(out=diag[dg], in_=ident, func=AF.Copy, scale=actg[:, sidx:sidx + 1]), reads=[b_ident, b_actg], writes=[b_diag[dg]])
                for n2 in range(2):
                    P.add("pe", lambda e, g=g, dg=dg, n2=n2, sidx=sidx: e.matmul(out=bank(6 + n2), lhsT=diag[dg], rhs=gbuf[g][:, n2 * 512:(n2 + 1) * 512], start=(sidx == 0), stop=(sidx == 127)),
                          reads=[b_diag[dg], b_gbuf[g]], writes=[PB[6 + n2]])
                pop(L, per)
            pop(L, len(L))
            back_head(t)
            if PIPE == 2:
                if t + 1 < NT:
                    frontC(t + 1)
                back(t)
            elif PIPE == 0:
                back(t)
                if t + 1 < NT:
                    front(t + 1)
        if PIPE == 1:
            back(NT - 1)

        P.barrier()
        P.emit_all()
    return nc


def host_consts(S, rel_bias):
    f32 = np.float32
    half = 32
    freqs = (1.0 / (np.float32(10000.0) ** (np.arange(half, dtype=f32) / f32(half)))).astype(f32)
    pos = np.arange(S, dtype=f32)
    ang = (pos[:, None] * freqs[None, :]).astype(f32)
    cos, sin = np.cos(ang).astype(f32), np.sin(ang).astype(f32)
    fidx = np.arange(128) % 32
    cosF = np.ascontiguousarray(cos[:, fidx].T)
    sinF = np.ascontiguousarray(sin[:, fidx].T)
    lg = np.log(np.array(GAMMAS, dtype=np.float64))
    i = np.arange(128)
    rel = i[None, :] - i[:, None]
    decayT = np.zeros((128, 4, 128), f32)
    for h in range(4):
        decayT[:, h, :] = np.where(rel >= 0, np.exp(np.maximum(rel, 0) * lg[h]), 0.0)
    zeta = np.exp((127 - i)[:, None] * lg[None, :]).astype(f32)
    xi = np.exp((i + 1)[None, :] * lg[:, None]).astype(f32)
    xiF = np.zeros((128, 2, 128), f32)
    for c in range(2):
        xiF[0:64, c, :] = xi[2 * c][None, :]
        xiF[64:128, c, :] = xi[2 * c + 1][None, :]
    k = np.arange(128)[:, None]
    q = np.arange(128)[None, :]
    bd = t5_bucket_np(np.maximum(q - k, 0))
    bp = t5_bucket_np(q - k + 128)
    nbias = np.zeros((128, 4, 2, 128), f32)
    for h in range(4):
        nbias[:, h, 0, :] = rel_bias[bd, h]
        nbias[:, h, 1, :] = rel_bias[bp, h]
    cmask = np.where(q - k >= 0, 0.0, -30000.0).astype(f32)
    iota16 = np.tile(np.arange(16, dtype=f32)[None, :], (128, 1))
    return dict(cosF=cosF, sinF=sinF, decayT=decayT, zeta=zeta, xiF=xiF, nbias=nbias, cmask=cmask,
                iota16=iota16, relb31=np.ascontiguousarray(rel_bias[31:32, :]))


def make_in_maps(S, nb, x, p, attn_norm, w_in, lam_q1, lam_k1, lam_q2, lam_k2, subln_gain, w_out,
                 rel_bias, ffn_norm, peer_query, peer_subkeys, peer_u, peer_v,
                 ple_norm, ple_gate_w, ple_gate_b, ple_proj, final_norm):
    f = lambda a: np.ascontiguousarray(np.asarray(a, dtype=np.float32))
    rel_bias = f(rel_bias)
    shared = dict(
        attn_norm=f(attn_norm).reshape(1, D), w_in=f(w_in)[0],
        lam=np.stack([f(lam_q1)[0], f(lam_k1)[0], f(lam_q2)[0], f(lam_k2)[0]], 0),
        subln_gain=f(subln_gain).reshape(1, 128), w_out=f(w_out)[0], ffn_norm=f(ffn_norm).reshape(1, D),
        peer_query=f(peer_query)[0], peer_subkeys=f(peer_subkeys)[0].reshape(16, 128, 128),
        peer_u=f(peer_u)[0], peer_v=f(peer_v)[0], ple_norm=f(ple_norm).reshape(1, D),
        ple_gate_w=f(ple_gate_w)[0], ple_gate_b=f(ple_gate_b).reshape(1, D), ple_proj=f(ple_proj)[0],
        final_norm=f(final_norm).reshape(1, D),
    )
    shared.update(host_consts(S, rel_bias))
    x = f(x)
    p = f(p)[0]
    maps = []
    for b in range(nb):
        m = dict(shared)
        m["x"] = np.ascontiguousarray(x[b])
        m["p"] = np.ascontiguousarray(p[b])
        maps.append(m)
    return maps


_NC_CACHE = {}


def kernel(**inputs):
    x = np.asarray(inputs["x"])
    B, S, _ = x.shape
    if S not in _NC_CACHE:
        _NC_CACHE[S] = build(S)
    nc = _NC_CACHE[S]
    maps = make_in_maps(S, B, **inputs)
    res = run_bass_kernel_spmd(nc, maps, core_ids=list(range(B)))
    out = np.stack([np.asarray(r["out"]) for r in res.results], 0)
    return out.astype(np.float32)
```

```python
import math
from contextlib import ExitStack
import numpy as np
import ml_dtypes
import concourse.bass as bass
import concourse.mybir as mybir
from concourse.bass_utils import run_bass_kernel_spmd

F32 = mybir.dt.float32
BF16 = mybir.dt.bfloat16
U32 = mybir.dt.uint32
I32 = mybir.dt.int32
ALU = mybir.AluOpType
AF = mybir.ActivationFunctionType
AX = mybir.AxisListType

D = 1024
NE = 16384
LAMBDA_INIT = 0.8 - 0.6 * math.exp(-0.3 * 0)
GAMMAS = [1.0 - 2.0 ** (-5.0 - h) for h in range(4)]
PIPE = 2


class Chan:
    def __init__(self, sem, step, name):
        self.sem, self.step, self.n, self.name = sem, step, 0, name


class Buf:
    __slots__ = ("name", "last_w", "readers")

    def __init__(self, name):
        self.name = name
        self.last_w = None
        self.readers = []


class Op:
    __slots__ = ("emit", "waits", "chan")


class Prog:
    ENGS = ("pe", "act", "dve", "pool", "sp")

    def __init__(self, nc, es):
        self.nc, self.es = nc, es
        self.ops = {e: [] for e in self.ENGS}
        self.allchans = []
        self.chan = {e: self.new_chan("c_" + e, 1) for e in self.ENGS}
        self.seen = {e: {} for e in self.ENGS}

    def new_chan(self, name, step=16):
        sem = self.es.enter_context(self.nc.semaphore(name))
        c = Chan(sem, step, name)
        self.allchans.append(c)
        return c

    defer_list = None

    def add(self, eng, emit, reads=(), writes=(), chan=None):
        if self.defer_list is not None:
            self.defer_list.append(((eng, emit), dict(reads=reads, writes=writes, chan=chan)))
            return
        deps = {}

        def need(cv):
            if cv is not None and deps.get(cv[0], 0) < cv[1]:
                deps[cv[0]] = cv[1]

        for b in reads:
            need(b.last_w)
        for b in writes:
            need(b.last_w)
            for r in b.readers:
                need(r)
        own = self.chan[eng]
        if chan is None:
            chan = own
        elif chan.n > 0:
            need((chan, chan.n))
        seen = self.seen[eng]
        waits = []
        for c, v in deps.items():
            if c is own and eng == "pe":
                continue
            if seen.get(c, 0) >= v:
                continue
            seen[c] = v
            waits.append((c.sem, v))
        chan.n += chan.step
        op = Op()
        op.emit, op.waits, op.chan = emit, waits, chan
        self.ops[eng].append(op)
        cv = (chan, chan.n)
        for b in writes:
            b.last_w = cv
            b.readers = []
        for b in reads:
            if b not in writes:
                b.readers.append(cv)

    def barrier(self):
        for eng in self.ENGS:
            seen = self.seen[eng]
            waits = []
            for c in self.allchans:
                if c.n > seen.get(c, 0):
                    seen[c] = c.n
                    waits.append((c.sem, c.n))
            op = Op()
            op.emit, op.waits, op.chan = None, waits, None
            self.ops[eng].append(op)

    def emit_all(self):
        nc = self.nc
        with nc.Block() as block:
            def run(name):
                def body(e):
                    for op in self.ops[name]:
                        for sem, v in op.waits:
                            e.wait_ge(sem, v)
                        if op.emit is not None:
                            op.emit(e).then_inc(op.chan.sem, op.chan.step)
                return body
            block.tensor(run("pe"))
            block.scalar(run("act"))
            block.vector(run("dve"))
            block.gpsimd(run("pool"))
            block.sync(run("sp"))


class Arena:
    def __init__(self, t, words):
        self.t, self.words, self.off = t, words, 0

    def reset(self, off=0):
        self.off = off

    def f32(self, *shape):
        n = int(np.prod(shape[1:]))
        n = (n + 7) // 8 * 8
        assert self.off + n <= self.words, ("arena overflow", self.off + n, self.words)
        ap = self.t[0:shape[0], self.off:self.off + int(np.prod(shape[1:]))]
        self.off += n
        return _shape(ap, shape)

    def typed(self, dt, *shape):
        per = {BF16: 2, U32: 1, I32: 1}[dt]
        n_el = int(np.prod(shape[1:]))
        n = (n_el + per - 1) // per
        n8 = (n + 7) // 8 * 8
        assert self.off + n8 <= self.words, ("arena overflow", self.off + n8, self.words)
        ap = self.t[0:shape[0], self.off:self.off + n].bitcast(dt)
        if per == 2 and n_el % 2:
            ap = ap[:, 0:n_el]
        self.off += n8
        return _shape(ap, shape)


def _shape(ap, shape):
    if len(shape) == 2:
        return ap
    if len(shape) == 3:
        return ap.rearrange("p (a b) -> p a b", a=shape[1], b=shape[2])
    if len(shape) == 4:
        return ap.rearrange("p (a b c) -> p a b c", a=shape[1], b=shape[2], c=shape[3])
    raise ValueError(shape)


def t5_bucket_np(n):
    n = np.asarray(n)
    nf = np.maximum(n, 1).astype(np.float32)
    large = 16 + (np.log(nf / np.float32(16)) / np.float32(math.log(128 / 16)) * np.float32(16)).astype(np.int32)
    large = np.minimum(large, 31)
    return np.where(n < 16, n, large)


def build(S, dbg=False, stop=None):
    NT = S // 128
    nc = bass.Bass("TRN2", target_bir_lowering=False)
    dram = {}

    def din(name, shape, dt=F32):
        dram[name] = nc.dram_tensor(name, list(shape), dt, kind="ExternalInput").ap()
        return dram[name]

    x_d = din("x", [S, D])
    p_d = din("p", [S, 256])
    attn_norm_d = din("attn_norm", [1, D])
    w_in_d = din("w_in", [D, 3072])
    lam_d = din("lam", [4, 64])
    subln_d = din("subln_gain", [1, 128])
    w_out_d = din("w_out", [D, D])
    ffn_norm_d = din("ffn_norm", [1, D])
    wq_d = din("peer_query", [D, 2048])
    sk_d = din("peer_subkeys", [16, 128, 128])
    u_d = din("peer_u", [NE, D])
    v_d = din("peer_v", [NE, D])
    ple_norm_d = din("ple_norm", [1, D])
    gw_d = din("ple_gate_w", [D, D])
    gb_d = din("ple_gate_b", [1, D])
    pp_d = din("ple_proj", [256, D])
    fin_d = din("final_norm", [1, D])
    relb31_d = din("relb31", [1, 4])
    nbias_d = din("nbias", [128, 4, 2, 128])
    cmask_d = din("cmask", [128, 128])
    cos_d = din("cosF", [128, S])
    sin_d = din("sinF", [128, S])
    decay_d = din("decayT", [128, 4, 128])
    zeta_d = din("zeta", [128, 4])
    xi_d = din("xiF", [128, 2, 128])
    iota_d = din("iota16", [128, 16])
    out_d = nc.dram_tensor("out", [S, D], F32, kind="ExternalOutput").ap()
    cT_d = nc.dram_tensor("cT_scr", [NT, 128, 8, 128], BF16, kind=("ExternalOutput" if dbg else "Internal")).ap()
    qk_d = nc.dram_tensor("qk_scr", [128, 2, 4, S], BF16).ap()
    ub_d = nc.dram_tensor("u_bf16", [NE, D], BF16).ap()
    vb_d = nc.dram_tensor("v_bf16", [NE, D], BF16).ap()
    dbg_d = {}
    if dbg:
        dbg_d["h1"] = nc.dram_tensor("dbg_h1", [S, D], F32, kind="ExternalOutput").ap()
        dbg_d["h2"] = nc.dram_tensor("dbg_h2", [S, D], F32, kind="ExternalOutput").ap()
        dbg_d["eidx"] = nc.dram_tensor("dbg_eidx", [S, 128], I32, kind="ExternalOutput").ap()
        dbg_d["gate"] = nc.dram_tensor("dbg_gate", [S, 128], F32, kind="ExternalOutput").ap()

    with ExitStack() as es:
        P = Prog(nc, es)
        AW = 48 * 1024 - 256
        arena_t = es.enter_context(nc.sbuf_tensor("arena", [128, AW], F32))
        A = Arena(arena_t, AW)
        psum_t = es.enter_context(nc.psum_tensor("psum", [128, 4096], F32))

        def bank(b, n=1):
            return psum_t[:, b * 512:(b + n) * 512]

        def bank_bf(b):
            return psum_t[:, b * 512:(b + 1) * 512].bitcast(BF16)

        PB = [Buf("bank%d" % i) for i in range(8)]

        ch = {k: P.new_chan(k) for k in
              ["ld0", "ld1", "ld2", "ld3", "st0", "st1", "w0", "w1", "c0", "c1", "c2", "q0", "q1", "pst0", "pst1", "cv0", "cv1", "pcv0", "pcv1"]}
        gch = [P.new_chan("g%d" % i) for i in range(14)]
        d_cT = [Buf("d_cT%d" % t) for t in range(NT)]

        def dma(q, out, in_, chan, reads=(), writes=(), **kw):
            P.add(q, lambda e: e.dma_start(out=out, in_=in_, **kw), reads=reads, writes=writes, chan=chan)

        ident = A.f32(128, 128); b_ident = Buf("ident")
        ident_bf = A.typed(BF16, 128, 128); b_identbf = Buf("identbf")
        P.add("pool", lambda e: e.memset(ident, 0.0), writes=[b_ident])
        P.add("pool", lambda e: e.affine_select(out=ident, in_=ident, pattern=[[-1, 128]], compare_op=ALU.not_equal,
                                                fill=1.0, base=0, channel_multiplier=1), reads=[b_ident], writes=[b_ident])
        P.add("pool", lambda e: e.tensor_copy(out=ident_bf, in_=ident), reads=[b_ident], writes=[b_identbf])

        b_const = Buf("const")
        subln = A.f32(128, 128)
        lamv = A.f32(128, 4, 64)
        relb31 = A.f32(128, 4)
        iota16 = A.f32(128, 16)
        zeta = A.f32(128, 4)
        def bcast(src):
            return src.rearrange("a b -> (a b)").partition_broadcast(128)

        for dst, src in [(subln, subln_d), (relb31, relb31_d)]:
            dma("sp", dst, bcast(src), ch["c0"], writes=[b_const])
        dma("sp", lamv.rearrange("p a b -> p (a b)"), bcast(lam_d), ch["c0"], writes=[b_const])
        dma("sp", iota16, iota_d, ch["c0"], writes=[b_const])
        dma("sp", zeta, zeta_d, ch["c0"], writes=[b_const])

        sm = A.f32(128, 16); b_sm = Buf("sm")
        lt = A.f32(128, 2, 64); b_lt = Buf("lt")
        P.add("dve", lambda e: e.tensor_tensor(out=lt[:, 0, :], in0=lamv[:, 0, :], in1=lamv[:, 1, :], op=ALU.mult), reads=[b_const], writes=[b_lt])
        P.add("dve", lambda e: e.tensor_tensor(out=lt[:, 1, :], in0=lamv[:, 2, :], in1=lamv[:, 3, :], op=ALU.mult), reads=[b_const, b_lt], writes=[b_lt])
        P.add("dve", lambda e: e.tensor_reduce(out=sm[:, 0:2], in_=lt, axis=AX.X, op=ALU.add), reads=[b_lt], writes=[b_sm])
        P.add("act", lambda e: e.activation(out=sm[:, 2:4], in_=sm[:, 0:2], func=AF.Exp), reads=[b_sm], writes=[b_sm])
        P.add("dve", lambda e: e.scalar_tensor_tensor(out=sm[:, 4:5], in0=sm[:, 3:4], scalar=-LAMBDA_INIT, in1=sm[:, 2:3], op0=ALU.add, op1=ALU.subtract), reads=[b_sm], writes=[b_sm])
        neglam = sm[:, 4:5]
        P.add("dve", lambda e: e.tensor_scalar(out=subln, in0=subln, scalar1=1.0 - LAMBDA_INIT, scalar2=None, op0=ALU.mult), reads=[b_const], writes=[b_const])
        const_end = A.off
        if stop == "const":
            P.barrier()
            P.emit_all()
            return nc

        v_all = A.typed(BF16, 128, NT, 4, 130)
        b_q = [Buf("q%d" % t) for t in range(NT)]
        b_v0 = Buf("v_ones")
        P.add("pool", lambda e: e.memset(v_all[:, :, :, 128:130], 1.0), writes=[b_v0])
        nb_bf = A.typed(BF16, 128, 4, 2, 128); b_nb = Buf("nb")
        RPC = 2
        cv_in = [A.f32(128, RPC * D) for _ in range(2)]; b_cvin = [Buf("cvin0"), Buf("cvin1")]
        cv_out = [A.typed(BF16, 128, RPC * D) for _ in range(2)]; b_cvout = [Buf("cvout0"), Buf("cvout1")]
        b_tab = Buf("tab_bf")
        rpp = NE // 128
        conv_jobs = []
        for src_t, dst_t in [(u_d, ub_d), (v_d, vb_d)]:
            sv = src_t.rearrange("(p r) d -> p r d", p=128)
            dv = dst_t.rearrange("(p r) d -> p r d", p=128)
            for j in range(0, rpp, RPC):
                conv_jobs.append((sv[:, j:j + RPC, :], dv[:, j:j + RPC, :]))
        conv_i = [0]

        def conv_step(n=1):
            for _ in range(n):
                if conv_i[0] >= len(conv_jobs):
                    return
                src, dst = conv_jobs[conv_i[0]]
                sl = conv_i[0] % 2
                conv_i[0] += 1
                dma("sp", cv_in[sl].rearrange("p (r d) -> p r d", r=RPC), src, ch["cv%d" % sl], writes=[b_cvin[sl]])
                P.add("pool", lambda e, sl=sl: e.tensor_copy(out=cv_out[sl], in_=cv_in[sl]), reads=[b_cvin[sl]], writes=[b_cvout[sl]])
                dma("pool", dst, cv_out[sl].rearrange("p (r d) -> p r d", r=RPC), ch["pcv%d" % sl], reads=[b_cvout[sl]], writes=[b_tab])

        ab_end = A.off
        qk_t = [A.typed(BF16, 128, 2, 4, 128) for _ in range(2)]; b_qkt = [Buf("qkt0"), Buf("qkt1")]

        g_attn = A.f32(128, D)
        dma("sp", g_attn, bcast(attn_norm_d), ch["c0"], writes=[b_const])
        w_in_bf = A.typed(BF16, 128, 8, 3072); b_win = Buf("w_in")
        wrot_bf = A.typed(BF16, 128, 8, 512); b_wrot = Buf("wrot")
        stage = [A.f32(128, 3072) for _ in range(1)]; b_stage = [Buf("stage0")]
        w_view = w_in_d.rearrange("(kc kp) n -> kc kp n", kp=128)
        for kc in range(8):
            s = 0
            dma("sp", stage[s], w_view[kc], ch["w%d" % s], writes=[b_stage[s]])
            eng = ["act", "dve"][kc % 2]
            if eng == "act":
                P.add("act", lambda e, s=s, kc=kc: e.copy(out=w_in_bf[:, kc, :], in_=stage[s]), reads=[b_stage[s]], writes=[b_win])
            else:
                P.add("dve", lambda e, s=s, kc=kc: e.tensor_copy(out=w_in_bf[:, kc, :], in_=stage[s]), reads=[b_stage[s]], writes=[b_win])
            src = stage[s][:, 1536:2048].rearrange("p (a b c) -> p a b c", a=8, b=2, c=32)
            dst = wrot_bf[:, kc, :].rearrange("p (a b c) -> p a b c", a=8, b=2, c=32)
            P.add("pool", lambda e, src=src, dst=dst: e.tensor_scalar(out=dst[:, :, 0, :], in0=src[:, :, 1, :], scalar1=-1.0, scalar2=None, op0=ALU.mult), reads=[b_stage[s]], writes=[b_wrot])
            P.add("pool", lambda e, src=src, dst=dst: e.tensor_copy(out=dst[:, :, 1, :], in_=src[:, :, 0, :]), reads=[b_stage[s]], writes=[b_wrot])

        nb_f = A.f32(128, 4, 2, 128); b_nbf = Buf("nbf")
        cmask = A.f32(128, 128)
        dma("sp", nb_f.rearrange("p a b c -> p (a b c)"), nbias_d.rearrange("p a b c -> p (a b c)"), ch["c1"], writes=[b_nbf])
        dma("sp", cmask, cmask_d, ch["c1"], writes=[b_const])
        for h in range(4):
            P.add("dve", lambda e, h=h: e.tensor_scalar(out=nb_f[:, h], in0=nb_f[:, h], scalar1=relb31[:, h:h + 1], scalar2=None, op0=ALU.subtract), reads=[b_const, b_nbf], writes=[b_nbf])
            P.add("dve", lambda e, h=h: e.tensor_tensor(out=nb_f[:, h, 0, :], in0=nb_f[:, h, 0, :], in1=cmask, op=ALU.add), reads=[b_const, b_nbf], writes=[b_nbf])
        P.add("dve", lambda e: e.tensor_copy(out=nb_bf, in_=nb_f), reads=[b_nbf], writes=[b_nb])

        decayT = A.f32(128, 4, 128)
        xiF = A.f32(128, 2, 128)
        dma("sp", decayT.rearrange("p a b -> p (a b)"), decay_d.rearrange("p a b -> p (a b)"), ch["c1"], writes=[b_const])
        dma("sp", xiF.rearrange("p a b -> p (a b)"), xi_d.rearrange("p a b -> p (a b)"), ch["c1"], writes=[b_const])

        xt = [A.f32(128, D) for _ in range(2)]; b_xt = [Buf("xt0"), Buf("xt1")]
        cst = [A.f32(128, 2, 128) for _ in range(2)]; b_cs = [Buf("cs0"), Buf("cs1")]
        junk = A.f32(128, D); b_junk = Buf("junk")
        a_bf = A.typed(BF16, 128, D); b_abf = Buf("a_bf")
        aT = A.typed(BF16, 128, 8, 128); b_aT = Buf("aT")
        st = A.f32(128, 16); b_st = Buf("st")
        rot1 = A.f32(128, 4, 128); b_rot1 = Buf("rot1")
        rot2 = A.f32(128, 4, 128); b_rot2 = Buf("rot2")
        rqT = A.typed(BF16, 128, 2, 128); b_rqT = Buf("rqT")
        rkT = A.typed(BF16, 128, 2, 128); b_rkT = Buf("rkT")
        qxT = A.typed(BF16, 128, 2, 128); b_qxT = Buf("qxT")
        rqTm = A.typed(BF16, 128, 2, 128); b_rqTm = Buf("rqTm")
        qxTm = A.typed(BF16, 128, 2, 128); b_qxTm = Buf("qxTm")
        P.add("pool", lambda e: e.memset(rqTm, 0.0), writes=[b_rqTm])
        P.add("pool", lambda e: e.memset(qxTm, 0.0), writes=[b_qxTm])
        rv_bf = A.typed(BF16, 128, 512); b_rv = Buf("rv")
        sg = A.f32(128, 512); b_sg = Buf("sg")
        sT_bf = A.typed(BF16, 128, 4, 128); b_sT = Buf("sT")
        kz = A.typed(BF16, 128, 2, 128); b_kz = Buf("kz")
        state = A.f32(128, 2, 128); b_state = Buf("state")
        state_bf = A.typed(BF16, 128, 2, 128); b_statebf = Buf("statebf")
        sq = A.f32(128, 512); b_sq = Buf("sq")
        yr = A.f32(128, 4, 128); b_yr = Buf("yr")
        y_bf = A.typed(BF16, 128, 512); b_ybf = Buf("ybf")
        retT = [A.typed(BF16, 128, 4, 128) for _ in range(2)]; b_retT = [Buf("retT0"), Buf("retT1")]
        P.add("dve", lambda e: e.memset(state, 0.0), writes=[b_state])
        P.add("dve", lambda e: e.memset(state_bf, 0.0), writes=[b_statebf])

        x_view = x_d.rearrange("(t p) n -> t p n", p=128)
        if stop == "A0":
            P.barrier(); P.emit_all(); return nc
        for t in range(NT):
            s = t % 2
            tc_ = slice(t * 128, (t + 1) * 128)
            dma("sp", xt[s], x_view[t], ch["ld%d" % s], writes=[b_xt[s]])
            dma("sp", cst[s][:, 0, :], cos_d[:, tc_], ch["ld%d" % (2 + s)], writes=[b_cs[s]])
            dma("sp", cst[s][:, 1, :], sin_d[:, tc_], ch["ld%d" % (2 + s)], writes=[b_cs[s]])
            P.add("act", lambda e, s=s: e.activation(out=junk, in_=xt[s], func=AF.Square, accum_out=st[:, 0:1]), reads=[b_xt[s]], writes=[b_junk, b_st])
            P.add("act", lambda e: e.activation(out=st[:, 1:2], in_=st[:, 0:1], func=AF.Sqrt, scale=1.0 / D, bias=1e-6), reads=[b_st], writes=[b_st])
            P.add("dve", lambda e: e.reciprocal(out=st[:, 2:3], in_=st[:, 1:2]), reads=[b_st], writes=[b_st])
            P.add("dve", lambda e, s=s: e.scalar_tensor_tensor(out=a_bf, in0=xt[s], scalar=st[:, 2:3], in1=g_attn, op0=ALU.mult, op1=ALU.mult), reads=[b_xt[s], b_st, b_const], writes=[b_abf])
            pT = bank_bf(0).rearrange("p (a b) -> p a b", a=8, b=128)
            for kc in range(8):
                P.add("pe", lambda e, kc=kc, pT=pT: e.transpose(out=pT[:, kc, :], in_=a_bf[:, kc * 128:(kc + 1) * 128], identity=ident_bf), reads=[b_abf, b_identbf], writes=[PB[0]])
            P.add("act", lambda e, pT=pT: e.copy(out=aT, in_=pT), reads=[PB[0]], writes=[b_aT])
            fm = [(1, 0, w_in_bf), (2, 512, w_in_bf), (3, 1536, w_in_bf), (4, 0, wrot_bf)]
            for bk, c0, W in fm:
                pv = bank(bk).rearrange("p (a b) -> p a b", a=4, b=128)
                for j in range(4):
                    for kc in range(8):
                        P.add("pe", lambda e, pv=pv, j=j, kc=kc, W=W, c0=c0: e.matmul(out=pv[:, j, :], lhsT=W[:, kc, c0 + j * 128:c0 + (j + 1) * 128], rhs=aT[:, kc, :], start=(kc == 0), stop=(kc == 7)),
                              reads=[b_aT, b_win, b_wrot], writes=[PB[bk]])
            for g, (bk, c0) in enumerate([(5, 1024), (6, 2048), (7, 2560)]):
                for kc in range(8):
                    P.add("pe", lambda e, bk=bk, c0=c0, kc=kc: e.matmul(out=bank(bk), lhsT=aT[:, kc, :], rhs=w_in_bf[:, kc, c0:c0 + 512], start=(kc == 0), stop=(kc == 7)),
                          reads=[b_aT, b_win], writes=[PB[bk]])
            pq = bank(1).rearrange("p (a b) -> p a b", a=4, b=128)
            pk = bank(2).rearrange("p (a b) -> p a b", a=4, b=128)
            P.add("act", lambda e, pq=pq, s=s: e.activation(out=qk_t[s][:, 0], in_=pq, func=AF.Copy, scale=0.125), reads=[PB[1]], writes=[b_qkt[s]])
            P.add("act", lambda e, pk=pk, s=s: e.copy(out=qk_t[s][:, 1], in_=pk), reads=[PB[2]], writes=[b_qkt[s]])
            dma("pool", qk_d[:, 0, :, tc_], qk_t[s][:, 0], ch["q%d" % s], reads=[b_qkt[s]])
            dma("pool", qk_d[:, 1, :, tc_], qk_t[s][:, 1], ch["q%d" % s], reads=[b_qkt[s]])
            pdv = bank(5).rearrange("p (a b) -> p a b", a=4, b=128)
            P.add("act", lambda e, pdv=pdv, t=t: e.copy(out=v_all[:, t, :, 0:128], in_=pdv), reads=[PB[5]], writes=[b_q[t]])
            P.add("dve", lambda e: e.tensor_copy(out=rv_bf, in_=bank(6)), reads=[PB[6]], writes=[b_rv])
            P.add("act", lambda e: e.activation(out=sg, in_=bank(7), func=AF.Silu), reads=[PB[7]], writes=[b_sg])
            if stop == "A1":
                continue
            p3 = bank(3).rearrange("p (a b) -> p a b", a=4, b=128)
            p4 = bank(4).rearrange("p (a b) -> p a b", a=4, b=128)
            cosb = cst[s][:, 0, :].unsqueeze(1).to_broadcast([128, 4, 128])
            sinb = cst[s][:, 1, :].unsqueeze(1).to_broadcast([128, 4, 128])
            P.add("dve", lambda e, p3=p3, cosb=cosb: e.tensor_tensor(out=rot1, in0=p3, in1=cosb, op=ALU.mult), reads=[PB[3], b_cs[s]], writes=[b_rot1])
            P.add("dve", lambda e, p4=p4, sinb=sinb: e.tensor_tensor(out=rot2, in0=p4, in1=sinb, op=ALU.mult), reads=[PB[4], b_cs[s]], writes=[b_rot2])
            P.add("pool", lambda e: e.tensor_tensor(out=rot1, in0=rot1, in1=rot2, op=ALU.add), reads=[b_rot1, b_rot2], writes=[b_rot1])
            P.add("act", lambda e: e.copy(out=rqT, in_=rot1[:, 0:2, :]), reads=[b_rot1], writes=[b_rqT])
            P.add("act", lambda e: e.activation(out=rkT, in_=rot1[:, 2:4, :], func=AF.Copy, scale=0.125), reads=[b_rot1], writes=[b_rkT])
            P.add("pool", lambda e: e.tensor_tensor(out=qxT, in0=rot1[:, 0:2, :], in1=xiF, op=ALU.mult), reads=[b_rot1, b_const], writes=[b_qxT])
            P.add("act", lambda e: e.copy(out=rqTm[64:128], in_=rot1[64:128, 0:2, :]), reads=[b_rot1], writes=[b_rqTm])
            P.add("dve", lambda e: e.tensor_copy(out=qxTm[64:128], in_=qxT[64:128]), reads=[b_qxT], writes=[b_qxTm])
            if stop == "A2":
                continue
            pS = bank(5).rearrange("p (a b) -> p a b", a=4, b=128)
            for h in range(4):
                c, pr = h // 2, slice((h % 2) * 64, (h % 2) * 64 + 64)
                if h % 2 == 0:
                    P.add("pe", lambda e, pS=pS, h=h, c=c: e.matmul(out=pS[:, h, :], lhsT=rkT[0:64, c, :], rhs=rqT[0:64, c, :], start=True, stop=True), reads=[b_rkT, b_rqT], writes=[PB[5]])
                else:
                    P.add("pe", lambda e, pS=pS, h=h, c=c: e.matmul(out=pS[:, h, :], lhsT=rkT[:, c, :], rhs=rqTm[:, c, :], start=True, stop=True), reads=[b_rkT, b_rqTm], writes=[PB[5]])
            P.add("dve", lambda e, pS=pS: e.tensor_tensor(out=sT_bf, in0=pS, in1=decayT, op=ALU.mult), reads=[PB[5], b_const], writes=[b_sT])
            if stop == "A3a":
                continue
            pI = bank(6).rearrange("p (a b) -> p a b", a=4, b=128)
            for h in range(4):
                c, pr = h // 2, slice((h % 2) * 64, (h % 2) * 64 + 64)
                P.add("pe", lambda e, pI=pI, h=h: e.matmul(out=pI[:, h, :], lhsT=sT_bf[:, h, :], rhs=rv_bf[:, h * 128:(h + 1) * 128], start=True, stop=False), reads=[b_sT, b_rv], writes=[PB[6]])
                if h % 2 == 0:
                    P.add("pe", lambda e, pI=pI, h=h, c=c: e.matmul(out=pI[:, h, :], lhsT=qxT[0:64, c, :], rhs=state_bf[0:64, c, :], start=False, stop=True), reads=[b_qxT, b_statebf], writes=[PB[6]])
                else:
                    P.add("pe", lambda e, pI=pI, h=h, c=c: e.matmul(out=pI[:, h, :], lhsT=qxTm[:, c, :], rhs=state_bf[:, c, :], start=False, stop=True), reads=[b_qxTm, b_statebf], writes=[PB[6]])
            if stop == "A3b":
                continue
            pK = bank_bf(0).rearrange("p (a b) -> p a b", a=8, b=128)
            for c in range(2):
                P.add("pe", lambda e, pK=pK, c=c: e.transpose(out=pK[:, c, :], in_=rkT[:, c, :], identity=ident_bf), reads=[b_rkT, b_identbf], writes=[PB[0]])
            P.add("dve", lambda e, pK=pK: e.tensor_tensor(out=kz.rearrange("p a (b c) -> p (a b) c", b=2, c=64), in0=pK[:, 0:2, :].rearrange("p a (b c) -> p (a b) c", b=2, c=64), in1=zeta.unsqueeze(2).to_broadcast([128, 4, 64]), op=ALU.mult), reads=[PB[0], b_const], writes=[b_kz])
            if stop == "A3c":
                continue
            pKV = bank(7).rearrange("p (a b) -> p a b", a=2, b=256)
            for c in range(2):
                P.add("pe", lambda e, pKV=pKV, c=c: e.matmul(out=pKV[:, c, :], lhsT=kz[:, c, :], rhs=rv_bf[:, c * 256:(c + 1) * 256], start=True, stop=True), reads=[b_kz, b_rv], writes=[PB[7]])
            for h in range(4):
                c, hh = h // 2, h % 2
                pr = slice(hh * 64, hh * 64 + 64)
                cd = GAMMAS[h] ** 128
                P.add("dve", lambda e, pKV=pKV, c=c, hh=hh, pr=pr, cd=cd: e.scalar_tensor_tensor(out=state[pr, c, :], in0=state[pr, c, :], scalar=cd, in1=pKV[pr, c, hh * 128:(hh + 1) * 128], op0=ALU.mult, op1=ALU.add), reads=[PB[7], b_state], writes=[b_state])
            P.add("act", lambda e: e.copy(out=state_bf, in_=state), reads=[b_state], writes=[b_statebf])
            if stop == "A3":
                continue
            P.add("act", lambda e: e.activation(out=sq, in_=bank(6), func=AF.Square), reads=[PB[6]], writes=[b_sq])
            P.add("dve", lambda e: e.tensor_reduce(out=st[:, 4:8], in_=sq.rearrange("p (a b) -> p a b", a=4, b=128), axis=AX.X, op=ALU.add), reads=[b_sq], writes=[b_st])
            P.add("act", lambda e: e.activation(out=st[:, 8:12], in_=st[:, 4:8], func=AF.Sqrt, scale=1.0 / 128, bias=1e-6), reads=[b_st], writes=[b_st])
            P.add("dve", lambda e: e.reciprocal(out=st[:, 12:16], in_=st[:, 8:12]), reads=[b_st], writes=[b_st])
            P.add("dve", lambda e, pI=pI: e.tensor_tensor(out=yr, in0=pI, in1=st[:, 12:16].unsqueeze(2).to_broadcast([128, 4, 128]), op=ALU.mult), reads=[PB[6], b_st], writes=[b_yr])
            P.add("pool", lambda e: e.tensor_tensor(out=y_bf, in0=yr.rearrange("p a b -> p (a b)"), in1=sg, op=ALU.mult), reads=[b_yr, b_sg], writes=[b_ybf])
            for h in range(4):
                P.add("pe", lambda e, pK=pK, h=h: e.transpose(out=pK[:, 4 + h, :], in_=y_bf[:, h * 128:(h + 1) * 128], identity=ident_bf), reads=[b_ybf, b_identbf], writes=[PB[0]])
            P.add("act", lambda e, pK=pK, s=s: e.copy(out=retT[s], in_=pK[:, 4:8, :]), reads=[PB[0]], writes=[b_retT[s]])
            dma("pool", cT_d[t, :, 4:8, :], retT[s], ch["pst%d" % s], reads=[b_retT[s]], writes=[d_cT[t]])
            conv_step(max(1, (len(conv_jobs) // 2 + NT - 1) // NT))

        P.barrier()
        if stop in ("A", "A1", "A2", "A3", "A3a", "A3b", "A3c"):
            P.emit_all()
            return nc
        A.reset(ab_end)
        qk_h = [A.typed(BF16, 128, 2, 2, S) for _ in range(2)]
        b_qkh = [Buf("qkh0"), Buf("qkh1")]
        NPT = 3
        PT = [A.typed(BF16, 128, 4, 128) for _ in range(NPT)]; b_PT = [Buf("PT%d" % i) for i in range(NPT)]
        rr = A.f32(128, 16); b_rr = Buf("rr")
        o1 = A.f32(128, 128); b_o1 = Buf("o1")
        o2 = A.f32(128, 128); b_o2 = Buf("o2")
        jk = A.f32(128, 128); b_jk = Buf("jk")
        ob = A.typed(BF16, 128, 128); b_ob = Buf("ob")
        dT = [A.typed(BF16, 128, 128) for _ in range(2)]; b_dT = [Buf("dT0"), Buf("dT1")]
        ST_BANKS = [0, 1, 2]
        O_BANKS = [(3, 4), (5, 6)]
        TR_BANK = 7
        it = 0
        sti = 0
        for h in range(4):
            pr = slice((h % 2) * 64, (h % 2) * 64 + 64)
            hs = h % 2
            for w_ in range(2):
                for m_ in range(2):
                    dma("sp", qk_h[hs][0:64, w_, m_, :], qk_d[pr, w_, m_ * 2 + h // 2, :], ch["ld%d" % (w_ * 2 + m_)], writes=[b_qkh[hs]])
            for qi in range(NT):
                qc = slice(qi * 128, (qi + 1) * 128)
                ob_pair = O_BANKS[it % 2]
                for m in range(2):
                    c = m * 2 + h // 2
                    obk = ob_pair[m]
                    O = bank(obk)[:, 0:130]
                    groups = [list(range(g, min(g + 4, qi + 1))) for g in range(0, qi + 1, 4)]
                    pend = None

                    def do_av(grp, pti, O=O, obk=obk, qi=qi, h=h):
                        for sl, kj in enumerate(grp):
                            P.add("pe", lambda e, sl=sl, kj=kj, pti=pti: e.matmul(out=O, lhsT=PT[pti][:, sl, :], rhs=v_all[:, kj, h, :], start=(kj == 0), stop=(kj == qi)),
                                  reads=[b_PT[pti], b_q[kj], b_v0], writes=[PB[obk]])

                    for grp in groups:
                        sb_ = ST_BANKS[sti % 3]
                        pti = sti % NPT
                        sti += 1
                        STv = bank(sb_).rearrange("p (a b) -> p a b", a=4, b=128)
                        for sl, kj in enumerate(grp):
                            near = kj >= qi - 1
                            kc_ = slice(kj * 128, (kj + 1) * 128)
                            P.add("pe", lambda e, STv=STv, sl=sl, kc_=kc_, near=near, m=m, hs=hs, qc=qc: e.matmul(out=STv[:, sl, :], lhsT=qk_h[hs][0:64, 1, m, kc_], rhs=qk_h[hs][0:64, 0, m, qc], start=True, stop=(not near)),
                                  reads=[b_qkh[hs]], writes=[PB[sb_]])
                            if near:
                                which = 0 if kj == qi else 1
                                P.add("pe", lambda e, STv=STv, sl=sl, which=which, h=h: e.matmul(out=STv[:, sl, :], lhsT=ident_bf, rhs=nb_bf[:, h, which, :], start=False, stop=True),
                                      reads=[b_identbf, b_nb], writes=[PB[sb_]])
                        n = len(grp)
                        P.add("act", lambda e, STv=STv, n=n, pti=pti: e.activation(out=PT[pti][:, 0:n, :], in_=STv[:, 0:n, :], func=AF.Exp), reads=[PB[sb_]], writes=[b_PT[pti]])
                        if pend is not None:
                            do_av(*pend)
                        pend = (grp, pti)
                    do_av(*pend)
                O1 = bank(ob_pair[0]); O2 = bank(ob_pair[1])
                rd = [PB[ob_pair[0]], PB[ob_pair[1]]]
                P.add("dve", lambda e, O1=O1: e.reciprocal(out=rr[:, 0:1], in_=O1[:, 128:129]), reads=rd, writes=[b_rr])
                P.add("dve", lambda e, O2=O2: e.reciprocal(out=rr[:, 1:2], in_=O2[:, 128:129]), reads=rd + [b_rr], writes=[b_rr])
                P.add("dve", lambda e: e.tensor_tensor(out=rr[:, 2:3], in0=rr[:, 1:2], in1=neglam, op=ALU.mult), reads=[b_rr, b_sm], writes=[b_rr])
                P.add("dve", lambda e, O1=O1: e.tensor_scalar(out=o1, in0=O1[:, 0:128], scalar1=rr[:, 0:1], scalar2=None, op0=ALU.mult), reads=rd + [b_rr], writes=[b_o1])
                P.add("dve", lambda e, O2=O2: e.scalar_tensor_tensor(out=o2, in0=O2[:, 0:128], scalar=rr[:, 2:3], in1=o1, op0=ALU.mult, op1=ALU.add), reads=rd + [b_rr, b_o1], writes=[b_o2])
                P.add("act", lambda e: e.activation(out=jk, in_=o2, func=AF.Square, accum_out=rr[:, 3:4]), reads=[b_o2], writes=[b_jk, b_rr])
                P.add("act", lambda e: e.activation(out=rr[:, 4:5], in_=rr[:, 3:4], func=AF.Sqrt, scale=1.0 / 128, bias=1e-5), reads=[b_rr], writes=[b_rr])
                P.add("dve", lambda e: e.reciprocal(out=rr[:, 5:6], in_=rr[:, 4:5]), reads=[b_rr], writes=[b_rr])
                P.add("dve", lambda e: e.scalar_tensor_tensor(out=ob, in0=o2, scalar=rr[:, 5:6], in1=subln, op0=ALU.mult, op1=ALU.mult), reads=[b_o2, b_rr, b_const], writes=[b_ob])
                pTr = bank_bf(TR_BANK)[:, 0:128]
                P.add("pe", lambda e, pTr=pTr: e.transpose(out=pTr, in_=ob, identity=ident_bf), reads=[b_ob, b_identbf], writes=[PB[TR_BANK]])
                s = it % 2
                P.add("act", lambda e, pTr=pTr, s=s: e.copy(out=dT[s], in_=pTr), reads=[PB[TR_BANK]], writes=[b_dT[s]])
                dma("sp", cT_d[qi, :, h, :], dT[s], ch["st%d" % s], reads=[b_dT[s]], writes=[d_cT[qi]])
                it += 1
                conv_step(max(1, (len(conv_jobs) // 2 + 4 * NT - 1) // (4 * NT)))
        conv_step(len(conv_jobs))

        P.barrier()
        if stop == "B":
            P.emit_all()
            return nc
        A.reset(const_end)
        w_out_bf = A.typed(BF16, 128, 8, D); b_wout = Buf("w_out")
        wq_bf = A.typed(BF16, 128, 8, 2048); b_wq = Buf("wq")
        gw_bf = A.typed(BF16, 128, 8, D); b_gw = Buf("gw")
        pp_bf = A.typed(BF16, 128, 2, D); b_pp = Buf("pp")
        skT_bf = A.typed(BF16, 128, 16, 128); b_skT = Buf("skT")
        g_ffn = A.f32(128, D)
        g_ple = A.f32(128, D)
        g_fin = A.f32(128, D)
        gbias = A.f32(128, D)
        for dst, src in [(g_ffn, ffn_norm_d), (g_ple, ple_norm_d), (g_fin, fin_d), (gbias, gb_d)]:
            dma("sp", dst, bcast(src), ch["c0"], writes=[b_const])
        NG = 13
        gall = A.f32(128, NG * 512)
        gall_bf = gall.bitcast(BF16)
        gbuf = [gall_bf[:, i * D:(i + 1) * D] for i in range(NG)]; b_gbuf = [Buf("gbuf%d" % i) for i in range(NG)]
        stg = [gall[:, 0:2048], gall[:, 2048:4096]]; b_stg = [Buf("stg0"), Buf("stg1")]
        ND = 4
        diag = [A.typed(BF16, 128, 128) for _ in range(ND)]; b_diag = [Buf("diag%d" % i) for i in range(ND)]
        m_bf = A.typed(BF16, 128, D); b_mbf = Buf("m_bf")
        junkb = A.typed(BF16, 128, D); b_junkb = Buf("junkb")
        si = 0

        def load_w(src_ap, ncols, dst_fn, b_dst):
            nonlocal si
            s = si % 2
            si += 1
            dma("sp", stg[s][:, 0:ncols], src_ap, ch["w%d" % s], writes=[b_stg[s]])
            eng = ["act", "dve", "pool"][si % 3]
            if eng == "act":
                P.add("act", lambda e: e.copy(out=dst_fn, in_=stg[s][:, 0:ncols]), reads=[b_stg[s]], writes=[b_dst])
            else:
                P.add(eng, lambda e: e.tensor_copy(out=dst_fn, in_=stg[s][:, 0:ncols]), reads=[b_stg[s]], writes=[b_dst])

        for kc in range(8):
            load_w(w_out_d[kc * 128:(kc + 1) * 128, :], D, w_out_bf[:, kc, :], b_wout)
            load_w(wq_d[kc * 128:(kc + 1) * 128, :], 2048, wq_bf[:, kc, :], b_wq)
            load_w(gw_d[kc * 128:(kc + 1) * 128, :], D, gw_bf[:, kc, :], b_gw)
        for kc in range(2):
            load_w(pp_d[kc * 128:(kc + 1) * 128, :], D, pp_bf[:, kc, :], b_pp)
        for half in range(2):
            s = si % 2
            si += 1
            skv = stg[s].rearrange("p (a b) -> p a b", a=16, b=128)
            dma("sp", skv[:, 0:8, :], sk_d[half * 8:(half + 1) * 8].rearrange("j k d -> k j d"), ch["w%d" % s], writes=[b_stg[s]])
            for q4 in range(2):
                bk = q4
                pv = bank(bk).rearrange("p (a b) -> p a b", a=4, b=128)
                for j in range(4):
                    P.add("pe", lambda e, pv=pv, j=j, skv=skv, q4=q4: e.transpose(out=pv[:, j, :], in_=skv[:, q4 * 4 + j, :], identity=ident), reads=[b_stg[s], b_ident], writes=[PB[bk]])
                P.add("act", lambda e, pv=pv, half=half, q4=q4: e.copy(out=skT_bf[:, half * 8 + q4 * 4: half * 8 + q4 * 4 + 4, :], in_=pv), reads=[PB[bk]], writes=[b_skT])

        cTt = [A.typed(BF16, 128, 8, 128) for _ in range(2)]; b_cTt = [Buf("cTt0"), Buf("cTt1")]
        xt2 = None; b_xt2 = None
        ptile = [A.f32(128, 256) for _ in range(2)]; b_pt = [Buf("pt0"), Buf("pt1")]
        hA2 = [A.f32(128, D) for _ in range(2)]; b_hA2 = [Buf("hA0"), Buf("hA1")]
        hB = A.f32(128, D); b_hB = Buf("hB")
        mt = A.f32(128, D); b_mt = Buf("mt")
        mtb = A.f32(128, D); b_mtb = Buf("mtb")
        junk2 = A.f32(128, D); b_junk2 = Buf("junk2")
        xt2 = junk2; b_xt2 = b_junk2
        junk_f = A.typed(BF16, 128, D); b_junkf = Buf("junk_f")
        mT_bf = A.typed(BF16, 128, 8, 128); b_mT = Buf("mT")
        nT_bf = mT_bf; b_nT = b_mT
        pT_bf = A.typed(BF16, 128, 2, 128); b_pT = Buf("pT")
        qryT_bf = A.typed(BF16, 128, 16, 128); b_qry = Buf("qryT")
        scs = A.f32(128, 16, 128); b_scs = Buf("scs")
        wk = A.f32(128, 256); b_wk = Buf("wk")
        vals = A.f32(128, 16, 16); b_vals = Buf("vals")
        idx = A.typed(U32, 128, 16, 16); b_idx = Buf("idx")
        idxf = A.f32(128, 16, 16); b_idxf = Buf("idxf")
        cand = A.f32(128, 8, 256); b_cand = Buf("cand")
        ts_ = A.f32(128, 8, 16); b_ts = Buf("ts")
        pos = A.typed(U32, 128, 8, 16); b_pos = Buf("pos")
        pi_u = A.typed(U32, 128, 8, 16); b_piu = Buf("piu")
        pj_u = A.typed(U32, 128, 8, 16); b_pju = Buf("pju")
        pi_f = A.f32(128, 8, 16); b_pif = Buf("pif")
        pj_f = A.f32(128, 8, 16); b_pjf = Buf("pjf")
        oh = scs.rearrange("p (a c) (d e) -> p a c d e", a=8, c=2, d=8, e=16).rearrange("p a c d e -> p a (c d) e"); b_oh = b_scs
        e1 = A.f32(128, 8, 16); b_e1 = Buf("e1")
        e2 = A.f32(128, 8, 16); b_e2 = Buf("e2")
        eidx2 = [A.typed(I32, 128, 128) for _ in range(2)]; b_eidx2 = [Buf("eidx0"), Buf("eidx1")]
        gate2 = [A.f32(128, 8, 16) for _ in range(2)]; b_gate2 = [Buf("gate0"), Buf("gate1")]
        m_bf2 = [m_bf, A.typed(BF16, 128, D)]; b_mbf2 = [b_mbf, Buf("m_bf1")]
        gs = A.f32(128, 16); b_gs = Buf("gs")
        dots = A.f32(128, 128); b_dots = Buf("dots")
        actg = A.f32(128, 128); b_actg = Buf("actg")
        st_f = A.f32(128, 8); b_stf = Buf("st_f")
        st_b = A.f32(128, 8); b_stb = Buf("st_b")
        gsb = junk2; b_gsb = b_junk2
        outt = mtb; b_outt = b_mtb
        gi = 0
        P.barrier()

        p_view = p_d.rearrange("(t p) n -> t p n", p=128)
        o_view = out_d.rearrange("(t p) n -> t p n", p=128)

        def rms(src, b_src, gain, dst, b_dst, stt, b_stt, col, jk_, b_jk):
            P.add("act", lambda e: e.activation(out=jk_, in_=src, func=AF.Square, accum_out=stt[:, col:col + 1]), reads=[b_src], writes=[b_jk, b_stt])
            P.add("act", lambda e: e.activation(out=stt[:, col + 1:col + 2], in_=stt[:, col:col + 1], func=AF.Sqrt, scale=1.0 / D, bias=1e-6), reads=[b_stt], writes=[b_stt])
            P.add("dve", lambda e: e.reciprocal(out=stt[:, col + 2:col + 3], in_=stt[:, col + 1:col + 2]), reads=[b_stt], writes=[b_stt])
            P.add("dve", lambda e: e.scalar_tensor_tensor(out=dst, in0=src, scalar=stt[:, col + 2:col + 3], in1=gain, op0=ALU.mult, op1=ALU.mult), reads=[b_src, b_stt, b_const], writes=[b_dst])

        def transpose_to_bf(src, b_src, nchunk, dstT, b_dstT, banks):
            for g in range(0, nchunk, 4):
                bk = banks[(g // 4) % len(banks)]
                n = min(4, nchunk - g)
                pv = bank(bk).rearrange("p (a b) -> p a b", a=4, b=128)
                for j in range(n):
                    P.add("pe", lambda e, pv=pv, j=j, g=g: e.transpose(out=pv[:, j, :], in_=src[:, (g + j) * 128:(g + j + 1) * 128], identity=ident), reads=[b_src, b_ident], writes=[PB[bk]])
                P.add("act", lambda e, pv=pv, g=g, n=n: e.copy(out=dstT[:, g:g + n, :], in_=pv[:, 0:n, :]), reads=[PB[bk]], writes=[b_dstT])

        def front(t, part="ABC"):
            if "A" in part:
                frontA(t)
            if "B" in part:
                frontB(t)
            if "C" in part:
                frontC(t)

        def frontA(t):
            s = t % 2
            hA, b_hA = hA2[s], b_hA2[s]
            dma("sp", cTt[s].rearrange("p a b -> p (a b)"), cT_d[t].rearrange("p a b -> p (a b)"), ch["ld%d" % s], reads=[d_cT[t]], writes=[b_cTt[s]])
            dma("sp", xt2, x_view[t], ch["ld2"], writes=[b_xt2])
            dma("sp", ptile[s], p_view[t], ch["c%d" % s], writes=[b_pt[s]])
            for n2 in range(2):
                for c in range(8):
                    P.add("pe", lambda e, n2=n2, c=c, s=s: e.matmul(out=bank(n2), lhsT=cTt[s][:, c, :], rhs=w_out_bf[:, c, n2 * 512:(n2 + 1) * 512], start=(c == 0), stop=(c == 7)),
                          reads=[b_cTt[s], b_wout], writes=[PB[n2]])
            P.add("dve", lambda e: e.tensor_tensor(out=hA, in0=bank(0, 2), in1=xt2, op=ALU.add), reads=[PB[0], PB[1], b_xt2], writes=[b_hA])
            if dbg:
                dma("sp", dbg_d["h1"].rearrange("(t p) n -> t p n", p=128)[t], hA, ch["c2"], reads=[b_hA])
            rms(hA, b_hA, g_ffn, mt, b_mt, st_f, b_stf, 0, junk_f, b_junkf)
            P.add("act", lambda e: e.copy(out=m_bf2[s], in_=mt), reads=[b_mt], writes=[b_mbf2[s]])
            transpose_to_bf(mt, b_mt, 8, mT_bf, b_mT, [2])
            for r4 in range(4):
                pv = bank(3).rearrange("p (a b) -> p a b", a=4, b=128)
                for j in range(4):
                    jj = r4 * 4 + j
                    for kc in range(8):
                        P.add("pe", lambda e, pv=pv, j=j, jj=jj, kc=kc: e.matmul(out=pv[:, j, :], lhsT=wq_bf[:, kc, jj * 128:(jj + 1) * 128], rhs=mT_bf[:, kc, :], start=(kc == 0), stop=(kc == 7)),
                              reads=[b_wq, b_mT], writes=[PB[3]])
                P.add("act", lambda e, pv=pv, r4=r4: e.copy(out=qryT_bf[:, r4 * 4:r4 * 4 + 4, :], in_=pv), reads=[PB[3]], writes=[b_qry])
            for r4 in range(4):
                pv = bank(4).rearrange("p (a b) -> p a b", a=4, b=128)
                for j in range(4):
                    jj = r4 * 4 + j
                    P.add("pe", lambda e, pv=pv, j=j, jj=jj: e.matmul(out=pv[:, j, :], lhsT=qryT_bf[:, jj, :], rhs=skT_bf[:, jj, :], start=True, stop=True), reads=[b_qry, b_skT], writes=[PB[4]])
                P.add("act", lambda e, pv=pv, r4=r4: e.copy(out=scs[:, r4 * 4:r4 * 4 + 4, :], in_=pv), reads=[PB[4]], writes=[b_scs])

        def frontB(t):
            s = t % 2
            eidx, b_eidx = eidx2[s], b_eidx2[s]
            gate, b_gate = gate2[s], b_gate2[s]
            for j in range(16):
                P.add("dve", lambda e, j=j: e.max(out=vals[:, j, 0:8], in_=scs[:, j, :]), reads=[b_scs], writes=[b_vals])
                P.add("dve", lambda e, j=j: e.match_replace(out=wk[:, 0:128], in_to_replace=vals[:, j, 0:8], in_values=scs[:, j, :], imm_value=-1e30), reads=[b_scs, b_vals], writes=[b_wk])
                P.add("dve", lambda e, j=j: e.max(out=vals[:, j, 8:16], in_=wk[:, 0:128]), reads=[b_wk], writes=[b_vals])
                P.add("dve", lambda e, j=j: e.max_index(out=idx[:, j, 0:8], in_max=vals[:, j, 0:8], in_values=scs[:, j, :]), reads=[b_scs, b_vals], writes=[b_idx])
                P.add("dve", lambda e, j=j: e.max_index(out=idx[:, j, 8:16], in_max=vals[:, j, 8:16], in_values=scs[:, j, :]), reads=[b_scs, b_vals], writes=[b_idx])
            P.add("dve", lambda e: e.tensor_copy(out=idxf, in_=idx), reads=[b_idx], writes=[b_idxf])
            v4 = vals.rearrange("p (h c) k -> p h c k", h=8, c=2)
            i4 = idxf.rearrange("p (h c) k -> p h c k", h=8, c=2)
            c4 = cand.rearrange("p h (a b) -> p h a b", a=16, b=16)
            P.add("dve", lambda e: e.tensor_tensor(out=c4, in0=v4[:, :, 0, :].unsqueeze(3).to_broadcast([128, 8, 16, 16]), in1=v4[:, :, 1, :].unsqueeze(2).to_broadcast([128, 8, 16, 16]), op=ALU.add), reads=[b_vals], writes=[b_cand])
            for h in range(8):
                P.add("dve", lambda e, h=h: e.max(out=ts_[:, h, 0:8], in_=cand[:, h, :]), reads=[b_cand], writes=[b_ts])
                P.add("dve", lambda e, h=h: e.match_replace(out=wk, in_to_replace=ts_[:, h, 0:8], in_values=cand[:, h, :], imm_value=-1e30), reads=[b_cand, b_ts], writes=[b_wk])
                P.add("dve", lambda e, h=h: e.max(out=ts_[:, h, 8:16], in_=wk), reads=[b_wk], writes=[b_ts])
                P.add("dve", lambda e, h=h: e.max_index(out=pos[:, h, 0:8], in_max=ts_[:, h, 0:8], in_values=cand[:, h, :]), reads=[b_cand, b_ts], writes=[b_pos])
                P.add("dve", lambda e, h=h: e.max_index(out=pos[:, h, 8:16], in_max=ts_[:, h, 8:16], in_values=cand[:, h, :]), reads=[b_cand, b_ts], writes=[b_pos])
            P.add("dve", lambda e: e.tensor_single_scalar(out=pi_u, in_=pos, scalar=4, op=ALU.logical_shift_right), reads=[b_pos], writes=[b_piu])
            P.add("dve", lambda e: e.tensor_single_scalar(out=pj_u, in_=pos, scalar=15, op=ALU.bitwise_and), reads=[b_pos], writes=[b_pju])
            P.add("dve", lambda e: e.tensor_copy(out=pi_f, in_=pi_u), reads=[b_piu], writes=[b_pif])
            P.add("dve", lambda e: e.tensor_copy(out=pj_f, in_=pj_u), reads=[b_pju], writes=[b_pjf])
            iob = iota16.unsqueeze(1).unsqueeze(1).to_broadcast([128, 8, 16, 16])
            for (pf, b_pf, cc, ee, b_ee) in [(pi_f, b_pif, 0, e1, b_e1), (pj_f, b_pjf, 1, e2, b_e2)]:
                P.add("dve", lambda e, pf=pf: e.tensor_tensor(out=oh, in0=pf.unsqueeze(3).to_broadcast([128, 8, 16, 16]), in1=iob, op=ALU.is_equal), reads=[b_pf, b_const], writes=[b_oh])
                P.add("dve", lambda e, cc=cc: e.tensor_tensor(out=oh, in0=oh, in1=i4[:, :, cc, :].unsqueeze(2).to_broadcast([128, 8, 16, 16]), op=ALU.mult), reads=[b_oh, b_idxf], writes=[b_oh])
                P.add("dve", lambda e, ee=ee: e.tensor_reduce(out=ee, in_=oh, axis=AX.X, op=ALU.add), reads=[b_oh], writes=[b_ee])
            P.add("dve", lambda e: e.scalar_tensor_tensor(out=e1, in0=e1, scalar=128.0, in1=e2, op0=ALU.mult, op1=ALU.add), reads=[b_e1, b_e2], writes=[b_e1])
            P.add("dve", lambda e: e.tensor_copy(out=eidx, in_=e1.rearrange("p a b -> p (a b)")), reads=[b_e1], writes=[b_eidx])
            P.add("dve", lambda e: e.tensor_tensor(out=gate, in0=ts_, in1=ts_[:, :, 0:1].to_broadcast([128, 8, 16]), op=ALU.subtract), reads=[b_ts], writes=[b_gate])

        def frontC(t):
            s = t % 2
            eidx, b_eidx = eidx2[s], b_eidx2[s]
            gate, b_gate = gate2[s], b_gate2[s]
            P.add("act", lambda e: e.activation(out=gate, in_=gate, func=AF.Exp), reads=[b_gate], writes=[b_gate])
            P.add("dve", lambda e: e.tensor_reduce(out=gs[:, 0:8], in_=gate, axis=AX.X, op=ALU.add), reads=[b_gate], writes=[b_gs])
            P.add("dve", lambda e: e.reciprocal(out=gs[:, 8:16], in_=gs[:, 0:8]), reads=[b_gs], writes=[b_gs])
            P.add("dve", lambda e: e.tensor_tensor(out=gate, in0=gate, in1=gs[:, 8:16].unsqueeze(2).to_broadcast([128, 8, 16]), op=ALU.mult), reads=[b_gate, b_gs], writes=[b_gate])
            if dbg:
                dma("sp", dbg_d["eidx"].rearrange("(t p) n -> t p n", p=128)[t], eidx, ch["c2"], reads=[b_eidx])
                dma("sp", dbg_d["gate"].rearrange("(t p) n -> t p n", p=128)[t], gate.rearrange("p a b -> p (a b)"), ch["c2"], reads=[b_gate])

        def back_head(t):
            s = t % 2
            hA, b_hA = hA2[s], b_hA2[s]
            P.add("dve", lambda e: e.tensor_tensor(out=hB, in0=bank(6, 2), in1=hA, op=ALU.add), reads=[PB[6], PB[7], b_hA], writes=[b_hB])

        def back(t):
            s = t % 2
            if dbg:
                dma("sp", dbg_d["h2"].rearrange("(t p) n -> t p n", p=128)[t], hB, ch["c2"], reads=[b_hB])
            rms(hB, b_hB, g_ple, mtb, b_mtb, st_b, b_stb, 0, junk2, b_junk2)
            transpose_to_bf(mtb, b_mtb, 8, nT_bf, b_nT, [2])
            transpose_to_bf(ptile[s], b_pt[s], 2, pT_bf, b_pT, [5])
            for n2 in range(2):
                for kc in range(8):
                    P.add("pe", lambda e, n2=n2, kc=kc: e.matmul(out=bank(n2), lhsT=nT_bf[:, kc, :], rhs=gw_bf[:, kc, n2 * 512:(n2 + 1) * 512], start=(kc == 0), stop=(kc == 7)), reads=[b_nT, b_gw], writes=[PB[n2]])
            for n2 in range(2):
                for kc in range(2):
                    P.add("pe", lambda e, n2=n2, kc=kc: e.matmul(out=bank(3 + n2), lhsT=pT_bf[:, kc, :], rhs=pp_bf[:, kc, n2 * 512:(n2 + 1) * 512], start=(kc == 0), stop=(kc == 1)), reads=[b_pT, b_pp], writes=[PB[3 + n2]])
            P.add("dve", lambda e: e.tensor_tensor(out=gsb, in0=bank(0, 2), in1=gbias, op=ALU.add), reads=[PB[0], PB[1], b_const], writes=[b_gsb])
            P.add("act", lambda e: e.activation(out=gsb, in_=gsb, func=AF.Sigmoid), reads=[b_gsb], writes=[b_gsb])
            P.add("dve", lambda e: e.tensor_tensor(out=gsb, in0=bank(3, 2), in1=gsb, op=ALU.mult), reads=[PB[3], PB[4], b_gsb], writes=[b_gsb])
            P.add("pool", lambda e: e.tensor_tensor(out=hB, in0=gsb, in1=hB, op=ALU.add), reads=[b_gsb, b_hB], writes=[b_hB])
            rms(hB, b_hB, g_fin, outt, b_outt, st_b, b_stb, 4, junk2, b_junk2)
            dma("sp", o_view[t], outt, ch["st0"], reads=[b_outt])

        def record(fns):
            P.defer_list = L = []
            for f in fns:
                f()
            P.defer_list = None
            return L

        def pop(L, n):
            for _ in range(n):
                if not L:
                    return
                a_, k_ = L.pop(0)
                P.add(*a_, **k_)

        front(0)
        for t in range(NT):
            s = t % 2
            eidx, b_eidx = eidx2[s], b_eidx2[s]
            gate, b_gate = gate2[s], b_gate2[s]
            L = []
            if PIPE == 1:
                fns = []
                if t >= 1:
                    fns.append(lambda t=t: back(t - 1))
                if t + 1 < NT:
                    fns.append(lambda t=t: front(t + 1))
                L = record(fns)
            per = (len(L) + 127) // 128
            if PIPE == 2 and t + 1 < NT:
                frontA(t + 1)
            P.add("dve", lambda e: e.memset(dots, 0.0), writes=[b_dots])
            for sidx in range(128):
                g = gi % NG
                gi += 1
                P.add("pool", lambda e, g=g, sidx=sidx, eidx=eidx: e.indirect_dma_start(out=gbuf[g], out_offset=None, in_=ub_d, in_offset=bass.IndirectOffsetOnAxis(ap=eidx[:, sidx:sidx + 1], axis=0)),
                      reads=[b_eidx, b_tab], writes=[b_gbuf[g]], chan=gch[g])
                P.add("dve", lambda e, g=g, sidx=sidx, s=s: e.scalar_tensor_tensor(out=junkb, in0=gbuf[g], scalar=1.0, in1=m_bf2[s], op0=ALU.mult, op1=ALU.mult, accum_out=dots[:, sidx:sidx + 1]),
                      reads=[b_gbuf[g], b_mbf2[s]], writes=[b_junkb, b_dots])
            P.add("act", lambda e: e.activation(out=actg, in_=dots, func=AF.Gelu), reads=[b_dots], writes=[b_actg])
            P.add("dve", lambda e, gate=gate: e.tensor_tensor(out=actg, in0=actg, in1=gate.rearrange("p a b -> p (a b)"), op=ALU.mult), reads=[b_actg, b_gate], writes=[b_actg])
            if PIPE == 2 and t + 1 < NT:
                frontB(t + 1)
            for sidx in range(128):
                g = gi % NG
                gi += 1
                dg = sidx % ND
                P.add("pool", lambda e, g=g, sidx=sidx, eidx=eidx: e.indirect_dma_start(out=gbuf[g], out_offset=None, in_=vb_d, in_offset=bass.IndirectOffsetOnAxis(ap=eidx[:, sidx:sidx + 1], axis=0)),
                      reads=[b_eidx, b_tab], writes=[b_gbuf[g]], chan=gch[g])
                P.add("act", lambda e, dg=dg, sidx=sidx: e.activation(out=diag[dg], in_=ident, func=AF.Copy, scale=actg[:, sidx:sidx + 1]), reads=[b_ident, b_actg], writes=[b_diag[dg]])
                for n2 in range(2):
                    P.add("pe", lambda e, g=g, dg=dg, n2=n2, sidx=sidx: e.matmul(out=bank(6 + n2), lhsT=diag[dg], rhs=gbuf[g][:, n2 * 512:(n2 + 1) * 512], start=(sidx == 0), stop=(sidx == 127)),
                          reads=[b_diag[dg], b_gbuf[g]], writes=[PB[6 + n2]])
                pop(L, per)
            pop(L, len(L))
            back_head(t)
            if PIPE == 2:
                if t + 1 < NT:
                    frontC(t + 1)
                back(t)
            elif PIPE == 0:
                back(t)
                if t + 1 < NT:
                    front(t + 1)
        if PIPE == 1:
            back(NT - 1)

        P.barrier()
        P.emit_all()
    return nc


def host_consts(S, rel_bias):
    f32 = np.float32
    half = 32
    freqs = (1.0 / (np.float32(10000.0) ** (np.arange(half, dtype=f32) / f32(half)))).astype(f32)
    pos = np.arange(S, dtype=f32)
    ang = (pos[:, None] * freqs[None, :]).astype(f32)
    cos, sin = np.cos(ang).astype(f32), np.sin(ang).astype(f32)
    fidx = np.arange(128) % 32
    cosF = np.ascontiguousarray(cos[:, fidx].T)
    sinF = np.ascontiguousarray(sin[:, fidx].T)
    lg = np.log(np.array(GAMMAS, dtype=np.float64))
    i = np.arange(128)
    rel = i[None, :] - i[:, None]
    decayT = np.zeros((128, 4, 128), f32)
    for h in range(4):
        decayT[:, h, :] = np.where(rel >= 0, np.exp(np.maximum(rel, 0) * lg[h]), 0.0)
    zeta = np.exp((127 - i)[:, None] * lg[None, :]).astype(f32)
    xi = np.exp((i + 1)[None, :] * lg[:, None]).astype(f32)
    xiF = np.zeros((128, 2, 128), f32)
    for c in range(2):
        xiF[0:64, c, :] = xi[2 * c][None, :]
        xiF[64:128, c, :] = xi[2 * c + 1][None, :]
    k = np.arange(128)[:, None]
    q = np.arange(128)[None, :]
    bd = t5_bucket_np(np.maximum(q - k, 0))
    bp = t5_bucket_np(q - k + 128)
    nbias = np.zeros((128, 4, 2, 128), f32)
    for h in range(4):
        nbias[:, h, 0, :] = rel_bias[bd, h]
        nbias[:, h, 1, :] = rel_bias[bp, h]
    cmask = np.where(q - k >= 0, 0.0, -30000.0).astype(f32)
    iota16 = np.tile(np.arange(16, dtype=f32)[None, :], (128, 1))
    return dict(cosF=cosF, sinF=sinF, decayT=decayT, zeta=zeta, xiF=xiF, nbias=nbias, cmask=cmask,
                iota16=iota16, relb31=np.ascontiguousarray(rel_bias[31:32, :]))


def make_in_maps(S, nb, x, p, attn_norm, w_in, lam_q1, lam_k1, lam_q2, lam_k2, subln_gain, w_out,
                 rel_bias, ffn_norm, peer_query, peer_subkeys, peer_u, peer_v,
                 ple_norm, ple_gate_w, ple_gate_b, ple_proj, final_norm):
    f = lambda a: np.ascontiguousarray(np.asarray(a, dtype=np.float32))
    rel_bias = f(rel_bias)
    shared = dict(
        attn_norm=f(attn_norm).reshape(1, D), w_in=f(w_in)[0],
        lam=np.stack([f(lam_q1)[0], f(lam_k1)[0], f(lam_q2)[0], f(lam_k2)[0]], 0),
        subln_gain=f(subln_gain).reshape(1, 128), w_out=f(w_out)[0], ffn_norm=f(ffn_norm).reshape(1, D),
        peer_query=f(peer_query)[0], peer_subkeys=f(peer_subkeys)[0].reshape(16, 128, 128),
        peer_u=f(peer_u)[0], peer_v=f(peer_v)[0], ple_norm=f(ple_norm).reshape(1, D),
        ple_gate_w=f(ple_gate_w)[0], ple_gate_b=f(ple_gate_b).reshape(1, D), ple_proj=f(ple_proj)[0],
        final_norm=f(final_norm).reshape(1, D),
    )
    shared.update(host_consts(S, rel_bias))
    x = f(x)
    p = f(p)[0]
    maps = []
    for b in range(nb):
        m = dict(shared)
        m["x"] = np.ascontiguousarray(x[b])
        m["p"] = np.ascontiguousarray(p[b])
        maps.append(m)
    return maps


_NC_CACHE = {}


def kernel(**inputs):
    x = np.asarray(inputs["x"])
    B, S, _ = x.shape
    if S not in _NC_CACHE:
        _NC_CACHE[S] = build(S)
    nc = _NC_CACHE[S]
    maps = make_in_maps(S, B, **inputs)
    res = run_bass_kernel_spmd(nc, maps, core_ids=list(range(B)))
    out = np.stack([np.asarray(r["out"]) for r in res.results], 0)
    return out.astype(np.float32)
```

```python
import math
from contextlib import ExitStack
import numpy as np
import ml_dtypes
import concourse.bass as bass
import concourse.mybir as mybir
from concourse.bass_utils import run_bass_kernel_spmd

F32 = mybir.dt.float32
BF16 = mybir.dt.bfloat16
U32 = mybir.dt.uint32
I32 = mybir.dt.int32
ALU = mybir.AluOpType
AF = mybir.ActivationFunctionType
AX = mybir.AxisListType

D = 1024
NE = 16384
LAMBDA_INIT = 0.8 - 0.6 * math.exp(-0.3 * 0)
GAMMAS = [1.0 - 2.0 ** (-5.0 - h) for h in range(4)]
PIPE = 2


class Chan:
    def __init__(self, sem, step, name):
        self.sem, self.step, self.n, self.name = sem, step, 0, name


class Buf:
    __slots__ = ("name", "last_w", "readers")

    def __init__(self, name):
        self.name = name
        self.last_w = None
        self.readers = []


class Op:
    __slots__ = ("emit", "waits", "chan")


class Prog:
    ENGS = ("pe", "act", "dve", "pool", "sp")

    def __init__(self, nc, es):
        self.nc, self.es = nc, es
        self.ops = {e: [] for e in self.ENGS}
        self.allchans = []
        self.chan = {e: self.new_chan("c_" + e, 1) for e in self.ENGS}
        self.seen = {e: {} for e in self.ENGS}

    def new_chan(self, name, step=16):
        sem = self.es.enter_context(self.nc.semaphore(name))
        c = Chan(sem, step, name)
        self.allchans.append(c)
        return c

    defer_list = None

    def add(self, eng, emit, reads=(), writes=(), chan=None):
        if self.defer_list is not None:
            self.defer_list.append(((eng, emit), dict(reads=reads, writes=writes, chan=chan)))
            return
        deps = {}

        def need(cv):
            if cv is not None and deps.get(cv[0], 0) < cv[1]:
                deps[cv[0]] = cv[1]

        for b in reads:
            need(b.last_w)
        for b in writes:
            need(b.last_w)
            for r in b.readers:
                need(r)
        own = self.chan[eng]
        if chan is None:
            chan = own
        elif chan.n > 0:
            need((chan, chan.n))
        seen = self.seen[eng]
        waits = []
        for c, v in deps.items():
            if c is own and eng == "pe":
                continue
            if seen.get(c, 0) >= v:
                continue
            seen[c] = v
            waits.append((c.sem, v))
        chan.n += chan.step
        op = Op()
        op.emit, op.waits, op.chan = emit, waits, chan
        self.ops[eng].append(op)
        cv = (chan, chan.n)
        for b in writes:
            b.last_w = cv
            b.readers = []
        for b in reads:
            if b not in writes:
                b.readers.append(cv)

    def barrier(self):
        for eng in self.ENGS:
            seen = self.seen[eng]
            waits = []
            for c in self.allchans:
                if c.n > seen.get(c, 0):
                    seen[c] = c.n
                    waits.append((c.sem, c.n))
            op = Op()
            op.emit, op.waits, op.chan = None, waits, None
            self.ops[eng].append(op)

    def emit_all(self):
        nc = self.nc
        with nc.Block() as block:
            def run(name):
                def body(e):
                    for op in self.ops[name]:
                        for sem, v in op.waits:
                            e.wait_ge(sem, v)
                        if op.emit is not None:
                            op.emit(e).then_inc(op.chan.sem, op.chan.step)
                return body
            block.tensor(run("pe"))
            block.scalar(run("act"))
            block.vector(run("dve"))
            block.gpsimd(run("pool"))
            block.sync(run("sp"))


class Arena:
    def __init__(self, t, words):
        self.t, self.words, self.off = t, words, 0

    def reset(self, off=0):
        self.off = off

    def f32(self, *shape):
        n = int(np.prod(shape[1:]))
        n = (n + 7) // 8 * 8
        assert self.off + n <= self.words, ("arena overflow", self.off + n, self.words)
        ap = self.t[0:shape[0], self.off:self.off + int(np.prod(shape[1:]))]
        self.off += n
        return _shape(ap, shape)

    def typed(self, dt, *shape):
        per = {BF16: 2, U32: 1, I32: 1}[dt]
        n_el = int(np.prod(shape[1:]))
        n = (n_el + per - 1) // per
        n8 = (n + 7) // 8 * 8
        assert self.off + n8 <= self.words, ("arena overflow", self.off + n8, self.words)
        ap = self.t[0:shape[0], self.off:self.off + n].bitcast(dt)
        if per == 2 and n_el % 2:
            ap = ap[:, 0:n_el]
        self.off += n8
        return _shape(ap, shape)


def _shape(ap, shape):
    if len(shape) == 2:
        return ap
    if len(shape) == 3:
        return ap.rearrange("p (a b) -> p a b", a=shape[1], b=shape[2])
    if len(shape) == 4:
        return ap.rearrange("p (a b c) -> p a b c", a=shape[1], b=shape[2], c=shape[3])
    raise ValueError(shape)


def t5_bucket_np(n):
    n = np.asarray(n)
    nf = np.maximum(n, 1).astype(np.float32)
    large = 16 + (np.log(nf / np.float32(16)) / np.float32(math.log(128 / 16)) * np.float32(16)).astype(np.int32)
    large = np.minimum(large, 31)
    return np.where(n < 16, n, large)


def build(S, dbg=False, stop=None):
    NT = S // 128
    nc = bass.Bass("TRN2", target_bir_lowering=False)
    dram = {}

    def din(name, shape, dt=F32):
        dram[name] = nc.dram_tensor(name, list(shape), dt, kind="ExternalInput").ap()
        return dram[name]

    x_d = din("x", [S, D])
    p_d = din("p", [S, 256])
    attn_norm_d = din("attn_norm", [1, D])
    w_in_d = din("w_in", [D, 3072])
    lam_d = din("lam", [4, 64])
    subln_d = din("subln_gain", [1, 128])
    w_out_d = din("w_out", [D, D])
    ffn_norm_d = din("ffn_norm", [1, D])
    wq_d = din("peer_query", [D, 2048])
    sk_d = din("peer_subkeys", [16, 128, 128])
    u_d = din("peer_u", [NE, D])
    v_d = din("peer_v", [NE, D])
    ple_norm_d = din("ple_norm", [1, D])
    gw_d = din("ple_gate_w", [D, D])
    gb_d = din("ple_gate_b", [1, D])
    pp_d = din("ple_proj", [256, D])
    fin_d = din("final_norm", [1, D])
    relb31_d = din("relb31", [1, 4])
    nbias_d = din("nbias", [128, 4, 2, 128])
    cmask_d = din("cmask", [128, 128])
    cos_d = din("cosF", [128, S])
    sin_d = din("sinF", [128, S])
    decay_d = din("decayT", [128, 4, 128])
    zeta_d = din("zeta", [128, 4])
    xi_d = din("xiF", [128, 2, 128])
    iota_d = din("iota16", [128, 16])
    out_d = nc.dram_tensor("out", [S, D], F32, kind="ExternalOutput").ap()
    cT_d = nc.dram_tensor("cT_scr", [NT, 128, 8, 128], BF16, kind=("ExternalOutput" if dbg else "Internal")).ap()
    qk_d = nc.dram_tensor("qk_scr", [128, 2, 4, S], BF16).ap()
    ub_d = nc.dram_tensor("u_bf16", [NE, D], BF16).ap()
    vb_d = nc.dram_tensor("v_bf16", [NE, D], BF16).ap()
    dbg_d = {}
    if dbg:
        dbg_d["h1"] = nc.dram_tensor("dbg_h1", [S, D], F32, kind="ExternalOutput").ap()
        dbg_d["h2"] = nc.dram_tensor("dbg_h2", [S, D], F32, kind="ExternalOutput").ap()
        dbg_d["eidx"] = nc.dram_tensor("dbg_eidx", [S, 128], I32, kind="ExternalOutput").ap()
        dbg_d["gate"] = nc.dram_tensor("dbg_gate", [S, 128], F32, kind="ExternalOutput").ap()

    with ExitStack() as es:
        P = Prog(nc, es)
        AW = 48 * 1024 - 256
        arena_t = es.enter_context(nc.sbuf_tensor("arena", [128, AW], F32))
        A = Arena(arena_t, AW)
        psum_t = es.enter_context(nc.psum_tensor("psum", [128, 4096], F32))

        def bank(b, n=1):
            return psum_t[:, b * 512:(b + n) * 512]

        def bank_bf(b):
            return psum_t[:, b * 512:(b + 1) * 512].bitcast(BF16)

        PB = [Buf("bank%d" % i) for i in range(8)]

        ch = {k: P.new_chan(k) for k in
              ["ld0", "ld1", "ld2", "ld3", "st0", "st1", "w0", "w1", "c0", "c1", "c2", "q0", "q1", "pst0", "pst1", "cv0", "cv1", "pcv0", "pcv1"]}
        gch = [P.new_chan("g%d" % i) for i in range(14)]
        d_cT = [Buf("d_cT%d" % t) for t in range(NT)]

        def dma(q, out, in_, chan, reads=(), writes=(), **kw):
            P.add(q, lambda e: e.dma_start(out=out, in_=in_, **kw), reads=reads, writes=writes, chan=chan)

        ident = A.f32(128, 128); b_ident = Buf("ident")
        ident_bf = A.typed(BF16, 128, 128); b_identbf = Buf("identbf")
        P.add("pool", lambda e: e.memset(ident, 0.0), writes=[b_ident])
        P.add("pool", lambda e: e.affine_select(out=ident, in_=ident, pattern=[[-1, 128]], compare_op=ALU.not_equal,
                                                fill=1.0, base=0, channel_multiplier=1), reads=[b_ident], writes=[b_ident])
        P.add("pool", lambda e: e.tensor_copy(out=ident_bf, in_=ident), reads=[b_ident], writes=[b_identbf])

        b_const = Buf("const")
        subln = A.f32(128, 128)
        lamv = A.f32(128, 4, 64)
        relb31 = A.f32(128, 4)
        iota16 = A.f32(128, 16)
        zeta = A.f32(128, 4)
        def bcast(src):
            return src.rearrange("a b -> (a b)").partition_broadcast(128)

        for dst, src in [(subln, subln_d), (relb31, relb31_d)]:
            dma("sp", dst, bcast(src), ch["c0"], writes=[b_const])
        dma("sp", lamv.rearrange("p a b -> p (a b)"), bcast(lam_d), ch["c0"], writes=[b_const])
        dma("sp", iota16, iota_d, ch["c0"], writes=[b_const])
        dma("sp", zeta, zeta_d, ch["c0"], writes=[b_const])

        sm = A.f32(128, 16); b_sm = Buf("sm")
        lt = A.f32(128, 2, 64); b_lt = Buf("lt")
        P.add("dve", lambda e: e.tensor_tensor(out=lt[:, 0, :], in0=lamv[:, 0, :], in1=lamv[:, 1, :], op=ALU.mult), reads=[b_const], writes=[b_lt])
        P.add("dve", lambda e: e.tensor_tensor(out=lt[:, 1, :], in0=lamv[:, 2, :], in1=lamv[:, 3, :], op=ALU.mult), reads=[b_const, b_lt], writes=[b_lt])
        P.add("dve", lambda e: e.tensor_reduce(out=sm[:, 0:2], in_=lt, axis=AX.X, op=ALU.add), reads=[b_lt], writes=[b_sm])
        P.add("act", lambda e: e.activation(out=sm[:, 2:4], in_=sm[:, 0:2], func=AF.Exp), reads=[b_sm], writes=[b_sm])
        P.add("dve", lambda e: e.scalar_tensor_tensor(out=sm[:, 4:5], in0=sm[:, 3:4], scalar=-LAMBDA_INIT, in1=sm[:, 2:3], op0=ALU.add, op1=ALU.subtract), reads=[b_sm], writes=[b_sm])
        neglam = sm[:, 4:5]
        P.add("dve", lambda e: e.tensor_scalar(out=subln, in0=subln, scalar1=1.0 - LAMBDA_INIT, scalar2=None, op0=ALU.mult), reads=[b_const], writes=[b_const])
        const_end = A.off
        if stop == "const":
            P.barrier()
            P.emit_all()
            return nc

        v_all = A.typed(BF16, 128, NT, 4, 130)
        b_q = [Buf("q%d" % t) for t in range(NT)]
        b_v0 = Buf("v_ones")
        P.add("pool", lambda e: e.memset(v_all[:, :, :, 128:130], 1.0), writes=[b_v0])
        nb_bf = A.typed(BF16, 128, 4, 2, 128); b_nb = Buf("nb")
        RPC = 2
        cv_in = [A.f32(128, RPC * D) for _ in range(2)]; b_cvin = [Buf("cvin0"), Buf("cvin1")]
        cv_out = [A.typed(BF16, 128, RPC * D) for _ in range(2)]; b_cvout = [Buf("cvout0"), Buf("cvout1")]
        b_tab = Buf("tab_bf")
        rpp = NE // 128
        conv_jobs = []
        for src_t, dst_t in [(u_d, ub_d), (v_d, vb_d)]:
            sv = src_t.rearrange("(p r) d -> p r d", p=128)
            dv = dst_t.rearrange("(p r) d -> p r d", p=128)
            for j in range(0, rpp, RPC):
                conv_jobs.append((sv[:, j:j + RPC, :], dv[:, j:j + RPC, :]))
        conv_i = [0]

        def conv_step(n=1):
            for _ in range(n):
                if conv_i[0] >= len(conv_jobs):
                    return
                src, dst = conv_jobs[conv_i[0]]
                sl = conv_i[0] % 2
                conv_i[0] += 1
                dma("sp", cv_in[sl].rearrange("p (r d) -> p r d", r=RPC), src, ch["cv%d" % sl], writes=[b_cvin[sl]])
                P.add("pool", lambda e, sl=sl: e.tensor_copy(out=cv_out[sl], in_=cv_in[sl]), reads=[b_cvin[sl]], writes=[b_cvout[sl]])
                dma("pool", dst, cv_out[sl].rearrange("p (r d) -> p r d", r=RPC), ch["pcv%d" % sl], reads=[b_cvout[sl]], writes=[b_tab])

        ab_end = A.off
        qk_t = [A.typed(BF16, 128, 2, 4, 128) for _ in range(2)]; b_qkt = [Buf("qkt0"), Buf("qkt1")]

        g_attn = A.f32(128, D)
        dma("sp", g_attn, bcast(attn_norm_d), ch["c0"], writes=[b_const])
        w_in_bf = A.typed(BF16, 128, 8, 3072); b_win = Buf("w_in")
        wrot_bf = A.typed(BF16, 128, 8, 512); b_wrot = Buf("wrot")
        stage = [A.f32(128, 3072) for _ in range(1)]; b_stage = [Buf("stage0")]
        w_view = w_in_d.rearrange("(kc kp) n -> kc kp n", kp=128)
        for kc in range(8):
            s = 0
            dma("sp", stage[s], w_view[kc], ch["w%d" % s], writes=[b_stage[s]])
            eng = ["act", "dve"][kc % 2]
            if eng == "act":
                P.add("act", lambda e, s=s, kc=kc: e.copy(out=w_in_bf[:, kc, :], in_=stage[s]), reads=[b_stage[s]], writes=[b_win])
            else:
                P.add("dve", lambda e, s=s, kc=kc: e.tensor_copy(out=w_in_bf[:, kc, :], in_=stage[s]), reads=[b_stage[s]], writes=[b_win])
            src = stage[s][:, 1536:2048].rearrange("p (a b c) -> p a b c", a=8, b=2, c=32)
            dst = wrot_bf[:, kc, :].rearrange("p (a b c) -> p a b c", a=8, b=2, c=32)
            P.add("pool", lambda e, src=src, dst=dst: e.tensor_scalar(out=dst[:, :, 0, :], in0=src[:, :, 1, :], scalar1=-1.0, scalar2=None, op0=ALU.mult), reads=[b_stage[s]], writes=[b_wrot])
            P.add("pool", lambda e, src=src, dst=dst: e.tensor_copy(out=dst[:, :, 1, :], in_=src[:, :, 0, :]), reads=[b_stage[s]], writes=[b_wrot])

        nb_f = A.f32(128, 4, 2, 128); b_nbf = Buf("nbf")
        cmask = A.f32(128, 128)
        dma("sp", nb_f.rearrange("p a b c -> p (a b c)"), nbias_d.rearrange("p a b c -> p (a b c)"), ch["c1"], writes=[b_nbf])
        dma("sp", cmask, cmask_d, ch["c1"], writes=[b_const])
        for h in range(4):
            P.add("dve", lambda e, h=h: e.tensor_scalar(out=nb_f[:, h], in0=nb_f[:, h], scalar1=relb31[:, h:h + 1], scalar2=None, op0=ALU.subtract), reads=[b_const, b_nbf], writes=[b_nbf])
            P.add("dve", lambda e, h=h: e.tensor_tensor(out=nb_f[:, h, 0, :], in0=nb_f[:, h, 0, :], in1=cmask, op=ALU.add), reads=[b_const, b_nbf], writes=[b_nbf])
        P.add("dve", lambda e: e.tensor_copy(out=nb_bf, in_=nb_f), reads=[b_nbf], writes=[b_nb])

        decayT = A.f32(128, 4, 128)
        xiF = A.f32(128, 2, 128)
        dma("sp", decayT.rearrange("p a b -> p (a b)"), decay_d.rearrange("p a b -> p (a b)"), ch["c1"], writes=[b_const])
        dma("sp", xiF.rearrange("p a b -> p (a b)"), xi_d.rearrange("p a b -> p (a b)"), ch["c1"], writes=[b_const])

        xt = [A.f32(128, D) for _ in range(2)]; b_xt = [Buf("xt0"), Buf("xt1")]
        cst = [A.f32(128, 2, 128) for _ in range(2)]; b_cs = [Buf("cs0"), Buf("cs1")]
        junk = A.f32(128, D); b_junk = Buf("junk")
        a_bf = A.typed(BF16, 128, D); b_abf = Buf("a_bf")
        aT = A.typed(BF16, 128, 8, 128); b_aT = Buf("aT")
        st = A.f32(128, 16); b_st = Buf("st")
        rot1 = A.f32(128, 4, 128); b_rot1 = Buf("rot1")
        rot2 = A.f32(128, 4, 128); b_rot2 = Buf("rot2")
        rqT = A.typed(BF16, 128, 2, 128); b_rqT = Buf("rqT")
        rkT = A.typed(BF16, 128, 2, 128); b_rkT = Buf("rkT")
        qxT = A.typed(BF16, 128, 2, 128); b_qxT = Buf("qxT")
        rqTm = A.typed(BF16, 128, 2, 128); b_rqTm = Buf("rqTm")
        qxTm = A.typed(BF16, 128, 2, 128); b_qxTm = Buf("qxTm")
        P.add("pool", lambda e: e.memset(rqTm, 0.0), writes=[b_rqTm])
        P.add("pool", lambda e: e.memset(qxTm, 0.0), writes=[b_qxTm])
        rv_bf = A.typed(BF16, 128, 512); b_rv = Buf("rv")
        sg = A.f32(128, 512); b_sg = Buf("sg")
        sT_bf = A.typed(BF16, 128, 4, 128); b_sT = Buf("sT")
        kz = A.typed(BF16, 128, 2, 128); b_kz = Buf("kz")
        state = A.f32(128, 2, 128); b_state = Buf("state")
        state_bf = A.typed(BF16, 128, 2, 128); b_statebf = Buf("statebf")
        sq = A.f32(128, 512); b_sq = Buf("sq")
        yr = A.f32(128, 4, 128); b_yr = Buf("yr")
        y_bf = A.typed(BF16, 128, 512); b_ybf = Buf("ybf")
        retT = [A.typed(BF16, 128, 4, 128) for _ in range(2)]; b_retT = [Buf("retT0"), Buf("retT1")]
        P.add("dve", lambda e: e.memset(state, 0.0), writes=[b_state])
        P.add("dve", lambda e: e.memset(state_bf, 0.0), writes=[b_statebf])

        x_view = x_d.rearrange("(t p) n -> t p n", p=128)
        if stop == "A0":
            P.barrier(); P.emit_all(); return nc
        for t in range(NT):
            s = t % 2
            tc_ = slice(t * 128, (t + 1) * 128)
            dma("sp", xt[s], x_view[t], ch["ld%d" % s], writes=[b_xt[s]])
            dma("sp", cst[s][:, 0, :], cos_d[:, tc_], ch["ld%d" % (2 + s)], writes=[b_cs[s]])
            dma("sp", cst[s][:, 1, :], sin_d[:, tc_], ch["ld%d" % (2 + s)], writes=[b_cs[s]])
            P.add("act", lambda e, s=s: e.activation(out=junk, in_=xt[s], func=AF.Square, accum_out=st[:, 0:1]), reads=[b_xt[s]], writes=[b_junk, b_st])
            P.add("act", lambda e: e.activation(out=st[:, 1:2], in_=st[:, 0:1], func=AF.Sqrt, scale=1.0 / D, bias=1e-6), reads=[b_st], writes=[b_st])
            P.add("dve", lambda e: e.reciprocal(out=st[:, 2:3], in_=st[:, 1:2]), reads=[b_st], writes=[b_st])
            P.add("dve", lambda e, s=s: e.scalar_tensor_tensor(out=a_bf, in0=xt[s], scalar=st[:, 2:3], in1=g_attn, op0=ALU.mult, op1=ALU.mult), reads=[b_xt[s], b_st, b_const], writes=[b_abf])
            pT = bank_bf(0).rearrange("p (a b) -> p a b", a=8, b=128)
            for kc in range(8):
                P.add("pe", lambda e, kc=kc, pT=pT: e.transpose(out=pT[:, kc, :], in_=a_bf[:, kc * 128:(kc + 1) * 128], identity=ident_bf), reads=[b_abf, b_identbf], writes=[PB[0]])
            P.add("act", lambda e, pT=pT: e.copy(out=aT, in_=pT), reads=[PB[0]], writes=[b_aT])
            fm = [(1, 0, w_in_bf), (2, 512, w_in_bf), (3, 1536, w_in_bf), (4, 0, wrot_bf)]
            for bk, c0, W in fm:
                pv = bank(bk).rearrange("p (a b) -> p a b", a=4, b=128)
                for j in range(4):
                    for kc in range(8):
                        P.add("pe", lambda e, pv=pv, j=j, kc=kc, W=W, c0=c0: e.matmul(out=pv[:, j, :], lhsT=W[:, kc, c0 + j * 128:c0 + (j + 1) * 128], rhs=aT[:, kc, :], start=(kc == 0), stop=(kc == 7)),
                              reads=[b_aT, b_win, b_wrot], writes=[PB[bk]])
            for g, (bk, c0) in enumerate([(5, 1024), (6, 2048), (7, 2560)]):
                for kc in range(8):
                    P.add("pe", lambda e, bk=bk, c0=c0, kc=kc: e.matmul(out=bank(bk), lhsT=aT[:, kc, :], rhs=w_in_bf[:, kc, c0:c0 + 512], start=(kc == 0), stop=(kc == 7)),
                          reads=[b_aT, b_win], writes=[PB[bk]])
            pq = bank(1).rearrange("p (a b) -> p a b", a=4, b=128)
            pk = bank(2).rearrange("p (a b) -> p a b", a=4, b=128)
            P.add("act", lambda e, pq=pq, s=s: e.activation(out=qk_t[s][:, 0], in_=pq, func=AF.Copy, scale=0.125), reads=[PB[1]], writes=[b_qkt[s]])
            P.add("act", lambda e, pk=pk, s=s: e.copy(out=qk_t[s][:, 1], in_=pk), reads=[PB[2]], writes=[b_qkt[s]])
            dma("pool", qk_d[:, 0, :, tc_], qk_t[s][:, 0], ch["q%d" % s], reads=[b_qkt[s]])
            dma("pool", qk_d[:, 1, :, tc_], qk_t[s][:, 1], ch["q%d" % s], reads=[b_qkt[s]])
            pdv = bank(5).rearrange("p (a b) -> p a b", a=4, b=128)
            P.add("act", lambda e, pdv=pdv, t=t: e.copy(out=v_all[:, t, :, 0:128], in_=pdv), reads=[PB[5]], writes=[b_q[t]])
            P.add("dve", lambda e: e.tensor_copy(out=rv_bf, in_=bank(6)), reads=[PB[6]], writes=[b_rv])
            P.add("act", lambda e: e.activation(out=sg, in_=bank(7), func=AF.Silu), reads=[PB[7]], writes=[b_sg])
            if stop == "A1":
                continue
            p3 = bank(3).rearrange("p (a b) -> p a b", a=4, b=128)
            p4 = bank(4).rearrange("p (a b) -> p a b", a=4, b=128)
            cosb = cst[s][:, 0, :].unsqueeze(1).to_broadcast([128, 4, 128])
            sinb = cst[s][:, 1, :].unsqueeze(1).to_broadcast([128, 4, 128])
            P.add("dve", lambda e, p3=p3, cosb=cosb: e.tensor_tensor(out=rot1, in0=p3, in1=cosb, op=ALU.mult), reads=[PB[3], b_cs[s]], writes=[b_rot1])
            P.add("dve", lambda e, p4=p4, sinb=sinb: e.tensor_tensor(out=rot2, in0=p4, in1=sinb, op=ALU.mult), reads=[PB[4], b_cs[s]], writes=[b_rot2])
            P.add("pool", lambda e: e.tensor_tensor(out=rot1, in0=rot1, in1=rot2, op=ALU.add), reads=[b_rot1, b_rot2], writes=[b_rot1])
            P.add("act", lambda e: e.copy(out=rqT, in_=rot1[:, 0:2, :]), reads=[b_rot1], writes=[b_rqT])
            P.add("act", lambda e: e.activation(out=rkT, in_=rot1[:, 2:4, :], func=AF.Copy, scale=0.125), reads=[b_rot1], writes=[b_rkT])
            P.add("pool", lambda e: e.tensor_tensor(out=qxT, in0=rot1[:, 0:2, :], in1=xiF, op=ALU.mult), reads=[b_rot1, b_const], writes=[b_qxT])
            P.add("act", lambda e: e.copy(out=rqTm[64:128], in_=rot1[64:128, 0:2, :]), reads=[b_rot1], writes=[b_rqTm])
            P.add("dve", lambda e: e.tensor_copy(out=qxTm[64:128], in_=qxT[64:128]), reads=[b_qxT], writes=[b_qxTm])
            if stop == "A2":
                continue
            pS = bank(5).rearrange("p (a b) -> p a b", a=4, b=128)
            for h in range(4):
                c, pr = h // 2, slice((h % 2) * 64, (h % 2) * 64 + 64)
                if h % 2 == 0:
                    P.add("pe", lambda e, pS=pS, h=h, c=c: e.matmul(out=pS[:, h, :], lhsT=rkT[0:64, c, :], rhs=rqT[0:64, c, :], start=True, stop=True), reads=[b_rkT, b_rqT], writes=[PB[5]])
                else:
                    P.add("pe", lambda e, pS=pS, h=h, c=c: e.matmul(out=pS[:, h, :], lhsT=rkT[:, c, :], rhs=rqTm[:, c, :], start=True, stop=True), reads=[b_rkT, b_rqTm], writes=[PB[5]])
            P.add("dve", lambda e, pS=pS: e.tensor_tensor(out=sT_bf, in0=pS, in1=decayT, op=ALU.mult), reads=[PB[5], b_const], writes=[b_sT])
            if stop == "A3a":
                continue
            pI = bank(6).rearrange("p (a b) -> p a b", a=4, b=128)
            for h in range(4):
                c, pr = h // 2, slice((h % 2) * 64, (h % 2) * 64 + 64)
                P.add("pe", lambda e, pI=pI, h=h: e.matmul(out=pI[:, h, :], lhsT=sT_bf[:, h, :], rhs=rv_bf[:, h * 128:(h + 1) * 128], start=True, stop=False), reads=[b_sT, b_rv], writes=[PB[6]])
                if h % 2 == 0:
                    P.add("pe", lambda e, pI=pI, h=h, c=c: e.matmul(out=pI[:, h, :], lhsT=qxT[0:64, c, :], rhs=state_bf[0:64, c, :], start=False, stop=True), reads=[b_qxT, b_statebf], writes=[PB[6]])
                else:
                    P.add("pe", lambda e, pI=pI, h=h, c=c: e.matmul(out=pI[:, h, :], lhsT=qxTm[:, c, :], rhs=state_bf[:, c, :], start=False, stop=True), reads=[b_qxTm, b_statebf], writes=[PB[6]])
            if stop == "A3b":
                continue
            pK = bank_bf(0).rearrange("p (a b) -> p a b", a=8, b=128)
            for c in range(2):
                P.add("pe", lambda e, pK=pK, c=c: e.transpose(out=pK[:, c, :], in_=rkT[:, c, :], identity=ident_bf), reads=[b_rkT, b_identbf], writes=[PB[0]])
            P.add("dve", lambda e, pK=pK: e.tensor_tensor(out=kz.rearrange("p a (b c) -> p (a b) c", b=2, c=64), in0=pK[:, 0:2, :].rearrange("p a (b c) -> p (a b) c", b=2, c=64), in1=zeta.unsqueeze(2).to_broadcast([128, 4, 64]), op=ALU.mult), reads=[PB[0], b_const], writes=[b_kz])
            if stop == "A3c":
                continue
            pKV = bank(7).rearrange("p (a b) -> p a b", a=2, b=256)
            for c in range(2):
                P.add("pe", lambda e, pKV=pKV, c=c: e.matmul(out=pKV[:, c, :], lhsT=kz[:, c, :], rhs=rv_bf[:, c * 256:(c + 1) * 256], start=True, stop=True), reads=[b_kz, b_rv], writes=[PB[7]])
            for h in range(4):
                c, hh = h // 2, h % 2
                pr = slice(hh * 64, hh * 64 + 64)
                cd = GAMMAS[h] ** 128
                P.add("dve", lambda e, pKV=pKV, c=c, hh=hh, pr=pr, cd=cd: e.scalar_tensor_tensor(out=state[pr, c, :], in0=state[pr, c, :], scalar=cd, in1=pKV[pr, c, hh * 128:(hh + 1) * 128], op0=ALU.mult, op1=ALU.add), reads=[PB[7], b_state], writes=[b_state])
            P.add("act", lambda e: e.copy(out=state_bf, in_=state), reads=[b_state], writes=[b_statebf])
            if stop == "A3":
                continue
            P.add("act", lambda e: e.activation(out=sq, in_=bank(6), func=AF.Square), reads=[PB[6]], writes=[b_sq])
            P.add("dve", lambda e: e.tensor_reduce(out=st[:, 4:8], in_=sq.rearrange("p (a b) -> p a b", a=4, b=128), axis=AX.X, op=ALU.add), reads=[b_sq], writes=[b_st])
            P.add("act", lambda e: e.activation(out=st[:, 8:12], in_=st[:, 4:8], func=AF.Sqrt, scale=1.0 / 128, bias=1e-6), reads=[b_st], writes=[b_st])
            P.add("dve", lambda e: e.reciprocal(out=st[:, 12:16], in_=st[:, 8:12]), reads=[b_st], writes=[b_st])
            P.add("dve", lambda e, pI=pI: e.tensor_tensor(out=yr, in0=pI, in1=st[:, 12:16].unsqueeze(2).to_broadcast([128, 4, 128]), op=ALU.mult), reads=[PB[6], b_st], writes=[b_yr])
            P.add("pool", lambda e: e.tensor_tensor(out=y_bf, in0=yr.rearrange("p a b -> p (a b)"), in1=sg, op=ALU.mult), reads=[b_yr, b_sg], writes=[b_ybf])
            for h in range(4):
                P.add("pe", lambda e, pK=pK, h=h: e.transpose(out=pK[:, 4 + h, :], in_=y_bf[:, h * 128:(h + 1) * 128], identity=ident_bf), reads=[b_ybf, b_identbf], writes=[PB[0]])
            P.add("act", lambda e, pK=pK, s=s: e.copy(out=retT[s], in_=pK[:, 4:8, :]), reads=[PB[0]], writes=[b_retT[s]])
            dma("pool", cT_d[t, :, 4:8, :], retT[s], ch["pst%d" % s], reads=[b_retT[s]], writes=[d_cT[t]])
            conv_step(max(1, (len(conv_jobs) // 2 + NT - 1) // NT))

        P.barrier()
        if stop in ("A", "A1", "A2", "A3", "A3a", "A3b", "A3c"):
            P.emit_all()
            return nc
        A.reset(ab_end)
        qk_h = [A.typed(BF16, 128, 2, 2, S) for _ in range(2)]
        b_qkh = [Buf("qkh0"), Buf("qkh1")]
        NPT = 3
        PT = [A.typed(BF16, 128, 4, 128) for _ in range(NPT)]; b_PT = [Buf("PT%d" % i) for i in range(NPT)]
        rr = A.f32(128, 16); b_rr = Buf("rr")
        o1 = A.f32(128, 128); b_o1 = Buf("o1")
        o2 = A.f32(128, 128); b_o2 = Buf("o2")
        jk = A.f32(128, 128); b_jk = Buf("jk")
        ob = A.typed(BF16, 128, 128); b_ob = Buf("ob")
        dT = [A.typed(BF16, 128, 128) for _ in range(2)]; b_dT = [Buf("dT0"), Buf("dT1")]
        ST_BANKS = [0, 1, 2]
        O_BANKS = [(3, 4), (5, 6)]
        TR_BANK = 7
        it = 0
        sti = 0
        for h in range(4):
            pr = slice((h % 2) * 64, (h % 2) * 64 + 64)
            hs = h % 2
            for w_ in range(2):
                for m_ in range(2):
                    dma("sp", qk_h[hs][0:64, w_, m_, :], qk_d[pr, w_, m_ * 2 + h // 2, :], ch["ld%d" % (w_ * 2 + m_)], writes=[b_qkh[hs]])
            for qi in range(NT):
                qc = slice(qi * 128, (qi + 1) * 128)
                ob_pair = O_BANKS[it % 2]
                for m in range(2):
                    c = m * 2 + h // 2
                    obk = ob_pair[m]
                    O = bank(obk)[:, 0:130]
                    groups = [list(range(g, min(g + 4, qi + 1))) for g in range(0, qi + 1, 4)]
                    pend = None

                    def do_av(grp, pti, O=O, obk=obk, qi=qi, h=h):
                        for sl, kj in enumerate(grp):
                            P.add("pe", lambda e, sl=sl, kj=kj, pti=pti: e.matmul(out=O, lhsT=PT[pti][:, sl, :], rhs=v_all[:, kj, h, :], start=(kj == 0), stop=(kj == qi)),
                                  reads=[b_PT[pti], b_q[kj], b_v0], writes=[PB[obk]])

                    for grp in groups:
                        sb_ = ST_BANKS[sti % 3]
                        pti = sti % NPT
                        sti += 1
                        STv = bank(sb_).rearrange("p (a b) -> p a b", a=4, b=128)
                        for sl, kj in enumerate(grp):
                            near = kj >= qi - 1
                            kc_ = slice(kj * 128, (kj + 1) * 128)
                            P.add("pe", lambda e, STv=STv, sl=sl, kc_=kc_, near=near, m=m, hs=hs, qc=qc: e.matmul(out=STv[:, sl, :], lhsT=qk_h[hs][0:64, 1, m, kc_], rhs=qk_h[hs][0:64, 0, m, qc], start=True, stop=(not near)),
                                  reads=[b_qkh[hs]], writes=[PB[sb_]])
                            if near:
                                which = 0 if kj == qi else 1
                                P.add("pe", lambda e, STv=STv, sl=sl, which=which, h=h: e.matmul(out=STv[:, sl, :], lhsT=ident_bf, rhs=nb_bf[:, h, which, :], start=False, stop=True),
                                      reads=[b_identbf, b_nb], writes=[PB[sb_]])
                        n = len(grp)
                        P.add("act", lambda e, STv=STv, n=n, pti=pti: e.activation(out=PT[pti][:, 0:n, :], in_=STv[:, 0:n, :], func=AF.Exp), reads=[PB[sb_]], writes=[b_PT[pti]])
                        if pend is not None:
                            do_av(*pend)
                        pend = (grp, pti)
                    do_av(*pend)
                O1 = bank(ob_pair[0]); O2 = bank(ob_pair[1])
                rd = [PB[ob_pair[0]], PB[ob_pair[1]]]
                P.add("dve", lambda e, O1=O1: e.reciprocal(out=rr[:, 0:1], in_=O1[:, 128:129]), reads=rd, writes=[b_rr])
                P.add("dve", lambda e, O2=O2: e.reciprocal(out=rr[:, 1:2], in_=O2[:, 128:129]), reads=rd + [b_rr], writes=[b_rr])
                P.add("dve", lambda e: e.tensor_tensor(out=rr[:, 2:3], in0=rr[:, 1:2], in1=neglam, op=ALU.mult), reads=[b_rr, b_sm], writes=[b_rr])
                P.add("dve", lambda e, O1=O1: e.tensor_scalar(out=o1, in0=O1[:, 0:128], scalar1=rr[:, 0:1], scalar2=None, op0=ALU.mult), reads=rd + [b_rr], writes=[b_o1])
                P.add("dve", lambda e, O2=O2: e.scalar_tensor_tensor(out=o2, in0=O2[:, 0:128], scalar=rr[:, 2:3], in1=o1, op0=ALU.mult, op1=ALU.add), reads=rd + [b_rr, b_o1], writes=[b_o2])
                P.add("act", lambda e: e.activation(out=jk, in_=o2, func=AF.Square, accum_out=rr[:, 3:4]), reads=[b_o2], writes=[b_jk, b_rr])
                P.add("act", lambda e: e.activation(out=rr[:, 4:5], in_=rr[:, 3:4], func=AF.Sqrt, scale=1.0 / 128, bias=1e-5), reads=[b_rr], writes=[b_rr])
                P.add("dve", lambda e: e.reciprocal(out=rr[:, 5:6], in_=rr[:, 4:5]), reads=[b_rr], writes=[b_rr])
                P.add("dve", lambda e: e.scalar_tensor_tensor(out=ob, in0=o2, scalar=rr[:, 5:6], in1=subln, op0=ALU.mult, op1=ALU.mult), reads=[b_o2, b_rr, b_const], writes=[b_ob])
                pTr = bank_bf(TR_BANK)[:, 0:128]
                P.add("pe", lambda e, pTr=pTr: e.transpose(out=pTr, in_=ob, identity=ident_bf), reads=[b_ob, b_identbf], writes=[PB[TR_BANK]])
                s = it % 2
                P.add("act", lambda e, pTr=pTr, s=s: e.copy(out=dT[s], in_=pTr), reads=[PB[TR_BANK]], writes=[b_dT[s]])
                dma("sp", cT_d[qi, :, h, :], dT[s], ch["st%d" % s], reads=[b_dT[s]], writes=[d_cT[qi]])
                it += 1
                conv_step(max(1, (len(conv_jobs) // 2 + 4 * NT - 1) // (4 * NT)))
        conv_step(len(conv_jobs))

        P.barrier()
        if stop == "B":
            P.emit_all()
            return nc
        A.reset(const_end)
        w_out_bf = A.typed(BF16, 128, 8, D); b_wout = Buf("w_out")
        wq_bf = A.typed(BF16, 128, 8, 2048); b_wq = Buf("wq")
        gw_bf = A.typed(BF16, 128, 8, D); b_gw = Buf("gw")
        pp_bf = A.typed(BF16, 128, 2, D); b_pp = Buf("pp")
        skT_bf = A.typed(BF16, 128, 16, 128); b_skT = Buf("skT")
        g_ffn = A.f32(128, D)
        g_ple = A.f32(128, D)
        g_fin = A.f32(128, D)
        gbias = A.f32(128, D)
        for dst, src in [(g_ffn, ffn_norm_d), (g_ple, ple_norm_d), (g_fin, fin_d), (gbias, gb_d)]:
            dma("sp", dst, bcast(src), ch["c0"], writes=[b_const])
        NG = 13
        gall = A.f32(128, NG * 512)
        gall_bf = gall.bitcast(BF16)
        gbuf = [gall_bf[:, i * D:(i + 1) * D] for i in range(NG)]; b_gbuf = [Buf("gbuf%d" % i) for i in range(NG)]
        stg = [gall[:, 0:2048], gall[:, 2048:4096]]; b_stg = [Buf("stg0"), Buf("stg1")]
        ND = 4
        diag = [A.typed(BF16, 128, 128) for _ in range(ND)]; b_diag = [Buf("diag%d" % i) for i in range(ND)]
        m_bf = A.typed(BF16, 128, D); b_mbf = Buf("m_bf")
        junkb = A.typed(BF16, 128, D); b_junkb = Buf("junkb")
        si = 0

        def load_w(src_ap, ncols, dst_fn, b_dst):
            nonlocal si
            s = si % 2
            si += 1
            dma("sp", stg[s][:, 0:ncols], src_ap, ch["w%d" % s], writes=[b_stg[s]])
            eng = ["act", "dve", "pool"][si % 3]
            if eng == "act":
                P.add("act", lambda e: e.copy(out=dst_fn, in_=stg[s][:, 0:ncols]), reads=[b_stg[s]], writes=[b_dst])
            else:
                P.add(eng, lambda e: e.tensor_copy(out=dst_fn, in_=stg[s][:, 0:ncols]), reads=[b_stg[s]], writes=[b_dst])

        for kc in range(8):
            load_w(w_out_d[kc * 128:(kc + 1) * 128, :], D, w_out_bf[:, kc, :], b_wout)
            load_w(wq_d[kc * 128:(kc + 1) * 128, :], 2048, wq_bf[:, kc, :], b_wq)
            load_w(gw_d[kc * 128:(kc + 1) * 128, :], D, gw_bf[:, kc, :], b_gw)
        for kc in range(2):
            load_w(pp_d[kc * 128:(kc + 1) * 128, :], D, pp_bf[:, kc, :], b_pp)
        for half in range(2):
            s = si % 2
            si += 1
            skv = stg[s].rearrange("p (a b) -> p a b", a=16, b=128)
            dma("sp", skv[:, 0:8, :], sk_d[half * 8:(half + 1) * 8].rearrange("j k d -> k j d"), ch["w%d" % s], writes=[b_stg[s]])
            for q4 in range(2):
                bk = q4
                pv = bank(bk).rearrange("p (a b) -> p a b", a=4, b=128)
                for j in range(4):
                    P.add("pe", lambda e, pv=pv, j=j, skv=skv, q4=q4: e.transpose(out=pv[:, j, :], in_=skv[:, q4 * 4 + j, :], identity=ident), reads=[b_stg[s], b_ident], writes=[PB[bk]])
                P.add("act", lambda e, pv=pv, half=half, q4=q4: e.copy(out=skT_bf[:, half * 8 + q4 * 4: half * 8 + q4 * 4 + 4, :], in_=pv), reads=[PB[bk]], writes=[b_skT])

        cTt = [A.typed(BF16, 128, 8, 128) for _ in range(2)]; b_cTt = [Buf("cTt0"), Buf("cTt1")]
        xt2 = None; b_xt2 = None
        ptile = [A.f32(128, 256) for _ in range(2)]; b_pt = [Buf("pt0"), Buf("pt1")]
        hA2 = [A.f32(128, D) for _ in range(2)]; b_hA2 = [Buf("hA0"), Buf("hA1")]
        hB = A.f32(128, D); b_hB = Buf("hB")
        mt = A.f32(128, D); b_mt = Buf("mt")
        mtb = A.f32(128, D); b_mtb = Buf("mtb")
        junk2 = A.f32(128, D); b_junk2 = Buf("junk2")
        xt2 = junk2; b_xt2 = b_junk2
        junk_f = A.typed(BF16, 128, D); b_junkf = Buf("junk_f")
        mT_bf = A.typed(BF16, 128, 8, 128); b_mT = Buf("mT")
        nT_bf = mT_bf; b_nT = b_mT
        pT_bf = A.typed(BF16, 128, 2, 128); b_pT = Buf("pT")
        qryT_bf = A.typed(BF16, 128, 16, 128); b_qry = Buf("qryT")
        scs = A.f32(128, 16, 128); b_scs = Buf("scs")
        wk = A.f32(128, 256); b_wk = Buf("wk")
        vals = A.f32(128, 16, 16); b_vals = Buf("vals")
        idx = A.typed(U32, 128, 16, 16); b_idx = Buf("idx")
        idxf = A.f32(128, 16, 16); b_idxf = Buf("idxf")
        cand = A.f32(128, 8, 256); b_cand = Buf("cand")
        ts_ = A.f32(128, 8, 16); b_ts = Buf("ts")
        pos = A.typed(U32, 128, 8, 16); b_pos = Buf("pos")
        pi_u = A.typed(U32, 128, 8, 16); b_piu = Buf("piu")
        pj_u = A.typed(U32, 128, 8, 16); b_pju = Buf("pju")
        pi_f = A.f32(128, 8, 16); b_pif = Buf("pif")
        pj_f = A.f32(128, 8, 16); b_pjf = Buf("pjf")
        oh = scs.rearrange("p (a c) (d e) -> p a c d e", a=8, c=2, d=8, e=16).rearrange("p a c d e -> p a (c d) e"); b_oh = b_scs
        e1 = A.f32(128, 8, 16); b_e1 = Buf("e1")
        e2 = A.f32(128, 8, 16); b_e2 = Buf("e2")
        eidx2 = [A.typed(I32, 128, 128) for _ in range(2)]; b_eidx2 = [Buf("eidx0"), Buf("eidx1")]
        gate2 = [A.f32(128, 8, 16) for _ in range(2)]; b_gate2 = [Buf("gate0"), Buf("gate1")]
        m_bf2 = [m_bf, A.typed(BF16, 128, D)]; b_mbf2 = [b_mbf, Buf("m_bf1")]
        gs = A.f32(128, 16); b_gs = Buf("gs")
        dots = A.f32(128, 128); b_dots = Buf("dots")
        actg = A.f32(128, 128); b_actg = Buf("actg")
        st_f = A.f32(128, 8); b_stf = Buf("st_f")
        st_b = A.f32(128, 8); b_stb = Buf("st_b")
        gsb = junk2; b_gsb = b_junk2
        outt = mtb; b_outt = b_mtb
        gi = 0
        P.barrier()

        p_view = p_d.rearrange("(t p) n -> t p n", p=128)
        o_view = out_d.rearrange("(t p) n -> t p n", p=128)

        def rms(src, b_src, gain, dst, b_dst, stt, b_stt, col, jk_, b_jk):
            P.add("act", lambda e: e.activation(out=jk_, in_=src, func=AF.Square, accum_out=stt[:, col:col + 1]), reads=[b_src], writes=[b_jk, b_stt])
            P.add("act", lambda e: e.activation(out=stt[:, col + 1:col + 2], in_=stt[:, col:col + 1], func=AF.Sqrt, scale=1.0 / D, bias=1e-6), reads=[b_stt], writes=[b_stt])
            P.add("dve", lambda e: e.reciprocal(out=stt[:, col + 2:col + 3], in_=stt[:, col + 1:col + 2]), reads=[b_stt], writes=[b_stt])
            P.add("dve", lambda e: e.scalar_tensor_tensor(out=dst, in0=src, scalar=stt[:, col + 2:col + 3], in1=gain, op0=ALU.mult, op1=ALU.mult), reads=[b_src, b_stt, b_const], writes=[b_dst])

        def transpose_to_bf(src, b_src, nchunk, dstT, b_dstT, banks):
            for g in range(0, nchunk, 4):
                bk = banks[(g // 4) % len(banks)]
                n = min(4, nchunk - g)
                pv = bank(bk).rearrange("p (a b) -> p a b", a=4, b=128)
                for j in range(n):
                    P.add("pe", lambda e, pv=pv, j=j, g=g: e.transpose(out=pv[:, j, :], in_=src[:, (g + j) * 128:(g + j + 1) * 128], identity=ident), reads=[b_src, b_ident], writes=[PB[bk]])
                P.add("act", lambda e, pv=pv, g=g, n=n: e.copy(out=dstT[:, g:g + n, :], in_=pv[:, 0:n, :]), reads=[PB[bk]], writes=[b_dstT])

        def front(t, part="ABC"):
            if "A" in part:
                frontA(t)
            if "B" in part:
                frontB(t)
            if "C" in part:
                frontC(t)

        def frontA(t):
            s = t % 2
            hA, b_hA = hA2[s], b_hA2[s]
            dma("sp", cTt[s].rearrange("p a b -> p (a b)"), cT_d[t].rearrange("p a b -> p (a b)"), ch["ld%d" % s], reads=[d_cT[t]], writes=[b_cTt[s]])
            dma("sp", xt2, x_view[t], ch["ld2"], writes=[b_xt2])
            dma("sp", ptile[s], p_view[t], ch["c%d" % s], writes=[b_pt[s]])
            for n2 in range(2):
                for c in range(8):
                    P.add("pe", lambda e, n2=n2, c=c, s=s: e.matmul(out=bank(n2), lhsT=cTt[s][:, c, :], rhs=w_out_bf[:, c, n2 * 512:(n2 + 1) * 512], start=(c == 0), stop=(c == 7)),
                          reads=[b_cTt[s], b_wout], writes=[PB[n2]])
            P.add("dve", lambda e: e.tensor_tensor(out=hA, in0=bank(0, 2), in1=xt2, op=ALU.add), reads=[PB[0], PB[1], b_xt2], writes=[b_hA])
            if dbg:
                dma("sp", dbg_d["h1"].rearrange("(t p) n -> t p n", p=128)[t], hA, ch["c2"], reads=[b_hA])
            rms(hA, b_hA, g_ffn, mt, b_mt, st_f, b_stf, 0, junk_f, b_junkf)
            P.add("act", lambda e: e.copy(out=m_bf2[s], in_=mt), reads=[b_mt], writes=[b_mbf2[s]])
            transpose_to_bf(mt, b_mt, 8, mT_bf, b_mT, [2])
            for r4 in range(4):
                pv = bank(3).rearrange("p (a b) -> p a b", a=4, b=128)
                for j in range(4):
                    jj = r4 * 4 + j
                    for kc in range(8):
                        P.add("pe", lambda e, pv=pv, j=j, jj=jj, kc=kc: e.matmul(out=pv[:, j, :], lhsT=wq_bf[:, kc, jj * 128:(jj + 1) * 128], rhs=mT_bf[:, kc, :], start=(kc == 0), stop=(kc == 7)),
                              reads=[b_wq, b_mT], writes=[PB[3]])
                P.add("act", lambda e, pv=pv, r4=r4: e.copy(out=qryT_bf[:, r4 * 4:r4 * 4 + 4, :], in_=pv), reads=[PB[3]], writes=[b_qry])
            for r4 in range(4):
                pv = bank(4).rearrange("p (a b) -> p a b", a=4, b=128)
                for j in range(4):
                    jj = r4 * 4 + j
                    P.add("pe", lambda e, pv=pv, j=j, jj=jj: e.matmul(out=pv[:, j, :], lhsT=qryT_bf[:, jj, :], rhs=skT_bf[:, jj, :], start=True, stop=True), reads=[b_qry, b_skT], writes=[PB[4]])
                P.add("act", lambda e, pv=pv, r4=r4: e.copy(out=scs[:, r4 * 4:r4 * 4 + 4, :], in_=pv), reads=[PB[4]], writes=[b_scs])

        def frontB(t):
            s = t % 2
            eidx, b_eidx = eidx2[s], b_eidx2[s]
            gate, b_gate = gate2[s], b_gate2[s]
            for j in range(16):
                P.add("dve", lambda e, j=j: e.max(out=vals[:, j, 0:8], in_=scs[:, j, :]), reads=[b_scs], writes=[b_vals])
                P.add("dve", lambda e, j=j: e.match_replace(out=wk[:, 0:128], in_to_replace=vals[:, j, 0:8], in_values=scs[:, j, :], imm_value=-1e30), reads=[b_scs, b_vals], writes=[b_wk])
                P.add("dve", lambda e, j=j: e.max(out=vals[:, j, 8:16], in_=wk[:, 0:128]), reads=[b_wk], writes=[b_vals])
                P.add("dve", lambda e, j=j: e.max_index(out=idx[:, j, 0:8], in_max=vals[:, j, 0:8], in_values=scs[:, j, :]), reads=[b_scs, b_vals], writes=[b_idx])
                P.add("dve", lambda e, j=j: e.max_index(out=idx[:, j, 8:16], in_max=vals[:, j, 8:16], in_values=scs[:, j, :]), reads=[b_scs, b_vals], writes=[b_idx])
            P.add("dve", lambda e: e.tensor_copy(out=idxf, in_=idx), reads=[b_idx], writes=[b_idxf])
            v4 = vals.rearrange("p (h c) k -> p h c k", h=8, c=2)
            i4 = idxf.rearrange("p (h c) k -> p h c k", h=8, c=2)
            c4 = cand.rearrange("p h (a b) -> p h a b", a=16, b=16)
            P.add("dve", lambda e: e.tensor_tensor(out=c4, in0=v4[:, :, 0, :].unsqueeze(3).to_broadcast([128, 8, 16, 16]), in1=v4[:, :, 1, :].unsqueeze(2).to_broadcast([128, 8, 16, 16]), op=ALU.add), reads=[b_vals], writes=[b_cand])
            for h in range(8):
                P.add("dve", lambda e, h=h: e.max(out=ts_[:, h, 0:8], in_=cand[:, h, :]), reads=[b_cand], writes=[b_ts])
                P.add("dve", lambda e, h=h: e.match_replace(out=wk, in_to_replace=ts_[:, h, 0:8], in_values=cand[:, h, :], imm_value=-1e30), reads=[b_cand, b_ts], writes=[b_wk])
                P.add("dve", lambda e, h=h: e.max(out=ts_[:, h, 8:16], in_=wk), reads=[b_wk], writes=[b_ts])
                P.add("dve", lambda e, h=h: e.max_index(out=pos[:, h, 0:8], in_max=ts_[:, h, 0:8], in_values=cand[:, h, :]), reads=[b_cand, b_ts], writes=[b_pos])
                P.add("dve", lambda e, h=h: e.max_index(out=pos[:, h, 8:16], in_max=ts_[:, h, 8:16], in_values=cand[:, h, :]), reads=[b_cand, b_ts], writes=[b_pos])
            P.add("dve", lambda e: e.tensor_single_scalar(out=pi_u, in_=pos, scalar=4, op=ALU.logical_shift_right), reads=[b_pos], writes=[b_piu])
            P.add("dve", lambda e: e.tensor_single_scalar(out=pj_u, in_=pos, scalar=15, op=ALU.bitwise_and), reads=[b_pos], writes=[b_pju])
            P.add("dve", lambda e: e.tensor_copy(out=pi_f, in_=pi_u), reads=[b_piu], writes=[b_pif])
            P.add("dve", lambda e: e.tensor_copy(out=pj_f, in_=pj_u), reads=[b_pju], writes=[b_pjf])
            iob = iota16.unsqueeze(1).unsqueeze(1).to_broadcast([128, 8, 16, 16])
            for (pf, b_pf, cc, ee, b_ee) in [(pi_f, b_pif, 0, e1, b_e1), (pj_f, b_pjf, 1, e2, b_e2)]:
                P.add("dve", lambda e, pf=pf: e.tensor_tensor(out=oh, in0=pf.unsqueeze(3).to_broadcast([128, 8, 16, 16]), in1=iob, op=ALU.is_equal), reads=[b_pf, b_const], writes=[b_oh])
                P.add("dve", lambda e, cc=cc: e.tensor_tensor(out=oh, in0=oh, in1=i4[:, :, cc, :].unsqueeze(2).to_broadcast([128, 8, 16, 16]), op=ALU.mult), reads=[b_oh, b_idxf], writes=[b_oh])
                P.add("dve", lambda e, ee=ee: e.tensor_reduce(out=ee, in_=oh, axis=AX.X, op=ALU.add), reads=[b_oh], writes=[b_ee])
            P.add("dve", lambda e: e.scalar_tensor_tensor(out=e1, in0=e1, scalar=128.0, in1=e2, op0=ALU.mult, op1=ALU.add), reads=[b_e1, b_e2], writes=[b_e1])
            P.add("dve", lambda e: e.tensor_copy(out=eidx, in_=e1.rearrange("p a b -> p (a b)")), reads=[b_e1], writes=[b_eidx])
            P.add("dve", lambda e: e.tensor_tensor(out=gate, in0=ts_, in1=ts_[:, :, 0:1].to_broadcast([128, 8, 16]), op=ALU.subtract), reads=[b_ts], writes=[b_gate])

        def frontC(t):
            s = t % 2
            eidx, b_eidx = eidx2[s], b_eidx2[s]
            gate, b_gate = gate2[s], b_gate2[s]
            P.add("act", lambda e: e.activation(out=gate, in_=gate, func=AF.Exp), reads=[b_gate], writes=[b_gate])
            P.add("dve", lambda e: e.tensor_reduce(out=gs[:, 0:8], in_=gate, axis=AX.X, op=ALU.add), reads=[b_gate], writes=[b_gs])
            P.add("dve", lambda e: e.reciprocal(out=gs[:, 8:16], in_=gs[:, 0:8]), reads=[b_gs], writes=[b_gs])
            P.add("dve", lambda e: e.tensor_tensor(out=gate, in0=gate, in1=gs[:, 8:16].unsqueeze(2).to_broadcast([128, 8, 16]), op=ALU.mult), reads=[b_gate, b_gs], writes=[b_gate])
            if dbg:
                dma("sp", dbg_d["eidx"].rearrange("(t p) n -> t p n", p=128)[t], eidx, ch["c2"], reads=[b_eidx])
                dma("sp", dbg_d["gate"].rearrange("(t p) n -> t p n", p=128)[t], gate.rearrange("p a b -> p (a b)"), ch["c2"], reads=[b_gate])

        def back_head(t):
            s = t % 2
            hA, b_hA = hA2[s], b_hA2[s]
            P.add("dve", lambda e: e.tensor_tensor(out=hB, in0=bank(6, 2), in1=hA, op=ALU.add), reads=[PB[6], PB[7], b_hA], writes=[b_hB])

        def back(t):
            s = t % 2
            if dbg:
                dma("sp", dbg_d["h2"].rearrange("(t p) n -> t p n", p=128)[t], hB, ch["c2"], reads=[b_hB])
            rms(hB, b_hB, g_ple, mtb, b_mtb, st_b, b_stb, 0, junk2, b_junk2)
            transpose_to_bf(mtb, b_mtb, 8, nT_bf, b_nT, [2])
            transpose_to_bf(ptile[s], b_pt[s], 2, pT_bf, b_pT, [5])
            for n2 in range(2):
                for kc in range(8):
                    P.add("pe", lambda e, n2=n2, kc=kc: e.matmul(out=bank(n2), lhsT=nT_bf[:, kc, :], rhs=gw_bf[:, kc, n2 * 512:(n2 + 1) * 512], start=(kc == 0), stop=(kc == 7)), reads=[b_nT, b_gw], writes=[PB[n2]])
            for n2 in range(2):
                for kc in range(2):
                    P.add("pe", lambda e, n2=n2, kc=kc: e.matmul(out=bank(3 + n2), lhsT=pT_bf[:, kc, :], rhs=pp_bf[:, kc, n2 * 512:(n2 + 1) * 512], start=(kc == 0), stop=(kc == 1)), reads=[b_pT, b_pp], writes=[PB[3 + n2]])
            P.add("dve", lambda e: e.tensor_tensor(out=gsb, in0=bank(0, 2), in1=gbias, op=ALU.add), reads=[PB[0], PB[1], b_const], writes=[b_gsb])
            P.add("act", lambda e: e.activation(out=gsb, in_=gsb, func=AF.Sigmoid), reads=[b_gsb], writes=[b_gsb])
            P.add("dve", lambda e: e.tensor_tensor(out=gsb, in0=bank(3, 2), in1=gsb, op=ALU.mult), reads=[PB[3], PB[4], b_gsb], writes=[b_gsb])
            P.add("pool", lambda e: e.tensor_tensor(out=hB, in0=gsb, in1=hB, op=ALU.add), reads=[b_gsb, b_hB], writes=[b_hB])
            rms(hB, b_hB, g_fin, outt, b_outt, st_b, b_stb, 4, junk2, b_junk2)
            dma("sp", o_view[t], outt, ch["st0"], reads=[b_outt])

        def record(fns):
            P.defer_list = L = []
            for f in fns:
                f()
            P.defer_list = None
            return L

        def pop(L, n):
            for _ in range(n):
                if not L:
                    return
                a_, k_ = L.pop(0)
                P.add(*a_, **k_)

        front(0)
        for t in range(NT):
            s = t % 2
            eidx, b_eidx = eidx2[s], b_eidx2[s]
            gate, b_gate = gate2[s], b_gate2[s]
            L = []
            if PIPE == 1:
                fns = []
                if t >= 1:
                    fns.append(lambda t=t: back(t - 1))
                if t + 1 < NT:
                    fns.append(lambda t=t: front(t + 1))
                L = record(fns)
            per = (len(L) + 127) // 128
            LA = []
            if PIPE == 2 and t + 1 < NT:
                LA = record([lambda t=t: frontA(t + 1)])
            P.add("dve", lambda e: e.memset(dots, 0.0), writes=[b_dots])
            for sidx in range(128):
                g = gi % NG
                gi += 1
                P.add("pool", lambda e, g=g, sidx=sidx, eidx=eidx: e.indirect_dma_start(out=gbuf[g], out_offset=None, in_=ub_d, in_offset=bass.IndirectOffsetOnAxis(ap=eidx[:, sidx:sidx + 1], axis=0)),
                      reads=[b_eidx, b_tab], writes=[b_gbuf[g]], chan=gch[g])
                P.add("dve", lambda e, g=g, sidx=sidx, s=s: e.scalar_tensor_tensor(out=junkb, in0=gbuf[g], scalar=1.0, in1=m_bf2[s], op0=ALU.mult, op1=ALU.mult, accum_out=dots[:, sidx:sidx + 1]),
                      reads=[b_gbuf[g], b_mbf2[s]], writes=[b_junkb, b_dots])
                if sidx == 15:
                    pop(LA, len(LA))
            pop(LA, len(LA))
            P.add("act", lambda e: e.activation(out=actg, in_=dots, func=AF.Gelu), reads=[b_dots], writes=[b_actg])
            P.add("dve", lambda e, gate=gate: e.tensor_tensor(out=actg, in0=actg, in1=gate.rearrange("p a b -> p (a b)"), op=ALU.mult), reads=[b_actg, b_gate], writes=[b_actg])
            if PIPE == 2 and t + 1 < NT:
                frontB(t + 1)
            for sidx in range(128):
                g = gi % NG
                gi += 1
                dg = sidx % ND
                P.add("pool", lambda e, g=g, sidx=sidx, eidx=eidx: e.indirect_dma_start(out=gbuf[g], out_offset=None, in_=vb_d, in_offset=bass.IndirectOffsetOnAxis(ap=eidx[:, sidx:sidx + 1], axis=0)),
                      reads=[b_eidx, b_tab], writes=[b_gbuf[g]], chan=gch[g])
                P.add("act", lambda e, dg=dg, sidx=sidx: e.activation(out=diag[dg], in_=ident, func=AF.Copy, scale=actg[:, sidx:sidx + 1]), reads=[b_ident, b_actg], writes=[b_diag[dg]])
                for n2 in range(2):
                    P.add("pe", lambda e, g=g, dg=dg, n2=n2, sidx=sidx: e.matmul(out=bank(6 + n2), lhsT=diag[dg], rhs=gbuf[g][:, n2 * 512:(n2 + 1) * 512], start=(sidx == 0), stop=(sidx == 127)),
                          reads=[b_diag[dg], b_gbuf[g]], writes=[PB[6 + n2]])
                pop(L, per)
            pop(L, len(L))
            back_head(t)
            if PIPE == 2:
                if t + 1 < NT:
                    frontC(t + 1)
                back(t)
            elif PIPE == 0:
                back(t)
                if t + 1 < NT:
                    front(t + 1)
        if PIPE == 1:
            back(NT - 1)

        P.barrier()
        P.emit_all()
    return nc


def host_consts(S, rel_bias):
    f32 = np.float32
    half = 32
    freqs = (1.0 / (np.float32(10000.0) ** (np.arange(half, dtype=f32) / f32(half)))).astype(f32)
    pos = np.arange(S, dtype=f32)
    ang = (pos[:, None] * freqs[None, :]).astype(f32)
    cos, sin = np.cos(ang).astype(f32), np.sin(ang).astype(f32)
    fidx = np.arange(128) % 32
    cosF = np.ascontiguousarray(cos[:, fidx].T)
    sinF = np.ascontiguousarray(sin[:, fidx].T)
    lg = np.log(np.array(GAMMAS, dtype=np.float64))
    i = np.arange(128)
    rel = i[None, :] - i[:, None]
    decayT = np.zeros((128, 4, 128), f32)
    for h in range(4):
        decayT[:, h, :] = np.where(rel >= 0, np.exp(np.maximum(rel, 0) * lg[h]), 0.0)
    zeta = np.exp((127 - i)[:, None] * lg[None, :]).astype(f32)
    xi = np.exp((i + 1)[None, :] * lg[:, None]).astype(f32)
    xiF = np.zeros((128, 2, 128), f32)
    for c in range(2):
        xiF[0:64, c, :] = xi[2 * c][None, :]
        xiF[64:128, c, :] = xi[2 * c + 1][None, :]
    k = np.arange(128)[:, None]
    q = np.arange(128)[None, :]
    bd = t5_bucket_np(np.maximum(q - k, 0))
    bp = t5_bucket_np(q - k + 128)
    nbias = np.zeros((128, 4, 2, 128), f32)
    for h in range(4):
        nbias[:, h, 0, :] = rel_bias[bd, h]
        nbias[:, h, 1, :] = rel_bias[bp, h]
    cmask = np.where(q - k >= 0, 0.0, -30000.0).astype(f32)
    iota16 = np.tile(np.arange(16, dtype=f32)[None, :], (128, 1))
    return dict(cosF=cosF, sinF=sinF, decayT=decayT, zeta=zeta, xiF=xiF, nbias=nbias, cmask=cmask,
                iota16=iota16, relb31=np.ascontiguousarray(rel_bias[31:32, :]))


def make_in_maps(S, nb, x, p, attn_norm, w_in, lam_q1, lam_k1, lam_q2, lam_k2, subln_gain, w_out,
                 rel_bias, ffn_norm, peer_query, peer_subkeys, peer_u, peer_v,
                 ple_norm, ple_gate_w, ple_gate_b, ple_proj, final_norm):
    f = lambda a: np.ascontiguousarray(np.asarray(a, dtype=np.float32))
    rel_bias = f(rel_bias)
    shared = dict(
        attn_norm=f(attn_norm).reshape(1, D), w_in=f(w_in)[0],
        lam=np.stack([f(lam_q1)[0], f(lam_k1)[0], f(lam_q2)[0], f(lam_k2)[0]], 0),
        subln_gain=f(subln_gain).reshape(1, 128), w_out=f(w_out)[0], ffn_norm=f(ffn_norm).reshape(1, D),
        peer_query=f(peer_query)[0], peer_subkeys=f(peer_subkeys)[0].reshape(16, 128, 128),
        peer_u=f(peer_u)[0], peer_v=f(peer_v)[0], ple_norm=f(ple_norm).reshape(1, D),
        ple_gate_w=f(ple_gate_w)[0], ple_gate_b=f(ple_gate_b).reshape(1, D), ple_proj=f(ple_proj)[0],
        final_norm=f(final_norm).reshape(1, D),
    )
    shared.update(host_consts(S, rel_bias))
    x = f(x)
    p = f(p)[0]
    maps = []
    for b in range(nb):
        m = dict(shared)
        m["x"] = np.ascontiguousarray(x[b])
        m["p"] = np.ascontiguousarray(p[b])
        maps.append(m)
    return maps


_NC_CACHE = {}


def kernel(**inputs):
    x = np.asarray(inputs["x"])
    B, S, _ = x.shape
    if S not in _NC_CACHE:
        _NC_CACHE[S] = build(S)
    nc = _NC_CACHE[S]
    maps = make_in_maps(S, B, **inputs)
    res = run_bass_kernel_spmd(nc, maps, core_ids=list(range(B)))
    out = np.stack([np.asarray(r["out"]) for r in res.results], 0)
    return out.astype(np.float32)
```
